# Optimizing a Trainium2 kernel written in Bass

```python
import math
import jax
import jax.numpy as jnp
from jax import lax
import numpy as np

D_MODEL = 1024
BATCH = 8
SEQ = 2048
DEPTH = 4

GRID_W = 64
CTX_LEN = 256
N_EVEN = (DEPTH + 1) // 2
N_ODD = DEPTH // 2
MLP_HIDDEN = 4 * D_MODEL
MIX_W = D_MODEL
GROUP_W = MIX_W // 2
EPS = 1e-6

S5_CH = GROUP_W
S5_GROUP = 16
S5_GROUPS = S5_CH // S5_GROUP
S5_STATE = 64
S5_MIN_DT = 1e-3
S5_MAX_DT = 1e-1

HY_CH = GROUP_W
HY_ORDER = 2
HY_EMB = 33
HY_BANDS = (HY_EMB - 1) // 2
HY_FFN = 64
SHORT_K = 3

RW_CH = GROUP_W
RW_HEAD = 64
RW_HEADS = RW_CH // RW_HEAD
RW_DECAY_LORA = 64
RW_A_LORA = 64
RW_G_LORA = 128
RW_LN_EPS = 64e-5
RW_IN = 3 * RW_CH + RW_DECAY_LORA + RW_A_LORA + RW_G_LORA

DA_HEADS = 4
DA_HEAD = 64
DA_V = 2 * DA_HEAD
DA_QK = DA_HEADS * 2 * DA_HEAD
DA_VW = DA_HEADS * DA_V
DA_SCALE = DA_HEAD ** -0.5
DA_SUBLN_EPS = 1e-5
Q_BLOCK = 128
ROPE_BASE = 10000.0
ROPE_FREQS = DA_HEAD // 4

EVEN_IN = S5_CH + (1 + HY_ORDER) * HY_CH
ODD_IN = RW_IN + 2 * DA_QK + DA_VW

F32 = jnp.float32

kernel_name = 'hybrid_s5_hyena_rwkv7_diffattn_prefix_dit'


def rmsnorm(x, g):
    xf = x.astype(F32)
    return xf * lax.rsqrt(jnp.mean(xf * xf, axis=-1, keepdims=True) + EPS) * g.astype(F32)


def modulate(h, shift, scale):
    return h * (1.0 + scale) + shift


def sq_relu_mlp(h, w1, w2):
    return jnp.square(jax.nn.relu(h @ w1)) @ w2


def short_conv(x, w, b=None):
    y = lax.conv_general_dilated(
        x.astype(F32), w.astype(F32)[:, None, :], window_strides=(1,),
        padding=((SHORT_K // 2, SHORT_K // 2),),
        dimension_numbers=('NWC', 'WIO', 'NWC'), feature_group_count=x.shape[-1])
    return y if b is None else y + b.astype(F32)


def s5_discretise(lam_re, lam_im, log_dt, b_re, b_im):
    lam_re = jnp.minimum(lam_re.astype(F32), -1e-4)
    lam_im = lam_im.astype(F32)
    dt = jnp.exp(log_dt.astype(F32))[:, None]
    mag = jnp.exp(lam_re * dt)
    lb_re = mag * jnp.cos(lam_im * dt)
    lb_im = mag * jnp.sin(lam_im * dt)
    den = lam_re * lam_re + lam_im * lam_im
    f_re = ((lb_re - 1.0) * lam_re + lb_im * lam_im) / den
    f_im = (lb_im * lam_re - (lb_re - 1.0) * lam_im) / den
    b_re = b_re.astype(F32)
    b_im = b_im.astype(F32)
    bb_re = f_re[..., None] * b_re - f_im[..., None] * b_im
    bb_im = f_re[..., None] * b_im + f_im[..., None] * b_re
    return lb_re, lb_im, bb_re, bb_im


def _complex_affine_combine(e1, e2):
    a1r, a1i, b1r, b1i = e1
    a2r, a2i, b2r, b2i = e2
    return (a2r * a1r - a2i * a1i, a2r * a1i + a2i * a1r,
            a2r * b1r - a2i * b1i + b2r, a2r * b1i + a2i * b1r + b2i)


def s5_scan(u, lb_re, lb_im, bb_re, bb_im, h0, reverse):
    bt, n = u.shape[:2]
    ug = u.astype(F32).reshape(bt, n, S5_GROUPS, S5_GROUP)
    bu_re = jnp.einsum('blgh,gph->blgp', ug, bb_re)
    bu_im = jnp.einsum('blgh,gph->blgp', ug, bb_im)
    shape = (1, n, S5_GROUPS, S5_STATE)
    a_re = jnp.broadcast_to(lb_re, shape)
    a_im = jnp.broadcast_to(lb_im, shape)
    acc_re, acc_im, h_re, h_im = lax.associative_scan(
        _complex_affine_combine, (a_re, a_im, bu_re, bu_im), reverse=reverse, axis=1)
    if h0 is not None:
        h0_re, h0_im = h0[0][:, None], h0[1][:, None]
        h_re = h_re + acc_re * h0_re - acc_im * h0_im
        h_im = h_im + acc_re * h0_im + acc_im * h0_re
    return h_re, h_im


def s5_readout(h_re, h_im, c_re, c_im):
    bt, n = h_re.shape[:2]
    y = (jnp.einsum('blgp,ghp->blgh', h_re, c_re.astype(F32))
         - jnp.einsum('blgp,ghp->blgh', h_im, c_im.astype(F32)))
    return y.reshape(bt, n, S5_CH)


def s5_mixer(u_ctx, u_lat, lam_re, lam_im, log_dt, b_re, b_im, c_re, c_im,
             d_skip, glu_w, glu_b, ctx_out):
    ys_c, ys_l = [], []
    for d in range(2):
        rev = d == 1
        lb_re, lb_im, bb_re, bb_im = s5_discretise(lam_re[d], lam_im[d], log_dt[d], b_re[d], b_im[d])
        hc_re, hc_im = s5_scan(u_ctx, lb_re, lb_im, bb_re, bb_im, None, rev)
        fin = 0 if rev else -1
        h0 = (hc_re[:, fin], hc_im[:, fin])
        hl_re, hl_im = s5_scan(u_lat, lb_re, lb_im, bb_re, bb_im, h0, rev)
        ys_l.append(s5_readout(hl_re, hl_im, c_re[d], c_im[d]))
        if ctx_out:
            ys_c.append(s5_readout(hc_re, hc_im, c_re[d], c_im[d]))

    def finish(ys, u):
        y = ys[0] + ys[1] + u * d_skip
        y = jax.nn.gelu(y, approximate=False)
        return y * jax.nn.sigmoid(y @ glu_w + glu_b)

    out_c = finish(ys_c, u_ctx) if ctx_out else None
    return out_c, finish(ys_l, u_lat)


def hyena_two_sided_filters(n, w1, b1, w2, b2, w3, freq, log_decay):
    t = jnp.linspace(0.0, 1.0, n, dtype=F32)[:, None]
    w = 2.0 * math.pi * jnp.arange(n, dtype=F32)[:, None] / n
    bands = jnp.linspace(1e-4, HY_BANDS - 1, HY_BANDS, dtype=F32)[None, :]
    z = jnp.concatenate([t, jnp.cos(bands * w), -jnp.sin(bands * w)], axis=-1)
    freq = freq.astype(F32)
    hid = jnp.sin(freq[0] * (z @ w1.astype(F32) + b1.astype(F32)))
    hid = jnp.sin(freq[1] * (hid @ w2.astype(F32) + b2.astype(F32)))
    h = (hid @ w3.astype(F32)).reshape(n, 2, HY_ORDER, HY_CH)
    h = h * jnp.exp(-t[:, :, None, None] * jnp.exp(log_decay.astype(F32)))
    h_fwd, h_bwd = h[:, 0], h[:, 1]
    return jnp.concatenate([h_fwd, jnp.zeros_like(h_fwd[:1]), h_bwd[:0:-1]], axis=0)


def fft_long_conv(u, filt, bias):
    n = u.shape[1]
    u_f = jnp.fft.rfft(u.astype(F32), n=2 * n, axis=1)
    k_f = jnp.fft.rfft(filt, n=2 * n, axis=0)
    y = jnp.fft.irfft(u_f * k_f[None], n=2 * n, axis=1)[:, :n]
    return y + u * bias.astype(F32)


def hyena_mixer(p_ctx, p_lat, conv_w, conv_b, f_w1, f_b1, f_w2, f_b2, f_w3, f_freq,
                log_decay, bias, ctx_out):
    def run(p):
        n = p.shape[1]
        streams = jnp.split(short_conv(p, conv_w, conv_b), 1 + HY_ORDER, axis=-1)
        filt = hyena_two_sided_filters(n, f_w1, f_b1, f_w2, f_b2, f_w3, f_freq, log_decay)
        z = streams[0]
        for o in range(HY_ORDER):
            z = streams[1 + o] * fft_long_conv(z, filt[:, o], bias[o])
        return z

    out_c = run(p_ctx) if ctx_out else None
    return out_c, run(p_lat)


def rwkv_scan(s0, decay, k, v, kk, a, r, reverse):
    seq = [decay, k, v, kk, kk * a] + ([] if r is None else [r])
    xs = tuple(jnp.moveaxis(t.astype(F32), 1, 0) for t in seq)

    def step(S, inp):
        w_t, k_t, v_t, kk_t, b_t = inp[:5]
        sa = jnp.einsum('bhij,bhj->bhi', S, kk_t)
        S = (S * w_t[:, :, None, :] - sa[..., None] * b_t[:, :, None, :]
             + v_t[..., None] * k_t[:, :, None, :])
        y = None if r is None else jnp.einsum('bhij,bhj->bhi', S, inp[5])
        return S, y

    s_fin, ys = lax.scan(step, s0, xs, reverse=reverse)
    y = None if r is None else jnp.moveaxis(ys, 0, 1)
    return y, s_fin


def rwkv7_mixer(p_ctx, p_lat, conv_w, w0, w_up, a0, a_up, g_up, k_k, k_a, r_k,
                ln_g, ln_b, ctx_out):
    def heads(t):
        return t.reshape(t.shape[0], t.shape[1], RW_HEADS, RW_HEAD)

    def prep(p):
        r, k, v = jnp.split(short_conv(p[..., :3 * RW_CH], conv_w), 3, axis=-1)
        o = 3 * RW_CH
        w_lo = p[..., o:o + RW_DECAY_LORA]
        o += RW_DECAY_LORA
        a_lo = p[..., o:o + RW_A_LORA]
        o += RW_A_LORA
        g_lo = p[..., o:o + RW_G_LORA]
        return r, k, v, w_lo, a_lo, g_lo

    def direction(k, w_lo, a_lo, d):
        w = -jax.nn.softplus(-(w0[d] + jnp.tanh(w_lo) @ w_up[d])) - 0.5
        a = jax.nn.sigmoid(a0[d] + a_lo @ a_up[d])
        kk = heads(k * k_k)
        kk = kk / jnp.maximum(jnp.sqrt(jnp.sum(kk * kk, axis=-1, keepdims=True)), 1e-12)
        k_eff = heads(k * (1.0 + (a - 1.0) * k_a))
        return heads(jnp.exp(-jnp.exp(w))), k_eff, kk, heads(a)

    def post(y, r, k_sum, v, g_lo):
        bt, n = y.shape[:2]
        mu = jnp.mean(y, axis=-1, keepdims=True)
        var = jnp.mean(jnp.square(y - mu), axis=-1, keepdims=True)
        yn = ((y - mu) * lax.rsqrt(var + RW_LN_EPS)).reshape(bt, n, RW_CH) * ln_g + ln_b
        bonus = jnp.sum(heads(r) * k_sum * r_k, axis=-1, keepdims=True) * heads(v)
        gate = jax.nn.sigmoid(g_lo) @ g_up
        return (yn + bonus.reshape(bt, n, RW_CH)) * gate

    r_c, k_c, v_c, w_c, al_c, g_c = prep(p_ctx)
    r_l, k_l, v_l, w_l, al_l, g_l = prep(p_lat)
    s_zero = jnp.zeros((p_lat.shape[0], RW_HEADS, RW_HEAD, RW_HEAD), F32)
    y_cs, ks_c, y_ls, ks_l = [], [], [], []
    for d in range(2):
        rev = d == 1
        dec, ke, kk, aa = direction(k_c, w_c, al_c, d)
        y_cd, s_ctx = rwkv_scan(s_zero, dec, ke, heads(v_c), kk, aa,
                                heads(r_c) if ctx_out else None, rev)
        y_cs.append(y_cd)
        ks_c.append(ke)
        dec, ke, kk, aa = direction(k_l, w_l, al_l, d)
        y_ld, _ = rwkv_scan(s_ctx, dec, ke, heads(v_l), kk, aa, heads(r_l), rev)
        y_ls.append(y_ld)
        ks_l.append(ke)
    out_l = post(y_ls[0] + y_ls[1], r_l, ks_l[0] + ks_l[1], v_l, g_l)
    out_c = post(y_cs[0] + y_cs[1], r_c, ks_c[0] + ks_c[1], v_c, g_c) if ctx_out else None
    return out_c, out_l


def axial_rope_tables(n_lat):
    rows = n_lat // GRID_W
    row = jnp.repeat(jnp.arange(rows, dtype=F32), GRID_W)
    col = jnp.tile(jnp.arange(GRID_W, dtype=F32), rows)
    inv = ROPE_BASE ** (-jnp.arange(ROPE_FREQS, dtype=F32) / ROPE_FREQS)
    ang = jnp.stack([row[:, None] * inv, col[:, None] * inv], axis=1)
    return jnp.cos(ang), jnp.sin(ang)


def apply_axial_rope(x, cos, sin):
    shp = x.shape
    xr = x.reshape(shp[:-1] + (2, 2, ROPE_FREQS))
    x1, x2 = xr[..., 0, :], xr[..., 1, :]
    cos = cos[None, :, None, None]
    sin = sin[None, :, None, None]
    out = jnp.stack([x1 * cos - x2 * sin, x1 * sin + x2 * cos], axis=-2)
    return out.reshape(shp)


def diff_attn_block(q, k, v, lam):
    s = jnp.einsum('bqhmd,bshmd->bhmqs', q.astype(F32), k.astype(F32)) * DA_SCALE
    p = jax.nn.softmax(s, axis=-1)
    w = p[:, :, 0] - lam * p[:, :, 1]
    return jnp.einsum('bhqs,bshd->bqhd', w, v.astype(F32))


def diff_attention(p_ctx, p_lat, lam_p, subln_g, lam_init, rope, ctx_out):
    def split(p):
        bt, n = p.shape[:2]
        q = p[..., :DA_QK].reshape(bt, n, DA_HEADS, 2, DA_HEAD)
        k = p[..., DA_QK:2 * DA_QK].reshape(bt, n, DA_HEADS, 2, DA_HEAD)
        v = p[..., 2 * DA_QK:].reshape(bt, n, DA_HEADS, DA_V)
        return q, k, v

    def post(o):
        bt, n = o.shape[:2]
        on = o * lax.rsqrt(jnp.mean(o * o, axis=-1, keepdims=True) + DA_SUBLN_EPS)
        return (on * subln_g * (1.0 - lam_init)).reshape(bt, n, DA_VW)

    lp = lam_p.astype(F32)
    lam = jnp.exp(jnp.sum(lp[0] * lp[1])) - jnp.exp(jnp.sum(lp[2] * lp[3])) + lam_init
    q_c, k_c, v_c = split(p_ctx)
    q_l, k_l, v_l = split(p_lat)
    q_l = apply_axial_rope(q_l, *rope)
    k_l = apply_axial_rope(k_l, *rope)
    k_all = jnp.concatenate([k_l, k_c], axis=1)
    v_all = jnp.concatenate([v_l, v_c], axis=1)
    bt, n = q_l.shape[:2]
    nb = n // Q_BLOCK
    qb = jnp.moveaxis(q_l.reshape((bt, nb, Q_BLOCK) + q_l.shape[2:]), 1, 0)
    ob = lax.map(lambda qq: diff_attn_block(qq, k_all, v_all, lam), qb)
    o_l = jnp.moveaxis(ob, 0, 1).reshape(bt, n, DA_HEADS, DA_V)
    out_c = post(diff_attn_block(q_c, k_c, v_c, lam)) if ctx_out else None
    return out_c, post(o_l)


def setup_inputs(seed: int = 0) -> dict:
    key = jax.random.key(seed)
    ks = jax.random.split(key, 64)
    count = [0]

    def nxt():
        count[0] += 1
        return ks[count[0] - 1]

    def nrm(shape, std):
        return std * jax.random.normal(nxt(), shape, F32)

    def unif(shape, lo, hi):
        return jax.random.uniform(nxt(), shape, F32, lo, hi)

    D = D_MODEL
    E, O = N_EVEN, N_ODD
    G, P, H = S5_GROUPS, S5_STATE, S5_GROUP
    a_imag = math.pi * jnp.arange(P, dtype=F32)
    return {
        'x': nrm((BATCH, SEQ, D), 1.0),
        'c': nrm((BATCH, D), 1.0),
        'ctx': nrm((BATCH, CTX_LEN, D), 1.0),
        'c_ctx': nrm((D,), 1.0),
        'ada_w': nrm((DEPTH, D, 6 * D), 0.3 * D ** -0.5),
        'ada_b': nrm((DEPTH, 6 * D), 0.02),
        'norm1_g': 1.0 + nrm((DEPTH, D), 0.02),
        'norm2_g': 1.0 + nrm((DEPTH, D), 0.02),
        'mlp_w1': nrm((DEPTH, D, MLP_HIDDEN), D ** -0.5),
        'mlp_w2': nrm((DEPTH, MLP_HIDDEN, D), MLP_HIDDEN ** -0.5),
        'final_g': 1.0 + nrm((D,), 0.02),
        'ev_w_in': nrm((E, D, EVEN_IN), D ** -0.5),
        'ev_w_out': nrm((E, MIX_W, D), MIX_W ** -0.5),
        's5_lam_re': -0.5 + nrm((E, 2, G, P), 0.01),
        's5_lam_im': a_imag + nrm((E, 2, G, P), 0.01),
        's5_log_dt': unif((E, 2, G), math.log(S5_MIN_DT), math.log(S5_MAX_DT)),
        's5_b_re': nrm((E, 2, G, P, H), (2 * H) ** -0.5),
        's5_b_im': nrm((E, 2, G, P, H), (2 * H) ** -0.5),
        's5_c_re': nrm((E, 2, G, H, P), (2 * P) ** -0.5),
        's5_c_im': nrm((E, 2, G, H, P), (2 * P) ** -0.5),
        's5_d': nrm((E, S5_CH), 1.0),
        's5_glu_w': nrm((E, S5_CH, S5_CH), S5_CH ** -0.5),
        's5_glu_b': nrm((E, S5_CH), 0.02),
        'hy_conv_w': nrm((E, SHORT_K, (1 + HY_ORDER) * HY_CH), SHORT_K ** -0.5),
        'hy_conv_b': nrm((E, (1 + HY_ORDER) * HY_CH), 0.02),
        'hy_f_w1': nrm((E, HY_EMB, HY_FFN), HY_EMB ** -0.5),
        'hy_f_b1': nrm((E, HY_FFN), 0.1),
        'hy_f_w2': nrm((E, HY_FFN, HY_FFN), HY_FFN ** -0.5),
        'hy_f_b2': nrm((E, HY_FFN), 0.1),
        'hy_f_w3': nrm((E, HY_FFN, 2 * HY_ORDER * HY_CH), 0.01),
        'hy_f_freq': 1.0 + nrm((E, 2, HY_FFN), 0.05),
        'hy_log_decay': unif((E, 2, HY_ORDER, HY_CH), math.log(3.0), math.log(15.0)),
        'hy_bias': nrm((E, HY_ORDER, HY_CH), 0.1),
        'od_w_in': nrm((O, D, ODD_IN), D ** -0.5),
        'od_w_out': nrm((O, MIX_W, D), MIX_W ** -0.5),
        'rw_conv_w': nrm((O, SHORT_K, 3 * RW_CH), SHORT_K ** -0.5),
        'rw_w0': unif((O, 2, RW_CH), -6.0, -1.0),
        'rw_w_up': nrm((O, 2, RW_DECAY_LORA, RW_CH), 0.1 * RW_DECAY_LORA ** -0.5),
        'rw_a0': nrm((O, 2, RW_CH), 0.5),
        'rw_a_up': nrm((O, 2, RW_A_LORA, RW_CH), 0.1 * RW_A_LORA ** -0.5),
        'rw_g_up': nrm((O, RW_G_LORA, RW_CH), RW_G_LORA ** -0.5),
        'rw_k_k': 0.85 + nrm((O, RW_CH), 0.02),
        'rw_k_a': 1.0 + nrm((O, RW_CH), 0.02),
        'rw_r_k': nrm((O, RW_HEADS, RW_HEAD), 0.1),
        'rw_ln_g': 1.0 + nrm((O, RW_CH), 0.02),
        'rw_ln_b': nrm((O, RW_CH), 0.02),
        'da_lam': nrm((O, 4, DA_HEAD), 0.1),
        'da_subln_g': 1.0 + nrm((O, DA_V), 0.02),
    }


def reference(x, c, ctx, c_ctx, ada_w, ada_b, norm1_g, norm2_g, mlp_w1, mlp_w2, final_g,
              ev_w_in, ev_w_out, s5_lam_re, s5_lam_im, s5_log_dt, s5_b_re, s5_b_im,
              s5_c_re, s5_c_im, s5_d, s5_glu_w, s5_glu_b, hy_conv_w, hy_conv_b,
              hy_f_w1, hy_f_b1, hy_f_w2, hy_f_b2, hy_f_w3, hy_f_freq, hy_log_decay, hy_bias,
              od_w_in, od_w_out, rw_conv_w, rw_w0, rw_w_up, rw_a0, rw_a_up, rw_g_up,
              rw_k_k, rw_k_a, rw_r_k, rw_ln_g, rw_ln_b, da_lam, da_subln_g):
    lat = x.astype(F32)
    cx = ctx.astype(F32)
    silu_c = jax.nn.silu(c.astype(F32))
    silu_cc = jax.nn.silu(c_ctx.astype(F32))
    rope = axial_rope_tables(lat.shape[1])
    for l in range(DEPTH):
        keep_ctx = l < DEPTH - 1
        i = l // 2
        mods_l = jnp.split((silu_c @ ada_w[l] + ada_b[l])[:, None, :], 6, axis=-1)
        mods_c = jnp.split(silu_cc @ ada_w[l] + ada_b[l], 6, axis=-1)
        h_l = modulate(rmsnorm(lat, norm1_g[l]), mods_l[0], mods_l[1])
        h_c = modulate(rmsnorm(cx, norm1_g[l]), mods_c[0], mods_c[1])
        if l % 2 == 0:
            p_l = h_l @ ev_w_in[i]
            p_c = h_c @ ev_w_in[i]
            a_c, a_l = s5_mixer(p_c[..., :S5_CH], p_l[..., :S5_CH], s5_lam_re[i], s5_lam_im[i],
                                s5_log_dt[i], s5_b_re[i], s5_b_im[i], s5_c_re[i], s5_c_im[i],
                                s5_d[i], s5_glu_w[i], s5_glu_b[i], keep_ctx)
            b_c, b_l = hyena_mixer(p_c[..., S5_CH:], p_l[..., S5_CH:], hy_conv_w[i], hy_conv_b[i],
                                   hy_f_w1[i], hy_f_b1[i], hy_f_w2[i], hy_f_b2[i], hy_f_w3[i],
                                   hy_f_freq[i], hy_log_decay[i], hy_bias[i], keep_ctx)
            w_out = ev_w_out[i]
        else:
            p_l = h_l @ od_w_in[i]
            p_c = h_c @ od_w_in[i]
            a_c, a_l = rwkv7_mixer(p_c[..., :RW_IN], p_l[..., :RW_IN], rw_conv_w[i], rw_w0[i],
                                   rw_w_up[i], rw_a0[i], rw_a_up[i], rw_g_up[i], rw_k_k[i],
                                   rw_k_a[i], rw_r_k[i], rw_ln_g[i], rw_ln_b[i], keep_ctx)
            lam_init = 0.8 - 0.6 * math.exp(-0.3 * l)
            b_c, b_l = diff_attention(p_c[..., RW_IN:], p_l[..., RW_IN:], da_lam[i], da_subln_g[i],
                                      lam_init, rope, keep_ctx)
            w_out = od_w_out[i]
        lat = lat + mods_l[2] * (jnp.concatenate([a_l, b_l], axis=-1) @ w_out)
        lat = lat + mods_l[5] * sq_relu_mlp(
            modulate(rmsnorm(lat, norm2_g[l]), mods_l[3], mods_l[4]), mlp_w1[l], mlp_w2[l])
        if keep_ctx:
            cx = cx + mods_c[2] * (jnp.concatenate([a_c, b_c], axis=-1) @ w_out)
            cx = cx + mods_c[5] * sq_relu_mlp(
                modulate(rmsnorm(cx, norm2_g[l]), mods_c[3], mods_c[4]), mlp_w1[l], mlp_w2[l])
    return rmsnorm(lat, final_g)
```

```python
import math
from contextlib import ExitStack
import numpy as np
import ml_dtypes
import concourse.bass as bass
import concourse.mybir as mybir
from concourse.bass_utils import run_bass_kernel_spmd

F32 = mybir.dt.float32
BF16 = mybir.dt.bfloat16
AF = mybir.ActivationFunctionType
ALU = mybir.AluOpType
AX = mybir.AxisListType

D = 1024
NT = 2304
NC_ = 256
NL = 2048
DEPTH = 4
EPS = 1e-6
TILES = [(0, 512), (512, 512), (1024, 512), (1536, 512), (2048, 256)]


class Prog:
    KD = 12

    def __init__(self, nc, es):
        self.nc = nc
        self.eng = {'pe': nc.tensor, 'act': nc.scalar, 'dve': nc.vector, 'pool': nc.gpsimd, 'sp': nc.sync}
        self.csem = {e: es.enter_context(nc.semaphore('c_' + e)) for e in ['pe', 'act', 'dve', 'pool']}
        self.ccnt = {e: 0 for e in self.csem}
        self.dsem = {q: [es.enter_context(nc.semaphore('d_%s%d' % (q, i))) for i in range(self.KD)]
                     for q in ['sp', 'pool']}
        self.dcnt = {q: [0] * self.KD for q in self.dsem}
        self.dnext = {q: 0 for q in self.dsem}
        self.waited = {e: {} for e in self.eng}
        self.res = {}
        self.psn = 0

    def _wait(self, e, tok):
        sid, sem, val = tok
        if self.waited[e].get(sid, 0) >= val:
            return
        self.eng[e].wait_ge(sem, val)
        self.waited[e][sid] = val

    def _deps(self, e, r, w):
        toks = []
        for k in r:
            st = self.res.get(k)
            if st and st['w'] is not None:
                toks.append(st['w'])
            if st and k.startswith('ps'):
                for t in st['r'].values():
                    if t[0] != 'c_' + e:
                        toks.append(t)
        for k in w:
            st = self.res.get(k)
            if st:
                if st['w'] is not None and st['w'][0] != 'c_' + e:
                    toks.append(st['w'])
                for t in st['r'].values():
                    if t[0] != 'c_' + e:
                        toks.append(t)
        for t in toks:
            self._wait(e, t)

    def _rec(self, r, w, tok):
        for k in r:
            st = self.res.setdefault(k, {'w': None, 'r': {}})
            st['r'][tok[0]] = tok
        for k in w:
            self.res[k] = {'w': tok, 'r': {}}

    def op(self, e, fn, r=(), w=()):
        self._deps(e, r, w)
        inst = fn(self.eng[e])
        self.ccnt[e] += 1
        inst.then_inc(self.csem[e], 1)
        tok = ('c_' + e, self.csem[e], self.ccnt[e])
        self._rec(r, w, tok)
        return tok

    def dma(self, q, out, in_, r=(), w=(), **kw):
        k = self.dnext[q]
        self.dnext[q] = (k + 1) % self.KD
        sid = 'd_%s%d' % (q, k)
        if self.dcnt[q][k] > 0:
            self._wait(q, (sid, self.dsem[q][k], 16 * self.dcnt[q][k]))
        self._deps(q, r, w)
        inst = self.eng[q].dma_start(out=out, in_=in_, **kw)
        self.dcnt[q][k] += 1
        inst.then_inc(self.dsem[q][k], 16)
        tok = (sid, self.dsem[q][k], 16 * self.dcnt[q][k])
        self._rec(r, w, tok)
        return tok

    def barrier(self):
        toks = []
        for e in self.csem:
            if self.ccnt[e]:
                toks.append(('c_' + e, self.csem[e], self.ccnt[e]))
        for q in self.dsem:
            for k in range(self.KD):
                if self.dcnt[q][k]:
                    toks.append(('d_%s%d' % (q, k), self.dsem[q][k], 16 * self.dcnt[q][k]))
        for e in self.eng:
            for t in toks:
                self._wait(e, t)
        self.res = {}

    def ps(self):
        self.psn = (self.psn + 1) % 8
        return self.psn


class Ctx:
    pass


def mm(P, out, lhsT, rhs, start, stop, r, w):
    return P.op('pe', lambda e: e.matmul(out, lhsT, rhs, start=start, stop=stop), r=r, w=w)


def load_w_block(K, wdram, kc, c0, ncols, slot):
    P = K.P
    src = wdram.rearrange("(kc p) m -> p kc m", p=128)
    for k0 in range(0, kc, 8):
        kn = min(8, kc - k0)
        sl = K.wstn
        K.wstn = (K.wstn + 1) % 2
        P.dma('sp', K.wst[sl][:, :kn, :ncols], src[:, k0:k0 + kn, c0:c0 + ncols], w=['wst%d' % sl])
        eng = 'pool' if (K.wcast % 2 == 0) else 'act'
        K.wcast += 1
        if eng == 'pool':
            P.op('pool', lambda e: e.tensor_copy(K.wb[slot][:, k0:k0 + kn, :ncols], K.wst[sl][:, :kn, :ncols]),
                 r=['wst%d' % sl], w=['wb%d' % slot])
        else:
            P.op('act', lambda e: e.activation(out=K.wb[slot][:, k0:k0 + kn, :ncols], in_=K.wst[sl][:, :kn, :ncols],
                                               func=AF.Copy), r=['wst%d' % sl], w=['wb%d' % slot])


def stage_mods(K, l):
    P = K.P
    psb = P.ps()
    for blk in range(4):
        sl = blk % 2
        P.dma('sp', K.adst[sl][:], K.ada_w[l].rearrange("(kc p) m -> p kc m", p=128)[:, :, blk * 1536:(blk + 1) * 1536],
              w=['adst%d' % sl])
        for mc in range(12):
            m = blk * 12 + mc
            for k in range(8):
                mm(P, K.psum[psb][:, 2 * m:2 * m + 2], K.adst[sl][:, k, mc * 128:(mc + 1) * 128], K.sc[:, k, :],
                   k == 0, k == 7, r=['adst%d' % sl, 'sc'], w=['ps%d' % psb])
    mv = K.modv[l]
    P.op('dve', lambda e: e.tensor_tensor(out=mv[:].rearrange("p a b -> p (a b)"), in0=K.psum[psb][:, 0:96],
                                          in1=K.adab[:, l].rearrange("p a b -> p (a b)"), op=ALU.add),
         r=['ps%d' % psb], w=['modv%d' % l])
    for j, (sci, g) in enumerate([(1, K.g1), (4, K.g2)]):
        P.op('dve', lambda e: e.tensor_scalar(out=K.gs[l][:, j], in0=mv[:, sci * 8:(sci + 1) * 8, :], scalar1=1.0,
                                              scalar2=None, op0=ALU.add), r=['modv%d' % l], w=['gs%d' % l])
        P.op('dve', lambda e: e.tensor_tensor(out=K.gs[l][:, j], in0=K.gs[l][:, j], in1=g[:, l], op=ALU.mult),
             r=['gs%d' % l], w=['gs%d' % l])


def stage_norm(K, l, j, gs_ap=None, shift_ap=None, out_dt_bf=True):
    P = K.P
    for ti in range(9):
        t0 = ti * 256
        col = 1 if ti == 0 else 0
        P.op('act', lambda e: e.activation(out=K.sq[:], in_=K.X[:, :, t0:t0 + 256], func=AF.Square),
             r=['X'], w=['sq'])
        pb = P.ps()
        for k in range(8):
            mm(P, K.psum[pb][:, 0:256], K.ones[:], K.sq[:, k, :], k == 0, k == 7, r=['sq'], w=['ps%d' % pb])
        P.op('act', lambda e: e.activation(out=K.rstd[:], in_=K.psum[pb][:, 0:256], func=AF.Sqrt, scale=1.0 / D,
                                           bias=K.epsc[:, 0:1]), r=['ps%d' % pb], w=['rstd'])
        P.op('dve', lambda e: e.reciprocal(K.rstd[:], K.rstd[:]), r=['rstd'], w=['rstd'])
        for k in range(8):
            if gs_ap is None:
                g_ap = K.gs[l][:, j, k, col:col + 1]
                s_ap = K.modv[l][:, (3 * j) * 8 + k, col:col + 1]
            else:
                g_ap = gs_ap[:, k:k + 1]
                s_ap = None
            P.op('dve', lambda e: e.scalar_tensor_tensor(out=K.ntmp[:, k, :], in0=K.X[:, k, t0:t0 + 256], scalar=g_ap,
                                                         in1=K.rstd[:], op0=ALU.mult, op1=ALU.mult),
                 r=['X', 'rstd', 'gs%d' % l], w=['ntmp%d' % k])
            if s_ap is not None:
                P.op('act', lambda e: e.activation(out=K.Hn[:, k, t0:t0 + 256], in_=K.ntmp[:, k, :], func=AF.Identity,
                                                   bias=s_ap, scale=1.0),
                     r=['ntmp%d' % k, 'modv%d' % l], w=['Hn'])
            else:
                P.op('act', lambda e: e.activation(out=K.Yf[:, k, t0:t0 + 256], in_=K.ntmp[:, k, :], func=AF.Copy),
                     r=['ntmp%d' % k], w=['Yf'])


def stage_proj(K, wdram, fin, out_dram):
    P = K.P
    nb = (fin + 511) // 512
    for b in range(nb):
        c0 = b * 512
        ncols = min(512, fin - c0)
        slot = b % 2
        load_w_block(K, wdram, 8, c0, ncols, slot)
        for mc in range(ncols // 128):
            ss = K.stn
            K.stn = (K.stn + 1) % 2
            for (t0, tn) in TILES:
                pb = P.ps()
                for k in range(8):
                    mm(P, K.psum[pb][:, :tn], K.wb[slot][:, k, mc * 128:(mc + 1) * 128], K.Hn[:, k, t0:t0 + tn],
                       k == 0, k == 7, r=['wb%d' % slot, 'Hn'], w=['ps%d' % pb])
                P.op('act', lambda e: e.activation(out=K.stg[ss][:, t0:t0 + tn], in_=K.psum[pb][:, :tn], func=AF.Copy),
                     r=['ps%d' % pb], w=['stg%d' % ss])
            P.dma('sp', out_dram[c0 + mc * 128:c0 + (mc + 1) * 128, :], K.stg[ss][:], r=['stg%d' % ss],
                  w=['pfm'])


def resid_evac(K, l, gi, pb, m, t0, tn):
    P = K.P
    segs = []
    if t0 < NC_:
        segs.append((t0, NC_ - t0, 1))
        segs.append((NC_, t0 + tn - NC_, 0))
    else:
        segs.append((t0, tn, 0))
    for (a, n, col) in segs:
        gate = K.modv[l][:, gi * 8 + m, col:col + 1]
        P.op('dve', lambda e: e.scalar_tensor_tensor(out=K.X[:, m, a:a + n], in0=K.psum[pb][:, a - t0:a - t0 + n],
                                                     scalar=gate, in1=K.X[:, m, a:a + n], op0=ALU.mult, op1=ALU.add),
             r=['ps%d' % pb, 'modv%d' % l, 'X'], w=['X'])


def stage_outproj(K, l, wdram, mix_dram):
    P = K.P
    P.dma('sp', K.Hn[:], mix_dram.rearrange("(kc p) t -> p kc t", p=128), r=['mix'], w=['Hn'])
    for b in range(2):
        slot = b % 2
        load_w_block(K, wdram, 8, b * 512, 512, slot)
        for mc in range(4):
            m = b * 4 + mc
            for (t0, tn) in TILES:
                pb = P.ps()
                for k in range(8):
                    mm(P, K.psum[pb][:, :tn], K.wb[slot][:, k, mc * 128:(mc + 1) * 128], K.Hn[:, k, t0:t0 + tn],
                       k == 0, k == 7, r=['wb%d' % slot, 'Hn'], w=['ps%d' % pb])
                resid_evac(K, l, 2, pb, m, t0, tn)


def stage_mlp(K, l):
    P = K.P
    w1 = K.mlp_w1[l]
    w2 = K.mlp_w2[l]
    for b in range(8):
        slot = b % 2
        load_w_block(K, w1, 8, b * 512, 512, slot)
        for mc in range(4):
            ss = K.stn
            K.stn = (K.stn + 1) % 2
            for (t0, tn) in TILES:
                pb = P.ps()
                for k in range(8):
                    mm(P, K.psum[pb][:, :tn], K.wb[slot][:, k, mc * 128:(mc + 1) * 128], K.Hn[:, k, t0:t0 + tn],
                       k == 0, k == 7, r=['wb%d' % slot, 'Hn'], w=['ps%d' % pb])
                P.op('act', lambda e: e.activation(out=K.rl[:, :tn], in_=K.psum[pb][:, :tn], func=AF.Relu),
                     r=['ps%d' % pb], w=['rl'])
                P.op('dve', lambda e: e.tensor_tensor(out=K.stgb[ss][:, t0:t0 + tn], in0=K.rl[:, :tn], in1=K.rl[:, :tn],
                                                      op=ALU.mult), r=['rl'], w=['stgb%d' % ss])
            r0 = b * 512 + mc * 128
            P.dma('sp', K.hid[r0:r0 + 128, :], K.stgb[ss][:], r=['stgb%d' % ss], w=['hid'])


def stage_mlp2(K, l):
    P = K.P
    w2 = K.mlp_w2[l]
    hsrc = K.hid.rearrange("(kc p) t -> p kc t", p=128)
    for b in range(4):
        load_w_block_to(K, w2, 32, b * 256, 256, K.wb2, 'wb2')
        for ti, (t0, tn) in enumerate(TILES):
            hs = ti % 2
            for kq in range(4):
                P.dma('sp', K.hb[hs][:, kq * 8:(kq + 1) * 8, :tn], hsrc[:, kq * 8:(kq + 1) * 8, t0:t0 + tn], r=['hid'],
                      w=['hb%d' % hs])
            for mc in range(2):
                m = b * 2 + mc
                pb = P.ps()
                for k in range(32):
                    mm(P, K.psum[pb][:, :tn], K.wb2[:, k, mc * 128:(mc + 1) * 128], K.hb[hs][:, k, :tn],
                       k == 0, k == 31, r=['wb2', 'hb%d' % hs], w=['ps%d' % pb])
                resid_evac(K, l, 5, pb, m, t0, tn)


def load_w_block_to(K, wdram, kc, c0, ncols, dst, key):
    P = K.P
    src = wdram.rearrange("(kc p) m -> p kc m", p=128)
    for k0 in range(0, kc, 8):
        sl = K.wstn
        K.wstn = (K.wstn + 1) % 2
        P.dma('sp', K.wst[sl][:, :8, :ncols], src[:, k0:k0 + 8, c0:c0 + ncols], w=['wst%d' % sl])
        eng = 'pool' if (K.wcast % 2 == 0) else 'act'
        K.wcast += 1
        if eng == 'pool':
            P.op('pool', lambda e: e.tensor_copy(dst[:, k0:k0 + 8, :ncols], K.wst[sl][:, :8, :ncols]),
                 r=['wst%d' % sl], w=[key])
        else:
            P.op('act', lambda e: e.activation(out=dst[:, k0:k0 + 8, :ncols], in_=K.wst[sl][:, :8, :ncols],
                                               func=AF.Copy), r=['wst%d' % sl], w=[key])


def stage_final(K):
    P = K.P
    stage_norm(K, 0, 0, gs_ap=K.fg)
    for tc in range(16):
        t0 = NC_ + tc * 128
        ss = tc % 2
        for half in range(2):
            pb = P.ps()
            for kk in range(4):
                k = half * 4 + kk
                P.op('pe', lambda e: e.transpose(K.psum[pb][:, kk * 128:(kk + 1) * 128], K.Yf[:, k, t0:t0 + 128],
                                                 K.ident[:]), r=['Yf'], w=['ps%d' % pb])
            P.op('act' if half else 'dve',
                 (lambda e: e.activation(out=K.ot[ss][:, half * 512:(half + 1) * 512], in_=K.psum[pb][:], func=AF.Copy))
                 if half else
                 (lambda e: e.tensor_copy(K.ot[ss][:, half * 512:(half + 1) * 512], K.psum[pb][:])),
                 r=['ps%d' % pb], w=['ot%d' % ss])
        P.dma('sp', K.out[tc * 128:(tc + 1) * 128, :], K.ot[ss][:], r=['ot%d' % ss], w=['out'])


def build(cfg):
    nc = bass.Bass("TRN2", target_bir_lowering=False)
    K = Ctx()
    K.nc = nc
    K.cfg = cfg
    K.uid = 0

    def mk(es_):
        def f(name, shape, dt=F32):
            K.uid += 1
            return es_.enter_context(nc.sbuf_tensor(name + '_u%d' % K.uid, list(shape), dt))
        return f
    K.mk = mk

    def din(name, shape, dt=F32):
        return nc.dram_tensor(name, list(shape), dt, kind="ExternalInput").ap()

    def dscr(name, shape, dt=F32):
        kind = "ExternalOutput" if name in cfg.get('dbg', ()) else "Internal"
        return nc.dram_tensor(name, list(shape), dt, kind=kind).ap()

    if cfg.get('mixer_test') is not None:
        _din = din

        def din(name, shape, dt=F32):
            if name in ('ada_w', 'mlp_w1', 'mlp_w2', 'ev_w_in', 'ev_w_out', 'od_w_in', 'od_w_out', 'xin'):
                return None
            return _din(name, shape, dt)
    K.xin = din("xin", [8, 128, NT])
    K.cc = din("cc", [128, 8, 2])
    K.ada_w = din("ada_w", [DEPTH, D, 6 * D])
    adab_d = din("adab", [128, DEPTH, 48, 2])
    g1_d = din("g1", [128, DEPTH, 8, 2])
    g2_d = din("g2", [128, DEPTH, 8, 2])
    fg_d = din("fg", [128, 8])
    ident_d = din("ident", [128, 128])
    K.mlp_w1 = din("mlp_w1", [DEPTH, D, 4 * D])
    K.mlp_w2 = din("mlp_w2", [DEPTH, 4 * D, D])
    K.ev_w_in = din("ev_w_in", [2, D, 2048])
    K.ev_w_out = din("ev_w_out", [2, D, D])
    K.od_w_in = din("od_w_in", [2, D, 3328])
    K.od_w_out = din("od_w_out", [2, D, D])
    K.out = nc.dram_tensor("out", [NL, D], F32, kind="ExternalOutput").ap()
    K.hid = dscr("hid", [4 * D, NT], BF16)
    K.pfm = din("pfm", [3328, NT]) if cfg.get('mixer_test') is not None else dscr("pfm", [3328, NT], F32)
    declare_mixer_inputs(K, din)
    if cfg.get('mix_in'):
        K.mixd = [din("mix%d" % l, [D, NT], BF16) for l in range(DEPTH)]
    else:
        K.mixd = [dscr("mix", [D, NT], BF16)] * DEPTH
    if cfg.get('mixer_test') is not None:
        with ExitStack() as es:
            P = K.P = Prog(nc, es)
            K.psum = [es.enter_context(nc.psum_tensor("psb%d" % i, [128, 512], F32)) for i in range(8)]
            K.ones = es.enter_context(nc.sbuf_tensor("ones", [128, 128], BF16))
            K.ident = es.enter_context(nc.sbuf_tensor("ident_s", [128, 128], F32))
            P.op('dve', lambda e: e.memset(K.ones[:], 1.0), w=['ones'])
            P.dma('sp', K.ident[:], ident_d, w=['g'])
            P.barrier()
            run_mixer(K, cfg['mixer_test'], cfg.get('which', 'ab'))
            P.barrier()
        return nc
    K.xdbg = [dscr("xdbg%d" % l, [8, 128, NT]) for l in range(DEPTH)] if cfg.get('xdbg') else None

    with ExitStack() as es:
        P = K.P = Prog(nc, es)

        def sb(name, shape, dt=F32):
            return es.enter_context(nc.sbuf_tensor(name, list(shape), dt))
        K.psum = [es.enter_context(nc.psum_tensor("psb%d" % i, [128, 512], F32)) for i in range(8)]
        K.sc = sb("sc", [128, 8, 2])
        K.adab = sb("adab_s", [128, DEPTH, 48, 2])
        K.g1 = sb("g1_s", [128, DEPTH, 8, 2])
        K.g2 = sb("g2_s", [128, DEPTH, 8, 2])
        K.fg = sb("fg_s", [128, 8])
        K.ident = sb("ident_s", [128, 128])
        K.ones = sb("ones", [128, 128], BF16)
        K.epsc = sb("epsc", [128, 1])
        K.modv = [sb("modv%d" % l, [128, 48, 2]) for l in range(DEPTH)]
        K.gs = [sb("gs%d" % l, [128, 2, 8, 2]) for l in range(DEPTH)]
        K.rstd = sb("rstd", [128, 256])
        K.wstn = 0
        K.wcast = 0
        K.stn = 0
        P.dma('sp', K.sc[:], K.cc, w=['sc'])
        P.dma('sp', K.adab[:], adab_d, w=['adab'])
        P.dma('sp', K.g1[:], g1_d, w=['g'])
        P.dma('sp', K.g2[:], g2_d, w=['g'])
        P.dma('sp', K.fg[:], fg_d, w=['g'])
        P.dma('sp', K.ident[:], ident_d, w=['g'])
        P.op('dve', lambda e: e.memset(K.ones[:], 1.0), w=['ones'])
        P.op('dve', lambda e: e.memset(K.epsc[:], EPS), w=['ones'])
        P.op('act', lambda e: e.activation(out=K.sc[:], in_=K.sc[:], func=AF.Silu), r=['sc'], w=['sc'])
        P.barrier()
        with ExitStack() as es2:
            K.uid = 0
            K.adst = [es2.enter_context(nc.sbuf_tensor("adst%d" % i, [128, 8, 1536], F32)) for i in range(2)]
            for l in range(DEPTH):
                stage_mods(K, l)
            P.barrier()
        K.xres = nc.dram_tensor("xres", [8, 128, NT], F32).ap()

        def load_X(src):
            P.dma('sp', K.X[:, 0:4], src.rearrange("k p t -> p k t")[:, 0:4], r=['xres'], w=['X'])
            P.dma('sp', K.X[:, 4:8], src.rearrange("k p t -> p k t")[:, 4:8], r=['xres'], w=['X'])
        for l in range(cfg.get('nlayers', DEPTH)):
            i = l // 2
            with ExitStack() as es2:
                sb2 = K.mk(es2)
                K.X = sb2("X", [128, 8, NT])
                K.Hn = sb2("Hn", [128, 8, NT], BF16)
                K.sq = sb2("sq", [128, 8, 256], BF16)
                K.ntmp = sb2("ntmp", [128, 8, 256])
                K.wst = [sb2("wst%d" % j, [128, 8, 512]) for j in range(2)]
                K.wb = [sb2("wb%d" % j, [128, 8, 512], BF16) for j in range(2)]
                K.stg = [sb2("stg%d" % j, [128, NT]) for j in range(2)]
                load_X(K.xin if l == 0 else K.xres)
                stage_norm(K, l, 0)
                if not cfg.get('mix_in'):
                    if l % 2 == 0:
                        stage_proj(K, K.ev_w_in[i], 2048, K.pfm)
                    else:
                        stage_proj(K, K.od_w_in[i], 3328, K.pfm)
                P.barrier()
            if not cfg.get('mix_in'):
                run_mixer(K, l)
                P.barrier()
            with ExitStack() as es1:
                K.X = K.mk(es1)("X", [128, 8, NT])
                load_X(K.xin if l == 0 else K.xres)
                with ExitStack() as es2:
                    sb2 = K.mk(es2)
                    K.Hn = sb2("Hn", [128, 8, NT], BF16)
                    K.sq = sb2("sq", [128, 8, 256], BF16)
                    K.ntmp = sb2("ntmp", [128, 8, 256])
                    K.wst = [sb2("wst%d" % j, [128, 8, 512]) for j in range(2)]
                    K.wb = [sb2("wb%d" % j, [128, 8, 512], BF16) for j in range(2)]
                    K.stgb = [sb2("stgb%d" % j, [128, NT], BF16) for j in range(2)]
                    K.rl = sb2("rl", [128, 512])
                    stage_outproj(K, l, (K.ev_w_out if l % 2 == 0 else K.od_w_out)[i], K.mixd[l])
                    stage_norm(K, l, 1)
                    stage_mlp(K, l)
                    P.barrier()
                with ExitStack() as es2:
                    sb2 = K.mk(es2)
                    K.wst = [sb2("wst%d" % j, [128, 8, 512]) for j in range(2)]
                    K.wb2 = sb2("wb2", [128, 32, 256], BF16)
                    K.hb = [sb2("hb%d" % j, [128, 32, 512], BF16) for j in range(2)]
                    stage_mlp2(K, l)
                    P.barrier()
                P.dma('sp', K.xres.rearrange("k p t -> p k t"), K.X[:], r=['X'], w=['xres'])
                if K.xdbg is not None:
                    P.dma('sp', K.xdbg[l].rearrange("k p t -> p k t"), K.X[:], r=['X'], w=['xdbg'])
                P.barrier()
        with ExitStack() as es2:
            sb2 = K.mk(es2)
            K.X = sb2("X", [128, 8, NT])
            K.sq = sb2("sq", [128, 8, 256], BF16)
            K.ntmp = sb2("ntmp", [128, 8, 256])
            K.Yf = sb2("Yf", [128, 8, NT])
            K.ot = [sb2("ot%d" % j, [128, D]) for j in range(2)]
            load_X(K.xres)
            stage_final(K)
            P.barrier()
    return nc


def host_common(inp, b):
    f = np.float32
    x = np.concatenate([inp['ctx'][b], inp['x'][b]], axis=0)
    m = {}
    m['xin'] = np.ascontiguousarray(x.T.reshape(8, 128, NT)).astype(f)
    cc = np.stack([inp['c'][b], inp['c_ctx']], axis=-1)
    m['cc'] = np.ascontiguousarray(cc.reshape(8, 128, 2).transpose(1, 0, 2)).astype(f)
    m['ada_w'] = inp['ada_w']
    ab = inp['ada_b'].reshape(DEPTH, 48, 128).transpose(2, 0, 1)
    m['adab'] = np.ascontiguousarray(np.repeat(ab[..., None], 2, axis=-1)).astype(f)
    for nm, key in [('g1', 'norm1_g'), ('g2', 'norm2_g')]:
        g = inp[key].reshape(DEPTH, 8, 128).transpose(2, 0, 1)
        m[nm] = np.ascontiguousarray(np.repeat(g[..., None], 2, axis=-1)).astype(f)
    m['fg'] = np.ascontiguousarray(inp['final_g'].reshape(8, 128).T).astype(f)
    m['ident'] = np.eye(128, dtype=f)
    for k in ['mlp_w1', 'mlp_w2', 'ev_w_in', 'ev_w_out', 'od_w_in', 'od_w_out']:
        m[k] = inp[k]
    return m


RW_IN = 1792


def declare_mixer_inputs(K, din):
    K.rw_cw = din("rw_cw", [128, 2, 12, 3])
    K.rw_w0a0 = din("rw_w0a0", [128, 2, 2, 4, 2])
    K.rw_wup = din("rw_wup", [2, 2, 64, 512])
    K.rw_aup = din("rw_aup", [2, 2, 64, 512])
    K.rw_gup = din("rw_gup", [2, 128, 512])
    K.rw_cols = din("rw_cols", [128, 2, 4, 5])
    K.rw_blk = din("rw_blk", [128, 128], BF16)
    K.rw_mk = din("rw_mk", [128, 2, 128])
    K.rw_mk3 = din("rw_mk3", [128, 2, 64])
    K.rw_idb = din("rw_idb", [128, 64], BF16)
    if K.cfg.get('mixer_test') is not None or True:
        dt_ = lambda nm, shp, d=F32: K.nc.dram_tensor(nm, list(shp), d).ap()
        K.rw_QR = dt_("rw_QR", [512, 36, 128], BF16)
        K.rw_BK = dt_("rw_BK", [512, 36, 128], BF16)
        K.rw_V = dt_("rw_V", [512, NT], BF16)
        K.rw_gC = dt_("rw_gC", [512, 36])
        K.rw_keff = dt_("rw_keff", [2, 512, NT])
        K.rw_rv = dt_("rw_rv", [2, 512, NT])
    K.dft = {2048: (din("dftc", [16, 128, 16, 128], BF16), din("dfts", [16, 128, 16, 128], BF16)),
             256: (din("dcc", [2, 128, 2, 128], BF16), din("dcs", [2, 128, 2, 128], BF16))}
    K.hy_zT = {2048: din("hy_zT_l", [33, 2048]), 256: din("hy_zT_c", [33, 256])}
    K.hy_tcol = {2048: din("hy_tcol_l", [128, 16]), 256: din("hy_tcol_c", [128, 2])}
    K.hy_w1 = din("hy_w1", [2, 33, 64])
    K.hy_w2 = din("hy_w2", [2, 64, 64])
    K.hy_w3 = din("hy_w3", [2, 64, 2048])
    K.hy_cols = din("hy_cols", [64, 2, 4])
    K.hy_ld = din("hy_ld", [128, 2, 2048])
    K.hy_biasb = din("hy_biasb", [128, 2, 1024])
    K.hy_cw = din("hy_cw", [128, 2, 12, 4])
    K.altc = din("altc", [128, 1], BF16)
    K.altr = din("altr", [1, 128], BF16)
    K.identb = din("identb", [128, 128], BF16)
    K.s5_par = din("s5_par", [128, 2, 32, 3])
    K.s5_B = din("s5_B", [2, 32, 128, 2, 128])
    K.s5_C = din("s5_C", [2, 32, 128, 2, 128])
    K.s5_dg = din("s5_dg", [128, 2, 2, 4])
    K.s5_glu_w = din("s5_glu_w", [2, 512, 512])
    K.iot = din("iot", [128, 96])
    K.da_lamb = din("da_lamb", [128, 2, 256])
    K.da_g = din("da_g", [2, 128, 1])
    K.ropec = din("ropec", [128, NL])
    K.ropes = din("ropes", [128, NL])


def host_mixer(inp):
    f = np.float32
    m = {}
    cwr = inp['rw_conv_w']
    m['rw_cw'] = np.ascontiguousarray(cwr.reshape(2, 3, 12, 128).transpose(3, 0, 2, 1)).astype(f)
    wa = np.stack([inp['rw_w0'], inp['rw_a0']], axis=-1)
    m['rw_w0a0'] = np.ascontiguousarray(wa.reshape(2, 2, 4, 128, 2).transpose(3, 0, 1, 2, 4)).astype(f)
    m['rw_wup'] = inp['rw_w_up']
    m['rw_aup'] = inp['rw_a_up']
    m['rw_gup'] = inp['rw_g_up']
    cl = np.stack([inp['rw_k_k'], inp['rw_k_a'], inp['rw_r_k'].reshape(2, 512), inp['rw_ln_g'], inp['rw_ln_b']], axis=-1)
    m['rw_cols'] = np.ascontiguousarray(cl.reshape(2, 4, 128, 5).transpose(2, 0, 1, 3)).astype(f)
    pp = np.arange(128)
    m['rw_blk'] = (pp[:, None] // 64 == pp[None, :] // 64).astype(ml_dtypes.bfloat16)
    s_ = (pp % 64)[:, None]
    t_ = np.arange(64)[None, :]
    su = (s_ < t_).astype(f)
    iu = (s_ <= t_).astype(f)
    mk = np.zeros((128, 2, 128), f)
    mk[:, 0, 0:64] = -su
    mk[:, 0, 64:128] = iu
    mk[:, 1, 0:64] = su
    mk[:, 1, 64:128] = iu
    m['rw_mk'] = mk
    mk3 = np.zeros((128, 2, 64), f)
    mk3[:, 0] = -(t_ < s_).astype(f)
    mk3[:, 1] = (s_ == t_).astype(f)
    m['rw_mk3'] = mk3
    m['rw_idb'] = (s_ == t_).astype(ml_dtypes.bfloat16)
    bf = ml_dtypes.bfloat16
    for n, (nc_, ns_) in [(2048, ('dftc', 'dfts')), (256, ('dcc', 'dcs'))]:
        N = 2 * n
        a = np.arange(n, dtype=np.int64)
        ph = (np.outer(a, a) % N).astype(np.float64) * (2 * np.pi / N)
        nb = n // 128
        for nm, fn in [(nc_, np.cos), (ns_, np.sin)]:
            T = fn(ph).reshape(nb, 128, nb, 128)
            m[nm] = np.ascontiguousarray(T.transpose(2, 1, 0, 3)).astype(bf)
        t = np.linspace(0.0, 1.0, n, dtype=f)[:, None]
        w = (f(2.0 * math.pi) * np.arange(n, dtype=f)[:, None] / f(n)).astype(f)
        bands = np.linspace(1e-4, 15, 16, dtype=f)[None, :]
        z = np.concatenate([t, np.cos(bands * w), -np.sin(bands * w)], axis=-1).astype(f)
        sfx = 'l' if n == 2048 else 'c'
        m['hy_zT_' + sfx] = np.ascontiguousarray(z.T)
        m['hy_tcol_' + sfx] = np.ascontiguousarray(t[:, 0].reshape(nb, 128).T)
    m['hy_w1'] = inp['hy_f_w1']
    m['hy_w2'] = inp['hy_f_w2']
    m['hy_w3'] = inp['hy_f_w3']
    m['hy_cols'] = np.ascontiguousarray(np.stack([inp['hy_f_b1'], inp['hy_f_b2'], inp['hy_f_freq'][:, 0],
                                                  inp['hy_f_freq'][:, 1]], axis=-1).transpose(1, 0, 2)).astype(f)
    m['hy_ld'] = np.ascontiguousarray(np.broadcast_to(inp['hy_log_decay'].reshape(1, 2, 2048), (128, 2, 2048))).astype(f)
    m['hy_biasb'] = np.ascontiguousarray(np.broadcast_to(inp['hy_bias'].reshape(1, 2, 1024), (128, 2, 1024))).astype(f)
    cw = np.concatenate([inp['hy_conv_w'], inp['hy_conv_b'][:, None, :]], axis=1)
    m['hy_cw'] = np.ascontiguousarray(cw.reshape(2, 4, 12, 128).transpose(3, 0, 2, 1)).astype(f)
    alt = np.where(np.arange(128) % 2 == 0, 1.0, -1.0)
    m['altc'] = alt.reshape(128, 1).astype(bf)
    m['altr'] = alt.reshape(1, 128).astype(bf)
    m['identb'] = np.eye(128).astype(bf)
    par = np.zeros((128, 2, 32, 3), f)
    Bp = np.zeros((2, 32, 128, 2, 128), f)
    Cp = np.zeros((2, 32, 128, 2, 128), f)
    for i in range(2):
        for d in range(2):
            for gp in range(16):
                idx = d * 16 + gp
                for g2 in range(2):
                    g = 2 * gp + g2
                    st = slice(g2 * 64, g2 * 64 + 64)
                    par[st, i, idx, 0] = inp['s5_lam_re'][i, d, g]
                    par[st, i, idx, 1] = inp['s5_lam_im'][i, d, g]
                    par[st, i, idx, 2] = inp['s5_log_dt'][i, d, g]
                    ch = slice((gp % 4) * 32 + g2 * 16, (gp % 4) * 32 + g2 * 16 + 16)
                    Bp[i, idx, ch, 0, st] = inp['s5_b_re'][i, d, g].T
                    Bp[i, idx, ch, 1, st] = inp['s5_b_im'][i, d, g].T
                    Cp[i, idx, st, 0, ch] = inp['s5_c_re'][i, d, g].T
                    Cp[i, idx, st, 1, ch] = inp['s5_c_im'][i, d, g].T
    m['s5_par'] = par
    m['s5_B'] = Bp
    m['s5_C'] = Cp
    dg = np.zeros((128, 2, 2, 4), f)
    for i in range(2):
        dg[:, i, 0, :] = inp['s5_d'][i].reshape(4, 128).T
        dg[:, i, 1, :] = inp['s5_glu_b'][i].reshape(4, 128).T
    m['s5_dg'] = dg
    m['s5_glu_w'] = inp['s5_glu_w']
    m['iot'] = np.ascontiguousarray(np.broadcast_to(
        np.concatenate([np.arange(48), 48 * np.arange(48)]).astype(f)[None], (128, 96)))
    m['da_lamb'] = np.ascontiguousarray(np.broadcast_to(inp['da_lam'].reshape(1, 2, 256), (128, 2, 256))).astype(f)
    m['da_g'] = np.ascontiguousarray(inp['da_subln_g'].reshape(2, 128, 1)).astype(f)
    p = np.arange(128)
    d = p % 64
    a = d // 32
    fr = d % 16
    half = (d % 32) // 16
    t = np.arange(NL)
    row = (t // 64).astype(f)
    col = (t % 64).astype(f)
    inv = (f(10000.0) ** (-(np.arange(16, dtype=f)) / f(16))).astype(f)
    pos = np.where(a[:, None] == 0, row[None, :], col[None, :]).astype(f)
    ang = (pos * inv[fr][:, None]).astype(f)
    m['ropec'] = np.cos(ang).astype(f)
    m['ropes'] = (np.sin(ang) * np.where(half == 0, -1.0, 1.0)[:, None]).astype(f)
    return m


def run_mixer(K, l, which='ab'):
    if l % 2 == 0:
        if 'a' in which:
            mixer_s5(K, l)
            K.P.barrier()
        if 'b' in which:
            mixer_hyena(K, l)
            K.P.barrier()
    if l % 2 == 1:
        if 'a' in which:
            mixer_rwkv(K, l)
            K.P.barrier()
        if 'b' in which:
            mixer_attn(K, l)
            K.P.barrier()


def mixer_attn(K, l):
    i = l // 2
    P = K.P
    cnt = 0
    lam_init = 0.8 - 0.6 * math.exp(-0.3 * l)
    QR = RW_IN
    with ExitStack() as es:
        sb = K.mk(es)
        xf = [sb('xf%d' % j, [128, NT]) for j in range(2)]
        xsw = [sb('xsw%d' % j, [128, NL]) for j in range(2)]
        t1 = sb('t1', [128, NL])
        t2 = sb('t2', [128, NL])
        cosT = sb('cosT', [128, NL])
        sinT = sb('sinT', [128, NL])
        qb = sb('qb', [128, 4, NT], BF16)
        kb = sb('kb', [128, 4, NT], BF16)
        vtm = sb('vtm', [128, 18, 512], BF16)
        lamt = sb('lamt', [128, 256])
        pr = sb('pr', [128, 128])
        sv = sb('sv', [128, 2])
        ev = sb('ev', [128, 2])
        nlam = sb('nlam', [128, 1])
        gsub = sb('gsub', [128, 1])
        eps5 = sb('eps5', [128, 1])
        E = [sb('E%d' % j, [128, 512], BF16) for j in range(3)]
        rz = [sb('rz%d' % j, [128, 512]) for j in range(2)]
        tt = [sb('tt%d' % j, [128, 512]) for j in range(2)]
        osb = sb('osb', [128, 512])
        sqb = sb('sqb', [128, 512], BF16)
        rs = sb('rs', [128, 512])
        ostg = [sb('ostg%d' % j, [128, NT], BF16) for j in range(2)]
        P.dma('sp', cosT[:], K.ropec, w=['cosT'])
        P.dma('sp', sinT[:], K.ropes, w=['sinT'])
        P.dma('sp', lamt[:], K.da_lamb[:, i], w=['lamt'])
        P.dma('sp', gsub[:], K.da_g[i], w=['gsub'])
        P.op('dve', lambda e: e.memset(eps5[:], 1e-5), w=['eps5'])
        P.op('dve', lambda e: e.tensor_tensor(out=pr[:, 0:64], in0=lamt[:, 0:64], in1=lamt[:, 64:128], op=ALU.mult),
             r=['lamt'], w=['pr'])
        P.op('dve', lambda e: e.tensor_tensor(out=pr[:, 64:128], in0=lamt[:, 128:192], in1=lamt[:, 192:256], op=ALU.mult),
             r=['lamt'], w=['pr'])
        P.op('dve', lambda e: e.tensor_reduce(out=sv[:], in_=pr[:].rearrange("p (a b) -> p a b", a=2), axis=AX.X,
                                              op=ALU.add), r=['pr'], w=['sv'])
        P.op('act', lambda e: e.activation(out=ev[:], in_=sv[:], func=AF.Exp), r=['sv'], w=['ev'])
        P.op('dve', lambda e: e.tensor_tensor(out=nlam[:], in0=ev[:, 1:2], in1=ev[:, 0:1], op=ALU.subtract),
             r=['ev'], w=['nlam'])
        P.op('dve', lambda e: e.tensor_scalar(out=nlam[:], in0=nlam[:], scalar1=-lam_init, scalar2=None, op0=ALU.add),
             r=['nlam'], w=['nlam'])
        P.op('dve', lambda e: e.tensor_scalar(out=gsub[:], in0=gsub[:], scalar1=1.0 - lam_init, scalar2=None,
                                              op0=ALU.mult), r=['gsub'], w=['gsub'])
        n = 0
        for (dst, off, key) in [(qb, 0, 'qb'), (kb, 512, 'kb')]:
            for h in range(4):
                s_ = n % 2
                n += 1
                r0 = QR + off + h * 128
                P.dma('sp', xf[s_][:], K.pfm[r0:r0 + 128, :], r=['pfm'], w=['xf%d' % s_])
                for j in range(8):
                    P.dma('sp', xsw[s_][16 * j:16 * j + 16, :], K.pfm[r0 + 16 * (j ^ 1):r0 + 16 * (j ^ 1) + 16, NC_:NT],
                          r=['pfm'], w=['xsw%d' % s_])
                P.op('act', lambda e: e.activation(out=dst[:, h, 0:NC_], in_=xf[s_][:, 0:NC_], func=AF.Copy),
                     r=['xf%d' % s_], w=[key])
                P.op('dve', lambda e: e.tensor_tensor(out=t1[:], in0=xf[s_][:, NC_:NT], in1=cosT[:], op=ALU.mult),
                     r=['xf%d' % s_, 'cosT'], w=['t1'])
                P.op('pool', lambda e: e.tensor_tensor(out=t2[:], in0=xsw[s_][:], in1=sinT[:], op=ALU.mult),
                     r=['xsw%d' % s_, 'sinT'], w=['t2'])
                P.op('dve', lambda e: e.tensor_tensor(out=dst[:, h, NC_:NT], in0=t1[:], in1=t2[:], op=ALU.add),
                     r=['t1', 't2'], w=[key])
        for h in range(4):
            s_ = n % 2
            n += 1
            r0 = QR + 1024 + h * 128
            P.dma('sp', xf[s_][:], K.pfm[r0:r0 + 128, :], r=['pfm'], w=['xf%d' % s_])
            for c0 in range(0, 18, 4):
                cn = min(4, 18 - c0)
                pb = 4 + (c0 // 4) % 3
                for cc in range(cn):
                    c = c0 + cc
                    P.op('pe', lambda e: e.transpose(K.psum[pb][:, cc * 128:(cc + 1) * 128],
                                                     xf[s_][:, c * 128:(c + 1) * 128], K.ident[:]),
                         r=['xf%d' % s_], w=['ps%d' % pb])
                P.op('act', lambda e: e.activation(
                    out=vtm[:, c0:c0 + cn, h * 128:(h + 1) * 128],
                    in_=K.psum[pb][:, 0:cn * 128].rearrange("p (c d) -> p c d", c=cn), func=AF.Copy),
                    r=['ps%d' % pb], w=['vtm'])
        for h in range(4):
            os_ = h % 2
            qtiles = [(0, 256, [0, 1])] + [(256 + 512 * j, 512, list(range(18))) for j in range(4)]
            for (q0, qn, kch) in qtiles:
                items = [(m_, ci, c) for m_ in range(2) for ci, c in enumerate(kch)]

                def emit_qk(m_, ci, c):
                    nonlocal cnt
                    pbs = m_ * 64
                    sbk = 4 + cnt % 3
                    eb = cnt % 3
                    cnt += 1
                    mm(P, K.psum[sbk][:, :qn], kb[pbs:pbs + 64, h, c * 128:(c + 1) * 128],
                       qb[pbs:pbs + 64, h, q0:q0 + qn], True, True, r=['kb', 'qb'], w=['ps%d' % sbk])
                    P.op('act', lambda e: e.activation(out=E[eb][:, :qn], in_=K.psum[sbk][:, :qn], func=AF.Exp,
                                                       scale=0.125), r=['ps%d' % sbk], w=['E%d' % eb])
                    return eb

                def emit_pv(m_, ci, c, eb):
                    mm(P, K.psum[m_][:, :qn], vtm[:, c, h * 128:(h + 1) * 128], E[eb][:, :qn], ci == 0,
                       ci == len(kch) - 1, r=['vtm', 'E%d' % eb], w=['ps%d' % m_])
                    mm(P, K.psum[2 + m_][:, :qn], K.ones[:], E[eb][:, :qn], ci == 0, ci == len(kch) - 1,
                       r=['E%d' % eb], w=['ps%d' % (2 + m_)])
                prev = None
                for it in items:
                    eb = emit_qk(*it)
                    if prev is not None:
                        emit_pv(*prev)
                    prev = it + (eb,)
                emit_pv(*prev)
                for m_ in range(2):
                    P.op('dve', lambda e: e.reciprocal(rz[m_][:, :qn], K.psum[2 + m_][:, :qn]), r=['ps%d' % (2 + m_)],
                         w=['rz%d' % m_])
                    P.op('dve', lambda e: e.tensor_tensor(out=tt[m_][:, :qn], in0=K.psum[m_][:, :qn], in1=rz[m_][:, :qn],
                                                          op=ALU.mult), r=['ps%d' % m_, 'rz%d' % m_], w=['tt%d' % m_])
                P.op('dve', lambda e: e.scalar_tensor_tensor(out=osb[:, :qn], in0=tt[1][:, :qn], scalar=nlam[:, 0:1],
                                                             in1=tt[0][:, :qn], op0=ALU.mult, op1=ALU.add),
                     r=['tt0', 'tt1', 'nlam'], w=['osb'])
                P.op('act', lambda e: e.activation(out=sqb[:, :qn], in_=osb[:, :qn], func=AF.Square), r=['osb'],
                     w=['sqb'])
                mm(P, K.psum[7][:, :qn], K.ones[:], sqb[:, :qn], True, True, r=['sqb'], w=['ps7'])
                P.op('act', lambda e: e.activation(out=rs[:, :qn], in_=K.psum[7][:, :qn], func=AF.Sqrt, scale=1.0 / 128,
                                                   bias=eps5[:, 0:1]), r=['ps7', 'eps5'], w=['rs'])
                P.op('dve', lambda e: e.reciprocal(rs[:, :qn], rs[:, :qn]), r=['rs'], w=['rs'])
                P.op('dve', lambda e: e.scalar_tensor_tensor(out=ostg[os_][:, q0:q0 + qn], in0=osb[:, :qn],
                                                             scalar=gsub[:, 0:1], in1=rs[:, :qn], op0=ALU.mult,
                                                             op1=ALU.mult), r=['osb', 'rs', 'gsub'], w=['ostg%d' % os_])
            P.dma('sp', K.mixd[l][512 + h * 128:512 + (h + 1) * 128, :], ostg[os_][:], r=['ostg%d' % os_], w=['mix'])


TS5 = [(0, 256), (256, 512), (768, 512), (1280, 512), (1792, 512)]
TWO_PI = 2.0 * math.pi


def wrap_sin(P, sb_t, dst, src, shift, key, n, np_=128):
    tmp, msk = sb_t
    tmp = tmp[:np_]
    msk = msk[:np_]
    P.op('dve', lambda e: e.tensor_scalar(out=tmp[:, :n], in0=src, scalar1=shift, scalar2=None, op0=ALU.add),
         r=[key], w=[key + 't'])
    for (cmp, thr, add) in [(ALU.is_gt, math.pi, -TWO_PI), (ALU.is_lt, -math.pi, TWO_PI)] * 2:
        P.op('dve', lambda e: e.tensor_scalar(out=msk[:, :n], in0=tmp[:, :n], scalar1=thr, scalar2=None, op0=cmp),
             r=[key + 't'], w=[key + 'm'])
        P.op('dve', lambda e: e.scalar_tensor_tensor(out=tmp[:, :n], in0=msk[:, :n], scalar=add, in1=tmp[:, :n],
                                                     op0=ALU.mult, op1=ALU.add), r=[key + 'm', key + 't'],
             w=[key + 't'])
    P.op('act', lambda e: e.activation(out=dst, in_=tmp[:, :n], func=AF.Sin), r=[key + 't'], w=[key + 'o'])


def reduce_angle(P, tiles, ang, n, key, np_=128):
    kf, ki = tiles
    kf = kf[:np_]
    ki = ki[:np_]
    ang = ang[:np_]
    P.op('dve', lambda e: e.tensor_scalar(out=ki[:, :n], in0=ang[:, :n], scalar1=1.0 / TWO_PI, scalar2=None,
                                          op0=ALU.mult), r=[key], w=[key + 'ki'])
    P.op('dve', lambda e: e.tensor_copy(kf[:, :n], ki[:, :n]), r=[key + 'ki'], w=[key + 'kf'])
    P.op('dve', lambda e: e.scalar_tensor_tensor(out=ang[:, :n], in0=kf[:, :n], scalar=-TWO_PI, in1=ang[:, :n],
                                                 op0=ALU.mult, op1=ALU.add), r=[key + 'kf', key], w=[key])


def mixer_s5(K, l):
    i = l // 2
    P = K.P
    I32 = mybir.dt.int32
    with ExitStack() as es:
        sb = K.mk(es)
        par = sb('par', [128, 32, 3])
        c_ = {nm: sb('c_' + nm, [128, 32]) for nm in
              ['dt', 'lre', 'th', 'a', 'rho', 'cth', 'sth', 'lbr', 'lbi', 'den', 'nr', 'ni', 'fr', 'fi', 'nfi', 't1',
               't2']}
        wt = (sb('wtmp', [128, 96]), sb('wmsk', [128, 96]))
        rt = (sb('rkf', [128, 96]), sb('rki', [128, 96], I32))
        iot = sb('iot', [128, 96])
        ang = sb('ang', [128, 96])
        stc = sb('stc', [128, 96])
        sts = sb('sts', [128, 96])
        dg = sb('dg', [128, 2, 4])
        uf = sb('uf', [128, NT])
        ub = sb('ub', [128, NT], BF16)
        cosF2 = [sb('cosF%d' % j, [128, NT], BF16) for j in range(2)]
        sinF2 = [sb('sinF%d' % j, [128, NT], BF16) for j in range(2)]
        tq = [sb('tq%d' % j, [128, NT]) for j in range(2)]
        zre2 = [sb('zre%d' % j, [128, NT]) for j in range(2)]
        zim2 = [sb('zim%d' % j, [128, NT]) for j in range(2)]
        wre = sb('wre', [128, NT], BF16)
        wim = sb('wim', [128, NT], BF16)
        prd = [sb('prd%d' % j, [128, NT], BF16) for j in range(4)]
        tm8 = [sb('tm%d' % j, [128, 512]) for j in range(8)]
        rhoT2 = [sb('rhoT%d' % j, [128, NL]) for j in range(2)]
        onesF = sb('onesF', [128, NL])
        ygb = sb('ygb', [128, 4, NT], BF16)
        Bst2 = [sb('Bst%d' % j, [128, 2, 128]) for j in range(2)]
        Bb2 = [sb('Bb%d' % j, [128, 2, 128], BF16) for j in range(2)]
        Cst2 = [sb('Cst%d' % j, [128, 2, 128]) for j in range(2)]
        Cb2 = [sb('Cb%d' % j, [128, 3, 128], BF16) for j in range(2)]
        ct = sb('ct', [128, 128])
        gst = sb('gst', [128, 4, 512])
        gwb = sb('gwb', [128, 4, 512], BF16)
        sig = sb('sig', [128, 512])
        ostg = [sb('ostg%d' % j, [128, NT], BF16) for j in range(1)] * 2
        P.dma('sp', par[:], K.s5_par[:, i], w=['par'])
        P.dma('sp', iot[:], K.iot, w=['iot'])
        P.dma('sp', dg[:], K.s5_dg[:, i], w=['dg'])
        P.dma('sp', gst[:], K.s5_glu_w[i].rearrange("(kc p) m -> p kc m", p=128), w=['gst'])
        P.op('act', lambda e: e.activation(out=gwb[:], in_=gst[:], func=AF.Copy), r=['gst'], w=['gwb'])
        P.op('pool', lambda e: e.memset(onesF[:], 1.0), w=['onesF'])
        C = c_
        lamre, lamim, logdt = par[:, :, 0], par[:, :, 1], par[:, :, 2]

        def tsc(out, in0, s1, op0, s2=None, op1=None, r=(), w=()):
            if op1 is None:
                P.op('dve', lambda e: e.tensor_scalar(out=out, in0=in0, scalar1=s1, scalar2=None, op0=op0), r=r, w=w)
            else:
                P.op('dve', lambda e: e.tensor_scalar(out=out, in0=in0, scalar1=s1, scalar2=s2, op0=op0, op1=op1),
                     r=r, w=w)

        def ttn(out, in0, in1, op, r=(), w=(), eng='dve'):
            P.op(eng, lambda e: e.tensor_tensor(out=out, in0=in0, in1=in1, op=op), r=r, w=w)
        kc = ['cs']
        P.op('act', lambda e: e.activation(out=C['dt'][:], in_=logdt, func=AF.Exp), r=['par'], w=kc)
        tsc(C['lre'][:], lamre, -1e-4, ALU.min, r=['par'], w=kc)
        ttn(C['th'][:], lamim, C['dt'][:], ALU.mult, r=kc + ['par'], w=kc)
        ttn(C['a'][:], C['lre'][:], C['dt'][:], ALU.mult, r=kc, w=kc)
        P.op('act', lambda e: e.activation(out=C['rho'][:], in_=C['a'][:], func=AF.Exp), r=kc, w=kc)
        P.op('dve', lambda e: e.tensor_copy(ang[:, :32], C['th'][:]), r=kc, w=['ang'])
        reduce_angle(P, rt, ang, 32, 'ang')
        wrap_sin(P, wt, C['sth'][:], ang[:, :32], 0.0, 'ang', 32)
        wrap_sin(P, wt, C['cth'][:], ang[:, :32], math.pi / 2, 'ang', 32)
        kc2 = ['cs', 'ango']
        ttn(C['lbr'][:], C['rho'][:], C['cth'][:], ALU.mult, r=kc2, w=kc)
        ttn(C['lbi'][:], C['rho'][:], C['sth'][:], ALU.mult, r=kc2, w=kc)
        ttn(C['den'][:], C['lre'][:], C['lre'][:], ALU.mult, r=kc, w=kc)
        ttn(C['t1'][:], lamim, lamim, ALU.mult, r=kc + ['par'], w=kc)
        ttn(C['den'][:], C['den'][:], C['t1'][:], ALU.add, r=kc, w=kc)
        P.op('dve', lambda e: e.reciprocal(C['den'][:], C['den'][:]), r=kc, w=kc)
        tsc(C['t1'][:], C['lbr'][:], -1.0, ALU.add, r=kc, w=kc)
        ttn(C['nr'][:], C['t1'][:], C['lre'][:], ALU.mult, r=kc, w=kc)
        ttn(C['t2'][:], C['lbi'][:], lamim, ALU.mult, r=kc + ['par'], w=kc)
        ttn(C['nr'][:], C['nr'][:], C['t2'][:], ALU.add, r=kc, w=kc)
        ttn(C['ni'][:], C['lbi'][:], C['lre'][:], ALU.mult, r=kc, w=kc)
        ttn(C['t2'][:], C['t1'][:], lamim, ALU.mult, r=kc + ['par'], w=kc)
        ttn(C['ni'][:], C['ni'][:], C['t2'][:], ALU.subtract, r=kc, w=kc)
        ttn(C['fr'][:], C['nr'][:], C['den'][:], ALU.mult, r=kc, w=kc)
        ttn(C['fi'][:], C['ni'][:], C['den'][:], ALU.mult, r=kc, w=kc)
        tsc(C['nfi'][:], C['fi'][:], -1.0, ALU.mult, r=kc, w=kc)

        ITS = [(cq, d, g4) for cq in range(4) for d in range(2) for g4 in range(4)]

        def tabv(T_, a, n):
            return T_[:, a:a + n]

        def gen(k_):
            cq, d, g4 = ITS[k_]
            gp = cq * 4 + g4
            idx = d * 16 + gp
            par = k_ % 2
            sp_ = str(par)
            cosF, sinF, zre, zim = cosF2[par], sinF2[par], zre2[par], zim2[par]
            Bst, Bb, Cst, Cb = Bst2[par], Bb2[par], Cst2[par], Cb2[par]
            rhoT = rhoT2[par]
            first = (d == 0 and g4 == 0)
            last = (d == 1 and g4 == 3)
            tm = tm8[0:4]
            P.dma('sp', Bst[:], K.s5_B[i, idx], w=['Bst' + sp_])
            P.op('act', lambda e: e.activation(out=Bb[:], in_=Bst[:], func=AF.Copy), r=['Bst' + sp_], w=['Bb' + sp_])
            P.dma('sp', Cst[:], K.s5_C[i, idx], w=['Cst' + sp_])
            fr, fi, nfi = C['fr'][:, idx:idx + 1], C['fi'][:, idx:idx + 1], C['nfi'][:, idx:idx + 1]
            tsc(ct[:], Cst[:, 1, :], fi, ALU.mult, r=['Cst' + sp_, 'cs'], w=['ct'])
            P.op('dve', lambda e: e.scalar_tensor_tensor(out=Cb[:, 0, :], in0=Cst[:, 0, :], scalar=fr, in1=ct[:],
                                                         op0=ALU.mult, op1=ALU.subtract),
                 r=['Cst' + sp_, 'ct', 'cs'], w=['Cb' + sp_])
            tsc(ct[:], Cst[:, 1, :], fr, ALU.mult, r=['Cst' + sp_, 'cs'], w=['ct'])
            P.op('dve', lambda e: e.scalar_tensor_tensor(out=Cb[:, 1, :], in0=Cst[:, 0, :], scalar=nfi, in1=ct[:],
                                                         op0=ALU.mult, op1=ALU.subtract),
                 r=['Cst' + sp_, 'ct', 'cs'], w=['Cb' + sp_])
            P.op('act', lambda e: e.activation(out=Cb[:, 2, :], in_=Cb[:, 0, :], func=AF.Copy, scale=-1.0),
                 r=['Cb' + sp_], w=['Cb' + sp_])
            tsc(ang[:], iot[:], C['th'][:, idx:idx + 1], ALU.mult, r=['iot', 'cs'], w=['ang'])
            reduce_angle(P, rt, ang, 96, 'ang')
            wrap_sin(P, wt, sts[:], ang[:], 0.0, 'ang', 96)
            wrap_sin(P, wt, stc[:], ang[:], math.pi / 2, 'ang', 96)
            bA = lambda t_: t_[:, 48:96].unsqueeze(2).to_broadcast([128, 48, 48])
            bB = lambda t_: t_[:, 0:48].unsqueeze(1).to_broadcast([128, 48, 48])
            v3 = lambda t_: t_[:].rearrange("p (a b) -> p a b", a=48)
            ttn(v3(tq[0]), bA(stc), bB(stc), ALU.mult, r=['ango'], w=['tq0'])
            ttn(v3(tq[1]), bA(sts), bB(sts), ALU.mult, r=['ango'], w=['tq1'], eng='pool')
            def tseg(T_):
                if d == 0:
                    return [(T_[:, 0:NT], slice(0, NT))]
                return [(T_[:, NC_ - 1::-1], slice(0, NC_)), (T_[:, NT - 1:NC_ - 1:-1], slice(NC_, NT))]
            for (o_, sl_) in tseg(cosF):
                ttn(o_, tq[0][:, sl_], tq[1][:, sl_], ALU.subtract, r=['tq0', 'tq1'], w=['cosF' + sp_], eng='pool')
            ttn(v3(tq[0]), bA(sts), bB(stc), ALU.mult, r=['ango', 'cosF' + sp_], w=['tq0'])
            ttn(v3(tq[1]), bA(stc), bB(sts), ALU.mult, r=['ango', 'cosF' + sp_], w=['tq1'], eng='pool')
            for (o_, sl_) in tseg(sinF):
                ttn(o_, tq[0][:, sl_], tq[1][:, sl_], ALU.add, r=['tq0', 'tq1'], w=['sinF' + sp_], eng='pool')
            P.op('act', lambda e: e.activation(out=rhoT[:], in_=onesF[:], func=AF.Identity,
                                               scale=C['rho'][:, idx:idx + 1]), r=['onesF', 'cs'], w=['rhoT' + sp_])

            def tabv(T_, a, n):
                return T_[:, a:a + n]

        def use1(k_):
            cq, d, g4 = ITS[k_]
            gp = cq * 4 + g4
            idx = d * 16 + gp
            par = k_ % 2
            sp_ = str(par)
            cosF, sinF, zre, zim = cosF2[par], sinF2[par], zre2[par], zim2[par]
            Bst, Bb, Cst, Cb = Bst2[par], Bb2[par], Cst2[par], Cb2[par]
            rhoT = rhoT2[par]
            first = (d == 0 and g4 == 0)
            last = (d == 1 and g4 == 3)
            tm = tm8[0:4]
            for ti, (a, n) in enumerate(TS5):
                tm = tm8[4 * (ti % 2):4 * (ti % 2) + 4]
                tk = 4 * (ti % 2)
                mm(P, K.psum[5][:, :n], Bb[:, 0, :], ub[:, a:a + n], True, True, r=['Bb' + sp_, 'ub'], w=['ps5'])
                mm(P, K.psum[6][:, :n], Bb[:, 1, :], ub[:, a:a + n], True, True, r=['Bb' + sp_, 'ub'], w=['ps6'])
                cv, sv_ = tabv(cosF, a, n), tabv(sinF, a, n)
                ttn(tm[0][:, :n], K.psum[5][:, :n], cv, ALU.mult, r=['ps5', 'cosF' + sp_], w=['tm%d' % (tk + 0)])
                ttn(tm[1][:, :n], K.psum[6][:, :n], sv_, ALU.mult, r=['ps6', 'sinF' + sp_], w=['tm%d' % (tk + 1)])
                ttn(zre[:, a:a + n], tm[0][:, :n], tm[1][:, :n], ALU.add, r=['tm%d' % (tk + 0), 'tm%d' % (tk + 1)], w=['zre' + sp_], eng='pool')
                ttn(tm[2][:, :n], K.psum[6][:, :n], cv, ALU.mult, r=['ps6', 'cosF' + sp_], w=['tm%d' % (tk + 2)])
                ttn(tm[3][:, :n], K.psum[5][:, :n], sv_, ALU.mult, r=['ps5', 'sinF' + sp_], w=['tm%d' % (tk + 3)])
                ttn(zim[:, a:a + n], tm[2][:, :n], tm[3][:, :n], ALU.subtract, r=['tm%d' % (tk + 2), 'tm%d' % (tk + 3)], w=['zim' + sp_],
                    eng='pool')

        def use2(k_):
            cq, d, g4 = ITS[k_]
            gp = cq * 4 + g4
            idx = d * 16 + gp
            par = k_ % 2
            sp_ = str(par)
            cosF, sinF, zre, zim = cosF2[par], sinF2[par], zre2[par], zim2[par]
            Bst, Bb, Cst, Cb = Bst2[par], Bb2[par], Cst2[par], Cb2[par]
            rhoT = rhoT2[par]
            first = (d == 0 and g4 == 0)
            last = (d == 1 and g4 == 3)
            tm = tm8[0:4]
            for (zz, ww, kz, kw) in [(zre, wre, 'zre' + sp_, 'wre'), (zim, wim, 'zim' + sp_, 'wim')]:
                if d == 0:
                    segs = [(zz[:, 0:NC_], ww[:, 0:NC_], rhoT[:, 0:NC_], 0.0),
                            (zz[:, NC_:NT], ww[:, NC_:NT], rhoT[:, 0:NL], ww[:, NC_ - 1:NC_])]
                else:
                    segs = [(zz[:, NC_ - 1::-1], ww[:, NC_ - 1::-1], rhoT[:, 0:NC_], 0.0),
                            (zz[:, NT - 1:NC_ - 1:-1], ww[:, NT - 1:NC_ - 1:-1], rhoT[:, 0:NL], ww[:, 0:1])]
                for (zi, wo, rh, init) in segs:
                    P.op('dve', lambda e: e.tensor_tensor_scan(wo, rh, zi, init, ALU.mult, ALU.add),
                         r=[kz, 'rhoT' + sp_, kw], w=[kw])
            ttn(prd[0][:], wre[:], cosF[:], ALU.mult, r=['wre', 'cosF' + sp_], w=['prd0'])
            ttn(prd[1][:], wim[:], sinF[:], ALU.mult, r=['wim', 'sinF' + sp_], w=['prd1'], eng='pool')
            ttn(prd[2][:], wre[:], sinF[:], ALU.mult, r=['wre', 'sinF' + sp_], w=['prd2'])
            ttn(prd[3][:], wim[:], cosF[:], ALU.mult, r=['wim', 'cosF' + sp_], w=['prd3'], eng='pool')
            for ti, (a, n) in enumerate(TS5):
                for j_, cs_i in enumerate([0, 2, 1, 1]):
                    mm(P, K.psum[ti][:, :n], Cb[:, cs_i, :], prd[j_][:, a:a + n], first and j_ == 0,
                       last and j_ == 3, r=['Cb' + sp_, 'prd%d' % j_], w=['ps%d' % ti])

        def epilogue(cq):
            tm = tm8[0:4]
            for ti, (a, n) in enumerate(TS5):
                P.op('dve', lambda e: e.scalar_tensor_tensor(out=tm[0][:, :n], in0=uf[:, a:a + n], scalar=dg[:, 0, cq:cq + 1],
                                                             in1=K.psum[ti][:, :n], op0=ALU.mult, op1=ALU.add),
                     r=['uf', 'dg', 'ps%d' % ti], w=['tm0'])
                P.op('act', lambda e: e.activation(out=ygb[:, cq, a:a + n], in_=tm[0][:, :n], func=AF.Gelu),
                     r=['tm0'], w=['ygb'])

        gen(0)
        for k_ in range(32):
            cq, d, g4 = ITS[k_]
            if d == 0 and g4 == 0:
                P.dma('sp', uf[:], K.pfm[cq * 128:(cq + 1) * 128, :], r=['pfm'], w=['uf'])
                P.op('act', lambda e: e.activation(out=ub[:], in_=uf[:], func=AF.Copy), r=['uf'], w=['ub'])
            use1(k_)
            if k_ + 1 < 32:
                gen(k_ + 1)
            use2(k_)
            if d == 1 and g4 == 3:
                epilogue(cq)
        cnt = 0
        for mc in range(4):
            os_ = 0
            for (a, n) in TILES:
                pb = 5 + cnt % 3
                cnt += 1
                for k in range(4):
                    mm(P, K.psum[pb][:, :n], gwb[:, k, mc * 128:(mc + 1) * 128], ygb[:, k, a:a + n], k == 0, k == 3,
                       r=['gwb', 'ygb'], w=['ps%d' % pb])
                P.op('act', lambda e: e.activation(out=sig[:, :n], in_=K.psum[pb][:, :n], func=AF.Sigmoid,
                                                   bias=dg[:, 1, mc:mc + 1], scale=1.0), r=['ps%d' % pb, 'dg'], w=['sig'])
                ttn(ostg[os_][:, a:a + n], ygb[:, mc, a:a + n], sig[:, :n], ALU.mult, r=['ygb', 'sig'],
                    w=['ostg%d' % os_])
            P.dma('sp', K.mixd[l][mc * 128:(mc + 1) * 128, :], ostg[os_][:], r=['ostg%d' % os_], w=['mix'])


def hyena_filters(K, i, n, Kr, Ki, Kn):
    P = K.P
    I32 = mybir.dt.int32
    nb = n // 128
    dc, ds = K.dft[n]
    with ExitStack() as es0:
      h2 = K.mk(es0)('h2', [64, n])
      with ExitStack() as es:
        sb = K.mk(es)
        zT = sb('zT', [33, n])
        w1 = sb('w1', [33, 64])
        w2 = sb('w2', [64, 64])
        cols = sb('cols', [64, 4])
        h1 = sb('h1', [64, n])
        arg = sb('arg', [128, 512])
        wt = (sb('wtmp', [128, 512]), sb('wmsk', [128, 512]))
        rt = (sb('rkf', [128, 512]), sb('rki', [128, 512], I32))
        P.dma('sp', zT[:], K.hy_zT[n], w=['zT'])
        P.dma('sp', w1[:], K.hy_w1[i], w=['w1'])
        P.dma('sp', w2[:], K.hy_w2[i], w=['w2'])
        P.dma('sp', cols[:], K.hy_cols[:, i], w=['cols'])
        tl = [(a, min(512, n - a)) for a in range(0, n, 512)]
        for (src, wmat, dst, bcol, fcol, kin, kout) in [(zT, w1, h1, 0, 2, 'zT', 'h1'), (h1, w2, h2, 1, 3, 'h1', 'h2')]:
            for (a, tn) in tl:
                pb = P.ps()
                mm(P, K.psum[pb][:64, :tn], wmat[:], src[:, a:a + tn], True, True, r=[kin, 'w1', 'w2'], w=['ps%d' % pb])
                P.op('dve', lambda e: e.tensor_scalar(out=arg[:64, :tn], in0=K.psum[pb][:64, :tn],
                                                      scalar1=cols[:, bcol:bcol + 1], scalar2=cols[:, fcol:fcol + 1],
                                                      op0=ALU.add, op1=ALU.mult), r=['ps%d' % pb, 'cols'], w=['ang'])
                reduce_angle(P, rt, arg, tn, 'ang', 64)
                wrap_sin(P, wt, dst[:, a:a + tn], arg[:64, :tn], 0.0, 'ang', tn, 64)
                P.op('dve', lambda e: e.tensor_copy(dst[:, a:a + tn], dst[:, a:a + tn]), r=['ango'], w=[kout])
        P.barrier()
      with ExitStack() as es:
        sb = K.mk(es)
        w3 = sb('w3', [64, 2048])
        rate = sb('rate', [128, 2048])
        win = sb('win', [128, 2048])
        ntc = sb('ntc', [128, nb])
        hw = sb('hw', [128, 2048])
        hs = sb('hs', [128, nb, 1024], BF16)
        hd = sb('hd', [128, nb, 1024], BF16)
        tc = [sb('tc%d' % j, [128, nb, 128], BF16) for j in range(1)] * 2
        ts = [sb('ts%d' % j, [128, nb, 128], BF16) for j in range(1)] * 2
        biasb = sb('biasb', [128, 1024])
        altc = sb('altc', [128, 1], BF16)
        kt = sb('kt', [128, 512])
        P.dma('sp', w3[:], K.hy_w3[i], w=['w3'])
        P.dma('sp', rate[:], K.hy_ld[:, i], w=['rate'])
        P.dma('sp', ntc[:], K.hy_tcol[n], w=['ntc'])
        P.dma('sp', biasb[:], K.hy_biasb[:, i], w=['biasb'])
        P.dma('sp', altc[:], K.altc, w=['altc'])
        P.op('act', lambda e: e.activation(out=rate[:], in_=rate[:], func=AF.Exp), r=['rate'], w=['rate'])
        P.op('dve', lambda e: e.tensor_scalar(out=ntc[:], in0=ntc[:], scalar1=-1.0, scalar2=None, op0=ALU.mult),
             r=['ntc'], w=['ntc'])
        for c in range(nb):
            P.op('act', lambda e: e.activation(out=win[:], in_=rate[:], func=AF.Exp, scale=ntc[:, c:c + 1]),
                 r=['rate', 'ntc'], w=['win'])
            for q in range(4):
                pb = P.ps()
                mm(P, K.psum[pb][:], h2[:, c * 128:(c + 1) * 128], w3[:, q * 512:(q + 1) * 512], True, True,
                   r=['h2', 'w3'], w=['ps%d' % pb])
                P.op('dve', lambda e: e.tensor_tensor(out=hw[:, q * 512:(q + 1) * 512], in0=K.psum[pb][:],
                                                      in1=win[:, q * 512:(q + 1) * 512], op=ALU.mult),
                     r=['ps%d' % pb, 'win'], w=['hw'])
            if c == 0:
                P.op('dve', lambda e: e.memset(hw[0:1, 1024:2048], 0.0), r=['hw'], w=['hw'])
            P.op('dve', lambda e: e.tensor_tensor(out=hs[:, c, :], in0=hw[:, 0:1024], in1=hw[:, 1024:2048], op=ALU.add),
                 r=['hw'], w=['hs'])
            P.op('pool', lambda e: e.tensor_tensor(out=hd[:, c, :], in0=hw[:, 1024:2048], in1=hw[:, 0:1024],
                                                   op=ALU.subtract), r=['hw'], w=['hd'])
        sc = 1.0 / n
        for fc in range(nb):
            s_ = 0
            P.dma('sp', tc[s_][:], dc[fc], w=['tc%d' % s_])
            P.dma('sp', ts[s_][:], ds[fc], w=['ts%d' % s_])
            for q in range(2):
                pb = P.ps()
                for jc in range(nb):
                    mm(P, K.psum[pb][:], tc[s_][:, jc, :], hs[:, jc, q * 512:(q + 1) * 512], jc == 0, jc == nb - 1,
                       r=['tc%d' % s_, 'hs'], w=['ps%d' % pb])
                P.op('dve', lambda e: e.tensor_tensor(out=kt[:], in0=K.psum[pb][:], in1=biasb[:, q * 512:(q + 1) * 512],
                                                      op=ALU.add), r=['ps%d' % pb, 'biasb'], w=['kt'])
                if fc == 0:
                    P.op('dve', lambda e: e.tensor_scalar(out=kt[0:1, :], in0=kt[0:1, :], scalar1=0.5, scalar2=None,
                                                          op0=ALU.mult), r=['kt'], w=['kt'])
                P.op('act', lambda e: e.activation(out=Kr[:, fc, q * 512:(q + 1) * 512], in_=kt[:], func=AF.Copy,
                                                   scale=sc), r=['kt'], w=['Kr'])
                pb = P.ps()
                for jc in range(nb):
                    mm(P, K.psum[pb][:], ts[s_][:, jc, :], hd[:, jc, q * 512:(q + 1) * 512], jc == 0, jc == nb - 1,
                       r=['ts%d' % s_, 'hd'], w=['ps%d' % pb])
                P.op('act', lambda e: e.activation(out=Ki[:, fc, q * 512:(q + 1) * 512], in_=K.psum[pb][:], func=AF.Copy,
                                                   scale=sc), r=['ps%d' % pb], w=['Ki'])
        for q in range(2):
            pb = P.ps()
            for jc in range(nb):
                mm(P, K.psum[pb][0:1, :], altc[:, 0:1], hs[:, jc, q * 512:(q + 1) * 512], jc == 0, jc == nb - 1,
                   r=['altc', 'hs'], w=['ps%d' % pb])
            P.op('dve', lambda e: e.tensor_tensor(out=kt[0:1, :], in0=K.psum[pb][0:1, :],
                                                  in1=biasb[0:1, q * 512:(q + 1) * 512], op=ALU.add),
                 r=['ps%d' % pb, 'biasb'], w=['kt'])
            P.op('act', lambda e: e.activation(out=Kn[0:1, q * 512:(q + 1) * 512], in_=kt[0:1, :], func=AF.Copy,
                                               scale=0.5 * sc), r=['kt'], w=['Kn'])
        P.barrier()


def hyena_conv(K, n, stm, Kr, Ki, Kn, emit_out):
    P = K.P
    nb = n // 128
    dc, ds = K.dft[n]
    with ExitStack() as es:
        sb = K.mk(es)
        Yr = sb('Yr', [128, nb, 512], BF16)
        Yi = sb('Yi', [128, nb, 512], BF16)
        Yn = sb('Yn', [1, 512], BF16)
        z1 = stm[0]
        tc = [sb('tc%d' % j, [128, nb, 128], BF16) for j in range(1)] * 2
        ts = [sb('ts%d' % j, [128, nb, 128], BF16) for j in range(1)] * 2
        tm = [sb('tm%d' % j, [128, 512]) for j in range(4)]
        altc = sb('altc', [128, 1], BF16)
        altr = sb('altr', [1, 128], BF16)
        zo = sb('zo', [128, 512], BF16)
        P.dma('sp', altc[:], K.altc, w=['altc'])
        P.dma('sp', altr[:], K.altr, w=['altr'])
        for o in range(2):
            u = stm[0] if o == 0 else z1
            ku = 'stm0'
            for fc in range(nb):
                s_ = 0
                P.dma('sp', tc[s_][:], dc[fc], w=['tc%d' % s_])
                P.dma('sp', ts[s_][:], ds[fc], w=['ts%d' % s_])
                pa = P.ps()
                for jc in range(nb):
                    mm(P, K.psum[pa][:], tc[s_][:, jc, :], u[:, jc, :], jc == 0, jc == nb - 1, r=['tc%d' % s_, ku],
                       w=['ps%d' % pa])
                pbb = P.ps()
                for jc in range(nb):
                    mm(P, K.psum[pbb][:], ts[s_][:, jc, :], u[:, jc, :], jc == 0, jc == nb - 1, r=['ts%d' % s_, ku],
                       w=['ps%d' % pbb])
                kr = Kr[:, fc, o * 512:(o + 1) * 512]
                ki = Ki[:, fc, o * 512:(o + 1) * 512]
                xr, xs_ = K.psum[pa][:], K.psum[pbb][:]
                rk = ['ps%d' % pa, 'ps%d' % pbb, 'Kr', 'Ki']
                P.op('dve', lambda e: e.tensor_tensor(out=tm[0][:], in0=xr, in1=kr, op=ALU.mult), r=rk, w=['tm0'])
                P.op('dve', lambda e: e.tensor_tensor(out=tm[1][:], in0=xs_, in1=ki, op=ALU.mult), r=rk, w=['tm1'])
                P.op('pool', lambda e: e.tensor_tensor(out=Yr[:, fc, :], in0=tm[0][:], in1=tm[1][:], op=ALU.add),
                     r=['tm0', 'tm1'], w=['Yr'])
                P.op('dve', lambda e: e.tensor_tensor(out=tm[2][:], in0=xs_, in1=kr, op=ALU.mult), r=rk, w=['tm2'])
                P.op('dve', lambda e: e.tensor_tensor(out=tm[3][:], in0=xr, in1=ki, op=ALU.mult), r=rk, w=['tm3'])
                P.op('pool', lambda e: e.tensor_tensor(out=Yi[:, fc, :], in0=tm[2][:], in1=tm[3][:], op=ALU.subtract),
                     r=['tm2', 'tm3'], w=['Yi'])
            pa = P.ps()
            for jc in range(nb):
                mm(P, K.psum[pa][0:1, :], altc[:, 0:1], u[:, jc, :], jc == 0, jc == nb - 1, r=['altc', ku], w=['ps%d' % pa])
            P.op('dve', lambda e: e.tensor_tensor(out=Yn[0:1, :], in0=K.psum[pa][0:1, :],
                                                  in1=Kn[0:1, o * 512:(o + 1) * 512], op=ALU.mult),
                 r=['ps%d' % pa, 'Kn'], w=['Yn'])
            for tci in range(nb):
                s_ = 0
                P.dma('sp', tc[s_][:], dc[tci], w=['tc%d' % s_])
                P.dma('sp', ts[s_][:], ds[tci], w=['ts%d' % s_])
                pa = P.ps()
                for fc in range(nb):
                    mm(P, K.psum[pa][:], tc[s_][:, fc, :], Yr[:, fc, :], fc == 0, False, r=['tc%d' % s_, 'Yr'],
                       w=['ps%d' % pa])
                    mm(P, K.psum[pa][:], ts[s_][:, fc, :], Yi[:, fc, :], False, False, r=['ts%d' % s_, 'Yi'],
                       w=['ps%d' % pa])
                mm(P, K.psum[pa][:], altr[0:1, :], Yn[0:1, :], False, True, r=['altr', 'Yn'], w=['ps%d' % pa])
                if o == 0:
                    P.op('dve', lambda e: e.tensor_tensor(out=z1[:, tci, :], in0=K.psum[pa][:], in1=stm[1][:, tci, :],
                                                          op=ALU.mult), r=['ps%d' % pa, 'stm1'], w=['stm0'])
                else:
                    P.op('dve', lambda e: e.tensor_tensor(out=zo[:], in0=K.psum[pa][:], in1=stm[2][:, tci, :],
                                                          op=ALU.mult), r=['ps%d' % pa, 'stm2'], w=['zo'])
                    emit_out(tci, zo)
        P.barrier()


def mixer_hyena(K, l):
    i = l // 2
    P = K.P
    with ExitStack() as es:
        sb = K.mk(es)
        Kr = {n: sb('Kr%d' % n, [128, n // 128, 1024], BF16) for n in (2048, 256)}
        Ki = {n: sb('Ki%d' % n, [128, n // 128, 1024], BF16) for n in (2048, 256)}
        Kn = {n: sb('Kn%d' % n, [1, 1024], BF16) for n in (2048, 256)}
        for n in (256, 2048):
            hyena_filters(K, i, n, Kr[n], Ki[n], Kn[n])
        stm = {2048: [sb('stl%d' % j, [128, 16, 512], BF16) for j in range(3)],
               256: [sb('stc%d' % j, [128, 2, 512], BF16) for j in range(3)]}
        identb = sb('identb', [128, 128], BF16)
        ostg = sb('ostg', [128, 4, NT], BF16)
        P.dma('sp', identb[:], K.identb, w=['identb'])
        with ExitStack() as es2:
            sb2 = K.mk(es2)
            xf = [sb2('xf%d' % j, [128, NT]) for j in range(2)]
            yf = sb2('yf', [128, NT])
            yb = sb2('yb', [128, NT], BF16)
            cw = sb2('cw', [128, 12, 4])
            P.dma('sp', cw[:], K.hy_cw[:, i], w=['cw'])
            for cc in range(12):
                s_ = cc % 2
                st_i, cs_ = cc // 4, cc % 4
                P.dma('sp', xf[s_][:], K.pfm[512 + cc * 128:512 + (cc + 1) * 128, :], r=['pfm'], w=['xf%d' % s_])
                kx = 'xf%d' % s_
                for (a, n_) in [(0, NC_), (NC_, NL)]:
                    P.op('act', lambda e: e.activation(out=yf[:, a:a + n_], in_=xf[s_][:, a:a + n_], func=AF.Identity,
                                                       scale=cw[:, cc, 1:2], bias=cw[:, cc, 3:4]), r=[kx, 'cw'], w=['yf'])
                    P.op('dve', lambda e: e.scalar_tensor_tensor(out=yf[:, a + 1:a + n_], in0=xf[s_][:, a:a + n_ - 1],
                                                                 scalar=cw[:, cc, 0:1], in1=yf[:, a + 1:a + n_],
                                                                 op0=ALU.mult, op1=ALU.add), r=[kx, 'cw', 'yf'], w=['yf'])
                    P.op('dve', lambda e: e.scalar_tensor_tensor(out=yb[:, a:a + n_ - 1], in0=xf[s_][:, a + 1:a + n_],
                                                                 scalar=cw[:, cc, 2:3], in1=yf[:, a:a + n_ - 1],
                                                                 op0=ALU.mult, op1=ALU.add), r=[kx, 'cw', 'yf'], w=['yb'])
                    P.op('act', lambda e: e.activation(out=yb[:, a + n_ - 1:a + n_], in_=yf[:, a + n_ - 1:a + n_],
                                                       func=AF.Copy), r=['yf'], w=['yb'])
                for gi_, (c0, cn) in enumerate([(0, 2), (2, 4), (6, 4), (10, 4), (14, 4)]):
                    pb = P.ps()
                    pst = K.psum[pb][:].bitcast(BF16)
                    for q in range(cn):
                        c = c0 + q
                        P.op('pe', lambda e: e.transpose(pst[:, q * 128:(q + 1) * 128], yb[:, c * 128:(c + 1) * 128],
                                                         identb[:]), r=['yb', 'identb'], w=['ps%d' % pb])
                    if c0 == 0:
                        dst = stm[256][st_i][:, 0:2, cs_ * 128:(cs_ + 1) * 128]
                        kd = 'stc'
                    else:
                        dst = stm[2048][st_i][:, c0 - 2:c0 - 2 + cn, cs_ * 128:(cs_ + 1) * 128]
                        kd = 'stl'
                    src_ = pst[:, 0:cn * 128].rearrange("p (q t) -> p q t", q=cn)
                    if gi_ % 2:
                        P.op('act', lambda e: e.activation(out=dst, in_=src_, func=AF.Copy), r=['ps%d' % pb], w=[kd])
                    else:
                        P.op('dve', lambda e: e.tensor_copy(dst, src_), r=['ps%d' % pb], w=[kd])
            P.barrier()
        for n, coff in [(256, 0), (2048, 2)]:
            def emit_out(tci, zo, coff=coff):
                pb = P.ps()
                pst = K.psum[pb][:].bitcast(BF16)
                for q in range(4):
                    P.op('pe', lambda e: e.transpose(pst[:, q * 128:(q + 1) * 128], zo[:, q * 128:(q + 1) * 128],
                                                     identb[:]), r=['zo', 'identb'], w=['ps%d' % pb])
                t0 = (coff + tci) * 128
                P.op('act', lambda e: e.activation(out=ostg[:, :, t0:t0 + 128],
                                                   in_=pst[:, 0:512].rearrange("p (q t) -> p q t", q=4), func=AF.Copy),
                     r=['ps%d' % pb], w=['ostg'])
            hyena_conv(K, n, stm[n], Kr[n], Ki[n], Kn[n], emit_out)
        for cq in range(4):
            P.dma('sp', K.mixd[l][512 + cq * 128:512 + (cq + 1) * 128, :], ostg[:, cq, :], r=['ostg'], w=['mix'])


def conv3(P, y, x, cw3, kx, ky):
    for (a, n_) in [(0, NC_), (NC_, NL)]:
        P.op('act', lambda e: e.activation(out=y[:, a:a + n_], in_=x[:, a:a + n_], func=AF.Identity,
                                           scale=cw3[:, 1:2]), r=[kx], w=[ky])
        P.op('dve', lambda e: e.scalar_tensor_tensor(out=y[:, a + 1:a + n_], in0=x[:, a:a + n_ - 1], scalar=cw3[:, 0:1],
                                                     in1=y[:, a + 1:a + n_], op0=ALU.mult, op1=ALU.add),
             r=[kx, ky], w=[ky])
        P.op('dve', lambda e: e.scalar_tensor_tensor(out=y[:, a:a + n_ - 1], in0=x[:, a + 1:a + n_], scalar=cw3[:, 2:3],
                                                     in1=y[:, a:a + n_ - 1], op0=ALU.mult, op1=ALU.add),
             r=[kx, ky], w=[ky])


def rw_vis(T_, d):
    if d == 0:
        return [(T_[:, 0:NC_].rearrange("p (c t) -> p c t", t=64), 0),
                (T_[:, NC_:NT].rearrange("p (c t) -> p c t", t=64), 4)]
    return [(T_[:, NC_ - 1::-1].rearrange("p (c t) -> p c t", t=64), 0),
            (T_[:, NT - 1:NC_ - 1:-1].rearrange("p (c t) -> p c t", t=64), 4)]


def rwkv_pre(K, i, d):
    P = K.P
    with ExitStack() as es:
        sb = K.mk(es)
        cw = sb('cw', [128, 12, 3])
        w0a0 = sb('w0a0', [128, 4, 2])
        cols = sb('cols', [128, 4, 5])
        omka = sb('omka', [128, 4])
        wlo = sb('wlo', [64, NT])
        alo = sb('alo', [64, NT])
        wup = sb('wup', [64, 512])
        aup = sb('aup', [64, 512])
        blk = sb('blk', [128, 128], BF16)
        ones64 = sb('ones64', [128, 64])
        xin = sb('xin', [128, NT])
        bufs = {nm: sb('b_' + nm, [128, NT]) for nm in ['r', 'k', 'v', 'lw', 'lam', 'gi', 'gv', 'ge', 'a', 'kk', 'ke', 'b']}
        sqb = sb('sqb', [128, 512], BF16)
        rs = sb('rs', [128, 512])
        QRs = sb('QRs', [128, 36, 128], BF16)
        BKs = sb('BKs', [128, 36, 128], BF16)
        Vs = sb('Vs', [128, NT], BF16)
        gCt = sb('gCt', [128, 36])
        B = bufs
        P.dma('sp', cw[:], K.rw_cw[:, i], w=['cw'])
        P.dma('sp', w0a0[:], K.rw_w0a0[:, i, d], w=['w0a0'])
        P.dma('sp', cols[:], K.rw_cols[:, i], w=['cols'])
        P.dma('sp', wlo[:], K.pfm[1536:1600, :], r=['pfm'], w=['wlo'])
        P.dma('sp', alo[:], K.pfm[1600:1664, :], r=['pfm'], w=['alo'])
        P.dma('sp', wup[:], K.rw_wup[i, d], w=['wup'])
        P.dma('sp', aup[:], K.rw_aup[i, d], w=['aup'])
        P.dma('sp', blk[:], K.rw_blk, w=['blk'])
        P.op('dve', lambda e: e.memset(ones64[:], 1.0), w=['ones64'])
        P.op('act', lambda e: e.activation(out=wlo[:], in_=wlo[:], func=AF.Tanh), r=['wlo'], w=['wlo'])
        P.op('dve', lambda e: e.tensor_scalar(out=omka[:], in0=cols[:, :, 1], scalar1=-1.0, scalar2=1.0, op0=ALU.mult,
                                              op1=ALU.add), r=['cols'], w=['omka'])

        def tt(out, in0, in1, op, r, w, eng='dve'):
            P.op(eng, lambda e: e.tensor_tensor(out=out, in0=in0, in1=in1, op=op), r=r, w=w)
        for cq in range(4):
            for j, nm in enumerate(['r', 'k', 'v']):
                P.dma('sp', xin[:], K.pfm[j * 512 + cq * 128:j * 512 + (cq + 1) * 128, :], r=['pfm'], w=['xin'])
                conv3(P, B[nm], xin, cw[:, j * 4 + cq, :], 'xin', nm)
            if d == 0:
                P.dma('sp', K.rw_rv[0, cq * 128:(cq + 1) * 128, :], B['r'][:], r=['r'], w=['rw_rv'])
                P.dma('sp', K.rw_rv[1, cq * 128:(cq + 1) * 128, :], B['v'][:], r=['v'], w=['rw_rv'])
            for (a, n) in TILES:
                pb = P.ps()
                mm(P, K.psum[pb][:, :n], wup[:, cq * 128:(cq + 1) * 128], wlo[:, a:a + n], True, True, r=['wup', 'wlo'],
                   w=['ps%d' % pb])
                P.op('act', lambda e: e.activation(out=B['lw'][:, a:a + n], in_=K.psum[pb][:, :n], func=AF.Sigmoid,
                                                   bias=w0a0[:, cq, 0:1], scale=1.0), r=['ps%d' % pb, 'w0a0'], w=['lw'])
                pb = P.ps()
                mm(P, K.psum[pb][:, :n], aup[:, cq * 128:(cq + 1) * 128], alo[:, a:a + n], True, True, r=['aup', 'alo'],
                   w=['ps%d' % pb])
                P.op('act', lambda e: e.activation(out=B['a'][:, a:a + n], in_=K.psum[pb][:, :n], func=AF.Sigmoid,
                                                   bias=w0a0[:, cq, 1:2], scale=1.0), r=['ps%d' % pb, 'w0a0'], w=['a'])
            P.op('dve', lambda e: e.tensor_scalar(out=B['lw'][:], in0=B['lw'][:], scalar1=-math.exp(-0.5), scalar2=None,
                                                  op0=ALU.mult), r=['lw'], w=['lw'])
            for c in range(36):
                sl = slice(c * 64, (c + 1) * 64)
                if d == 0:
                    o_, i_ = B['lam'][:, sl], B['lw'][:, sl]
                else:
                    lo = c * 64
                    hi = c * 64 + 63
                    o_ = B['lam'][:, hi::-1] if lo == 0 else B['lam'][:, hi:lo - 1:-1]
                    i_ = B['lw'][:, hi::-1] if lo == 0 else B['lw'][:, hi:lo - 1:-1]
                P.op('dve', lambda e: e.tensor_tensor_scan(o_, ones64[:], i_, 0.0, ALU.mult, ALU.add),
                     r=['lw', 'ones64'], w=['lam'])
            P.op('act', lambda e: e.activation(out=B['gi'][:], in_=B['lam'][:], func=AF.Exp), r=['lam'], w=['gi'])
            P.op('act', lambda e: e.activation(out=B['gv'][:], in_=B['lam'][:], func=AF.Exp, scale=-1.0), r=['lam'],
                 w=['gv'])
            tt(B['ge'][:], B['lam'][:], B['lw'][:], ALU.subtract, ['lam', 'lw'], ['ge'])
            P.op('act', lambda e: e.activation(out=B['ge'][:], in_=B['ge'][:], func=AF.Exp), r=['ge'], w=['ge'])
            gsrc = B['gi'][:, 63::64] if d == 0 else B['gi'][:, 0::64]
            P.op('dve', lambda e: e.tensor_copy(gCt[:], gsrc), r=['gi'], w=['gCt'])
            P.dma('sp', K.rw_gC[cq * 128:(cq + 1) * 128, :], gCt[:], r=['gCt'], w=['rw_gC'])
            P.op('dve', lambda e: e.tensor_scalar(out=B['kk'][:], in0=B['k'][:], scalar1=cols[:, cq, 0:1], scalar2=None,
                                                  op0=ALU.mult), r=['k', 'cols'], w=['kk'])
            for (a, n) in TILES:
                P.op('act', lambda e: e.activation(out=sqb[:, :n], in_=B['kk'][:, a:a + n], func=AF.Square), r=['kk'],
                     w=['sqb'])
                pb = P.ps()
                mm(P, K.psum[pb][:, :n], blk[:], sqb[:, :n], True, True, r=['blk', 'sqb'], w=['ps%d' % pb])
                P.op('act', lambda e: e.activation(out=rs[:, :n], in_=K.psum[pb][:, :n], func=AF.Sqrt), r=['ps%d' % pb],
                     w=['rs'])
                P.op('dve', lambda e: e.tensor_scalar(out=rs[:, :n], in0=rs[:, :n], scalar1=1e-12, scalar2=None,
                                                      op0=ALU.max), r=['rs'], w=['rs'])
                P.op('dve', lambda e: e.reciprocal(rs[:, :n], rs[:, :n]), r=['rs'], w=['rs'])
                tt(B['kk'][:, a:a + n], B['kk'][:, a:a + n], rs[:, :n], ALU.mult, ['kk', 'rs'], ['kk'])
            P.op('dve', lambda e: e.tensor_scalar(out=B['ke'][:], in0=B['a'][:], scalar1=cols[:, cq, 1:2],
                                                  scalar2=omka[:, cq:cq + 1], op0=ALU.mult, op1=ALU.add),
                 r=['a', 'cols', 'omka'], w=['ke'])
            tt(B['ke'][:], B['ke'][:], B['k'][:], ALU.mult, ['ke', 'k'], ['ke'])
            tt(B['b'][:], B['kk'][:], B['a'][:], ALU.mult, ['kk', 'a'], ['b'], eng='pool')
            P.dma('sp', K.rw_keff[d, cq * 128:(cq + 1) * 128, :], B['ke'][:], r=['ke'], w=['rw_keff'])
            for (dst, half, x_, g_, kd, eng) in [(QRs, 0, 'kk', 'ge', 'QRs', 'dve'), (QRs, 1, 'r', 'gi', 'QRs', 'pool'),
                                                 (BKs, 0, 'b', 'gv', 'BKs', 'dve'), (BKs, 1, 'ke', 'gv', 'BKs', 'pool')]:
                for (xv, c0), (gv_, _) in zip(rw_vis(B[x_], d), rw_vis(B[g_], d)):
                    ncn = xv.shape[1]
                    tt(dst[:, c0:c0 + ncn, half * 64:(half + 1) * 64], xv, gv_, ALU.mult, [x_, g_], [kd], eng=eng)
            for (xv, c0) in rw_vis(B['v'], d):
                ncn = xv.shape[1]
                P.op('act', lambda e: e.activation(out=Vs[:, c0 * 64:(c0 + ncn) * 64].rearrange("p (c t) -> p c t", t=64),
                                                   in_=xv, func=AF.Copy), r=['v'], w=['Vs'])
            P.dma('sp', K.rw_QR[cq * 128:(cq + 1) * 128], QRs[:], r=['QRs'], w=['rw_QR'])
            P.dma('sp', K.rw_BK[cq * 128:(cq + 1) * 128], BKs[:], r=['BKs'], w=['rw_BK'])
            P.dma('sp', K.rw_V[cq * 128:(cq + 1) * 128, :], Vs[:], r=['Vs'], w=['rw_V'])
        P.barrier()


def rwkv_scan(K, d, yacc):
    P = K.P
    with ExitStack() as es:
        sb = K.mk(es)
        QR = sb('QR', [128, 4, 36, 128], BF16)
        BK = sb('BK', [128, 4, 36, 128], BF16)
        V = sb('V', [128, 4, NT], BF16)
        gC = sb('gC', [128, 4, 36])
        mk = sb('mk', [128, 2, 128])
        mk3 = sb('mk3', [128, 2, 64])
        idb = sb('idb', [128, 64], BF16)
        Mf = sb('Mf', [128, 4, 64])
        Mg = sb('Mg', [128, 4, 64])
        Mb = sb('Mb', [128, 4, 64], BF16)
        W1 = sb('W1', [128, 4, 128], BF16)
        W2 = sb('W2', [128, 4, 128], BF16)
        Aa = [sb('Aa%d' % j, [128, 4, 64], BF16) for j in range(2)]
        Bb = [sb('Bb%d' % j, [128, 4, 64], BF16) for j in range(2)]
        Pm = sb('Pm', [128, 4, 64], BF16)
        Vtm = sb('Vtm', [128, 4, 64], BF16)
        RH = sb('RH', [128, 4, 64], BF16)
        nU = sb('nU', [128, 4, 64], BF16)
        bT = sb('bT', [128, 4, 64], BF16)
        kT = sb('kT', [128, 4, 64], BF16)
        for cq in range(4):
            P.dma('sp', QR[:, cq], K.rw_QR[cq * 128:(cq + 1) * 128], r=['rw_QR'], w=['QR'])
            P.dma('sp', BK[:, cq], K.rw_BK[cq * 128:(cq + 1) * 128], r=['rw_BK'], w=['BK'])
            P.dma('sp', V[:, cq], K.rw_V[cq * 128:(cq + 1) * 128, :], r=['rw_V'], w=['V'])
            P.dma('sp', gC[:, cq], K.rw_gC[cq * 128:(cq + 1) * 128, :], r=['rw_gC'], w=['gC'])
        P.dma('sp', mk[:], K.rw_mk, w=['mk'])
        P.dma('sp', mk3[:], K.rw_mk3, w=['mk3'])
        P.dma('sp', idb[:], K.rw_idb, w=['idb'])
        P.op('dve', lambda e: e.memset(Mf[:], 0.0), w=['Mf0', 'Mf1'])
        P.op('dve', lambda e: e.memset(Mb[:], 0.0), w=['Mb0', 'Mb1'])
        HH = (0, 1)

        def rg(hh):
            return slice(hh * 64, hh * 64 + 64)

        def bank(hh, j):
            return K.psum[hh * 4 + j], 'ps%d' % (hh * 4 + j)

        def bc(ap, shape):
            return ap.unsqueeze(1).to_broadcast(shape)
        for n in range(36):
            for hh in HH:
                p_ = rg(hh)
                (x1, k1), (x2, k2), (x3, k3) = bank(hh, 0), bank(hh, 1), bank(hh, 2)
                for u in range(4):
                    bt, kt_ = BK[p_, u, n, 0:64], BK[p_, u, n, 64:128]
                    qr, qt = QR[p_, u, n, :], QR[p_, u, n, 0:64]
                    mm(P, x1[p_, u * 128:(u + 1) * 128], bt, qr, True, True, r=['BK', 'QR'], w=[k1])
                    mm(P, x2[p_, u * 128:(u + 1) * 128], kt_, qr, True, True, r=['BK', 'QR'], w=[k2])
                    mm(P, x3[p_, u * 64:(u + 1) * 64], qt, bt, True, True, r=['BK', 'QR'], w=[k3])
            for hh in HH:
                p_ = rg(hh)
                h = str(hh)
                (x1, k1), (x2, k2), (x3, k3) = bank(hh, 0), bank(hh, 1), bank(hh, 2)
                v4 = lambda x, w_: x[p_, 0:4 * w_].rearrange("p (u c) -> p u c", u=4)
                P.op('dve', lambda e: e.tensor_tensor(out=W1[p_], in0=v4(x1, 128), in1=bc(mk[p_, 0, :], [64, 4, 128]),
                                                      op=ALU.mult), r=[k1, 'mk'], w=['W1' + h])
                P.op('dve', lambda e: e.tensor_tensor(out=W2[p_], in0=v4(x2, 128), in1=bc(mk[p_, 1, :], [64, 4, 128]),
                                                      op=ALU.mult), r=[k2, 'mk'], w=['W2' + h])
                P.op('dve', lambda e: e.tensor_tensor(out=Bb[0][p_], in0=v4(x3, 64), in1=bc(mk3[p_, 0, :], [64, 4, 64]),
                                                      op=ALU.mult), r=[k3, 'mk3'], w=['B0' + h])
                P.op('dve', lambda e: e.tensor_tensor(out=Pm[p_], in0=W1[p_, :, 0:64], in1=bc(mk3[p_, 1, :], [64, 4, 64]),
                                                      op=ALU.add), r=['W1' + h, 'mk3'], w=['Pm' + h])
                P.op('act', lambda e: e.activation(out=Aa[0][p_], in_=W1[p_, :, 0:64], func=AF.Copy), r=['W1' + h],
                     w=['A0' + h])
            cur = 0
            for rnd in range(1, 7):
                nxt = 1 - cur
                do_sq = rnd <= 5
                do_p = rnd >= 2
                for hh in HH:
                    p_ = rg(hh)
                    h = str(hh)
                    (xa, ka), (xb, kb), (xp, kp) = bank(hh, 0), bank(hh, 1), bank(hh, 2)
                    for u in range(4):
                        if do_sq:
                            if rnd < 5:
                                mm(P, xa[p_, u * 64:(u + 1) * 64], Bb[cur][p_, u, :], Aa[cur][p_, u, :], True, True,
                                   r=['A%d' % cur + h, 'B%d' % cur + h], w=[ka])
                            mm(P, xb[p_, u * 64:(u + 1) * 64], Aa[cur][p_, u, :], Bb[cur][p_, u, :], True, True,
                               r=['A%d' % cur + h, 'B%d' % cur + h], w=[kb])
                        if do_p:
                            mm(P, xp[p_, u * 64:(u + 1) * 64], Bb[cur][p_, u, :], Pm[p_, u, :], True, True,
                               r=['B%d' % cur + h, 'Pm' + h], w=[kp])
                for hh in HH:
                    p_ = rg(hh)
                    h = str(hh)
                    (xa, ka), (xb, kb), (xp, kp) = bank(hh, 0), bank(hh, 1), bank(hh, 2)
                    v4 = lambda x: x[p_, 0:256].rearrange("p (u c) -> p u c", u=4)
                    if do_sq:
                        if rnd < 5:
                            P.op('act', lambda e: e.activation(out=Aa[nxt][p_], in_=v4(xa), func=AF.Copy), r=[ka],
                                 w=['A%d' % nxt + h])
                        if hh == 0:
                            P.op('dve', lambda e: e.tensor_copy(Bb[nxt][p_], v4(xb)), r=[kb], w=['B%d' % nxt + h])
                        else:
                            P.op('act', lambda e: e.activation(out=Bb[nxt][p_], in_=v4(xb), func=AF.Copy), r=[kb],
                                 w=['B%d' % nxt + h])
                    if do_p:
                        P.op('dve', lambda e: e.tensor_tensor(out=Pm[p_], in0=v4(xp), in1=Pm[p_], op=ALU.add),
                             r=[kp, 'Pm' + h], w=['Pm' + h])
                if do_sq:
                    cur = nxt
            for hh in HH:
                p_ = rg(hh)
                (xt, kx) = bank(hh, 3)
                xtb = xt[:].bitcast(BF16)
                for u in range(4):
                    P.op('pe', lambda e: e.transpose(xtb[p_, u * 64:(u + 1) * 64], V[p_, u, n * 64:(n + 1) * 64],
                                                     idb[p_, :]), r=['V', 'idb'], w=[kx])
            for hh in HH:
                p_ = rg(hh)
                (xt, kx) = bank(hh, 3)
                xtb = xt[:].bitcast(BF16)
                src_ = xtb[p_, 0:256].rearrange("p (u c) -> p u c", u=4)
                if hh == 0:
                    P.op('dve', lambda e: e.tensor_copy(Vtm[p_], src_), r=[kx], w=['Vtm0'])
                else:
                    P.op('act', lambda e: e.activation(out=Vtm[p_], in_=src_, func=AF.Copy), r=[kx], w=['Vtm1'])
            for hh in HH:
                p_ = rg(hh)
                h = str(hh)
                (xr, kr) = bank(hh, 0)
                for u in range(4):
                    mm(P, xr[p_, u * 64:(u + 1) * 64], QR[p_, u, n, 0:64], Mb[p_, u, :], True, False,
                       r=['QR', 'Mb' + h], w=[kr])
                    mm(P, xr[p_, u * 64:(u + 1) * 64], W2[p_, u, 0:64], Vtm[p_, u, :], False, True,
                       r=['W2' + h, 'Vtm' + h], w=[kr])
            for hh in HH:
                p_ = rg(hh)
                h = str(hh)
                (xr, kr) = bank(hh, 0)
                src_ = xr[p_, 0:256].rearrange("p (u c) -> p u c", u=4)
                if hh == 0:
                    P.op('dve', lambda e: e.tensor_copy(RH[p_], src_), r=[kr], w=['RH0'])
                else:
                    P.op('act', lambda e: e.activation(out=RH[p_], in_=src_, func=AF.Copy), r=[kr], w=['RH1'])
            for hh in HH:
                p_ = rg(hh)
                h = str(hh)
                (xu, ku) = bank(hh, 1)
                for u in range(4):
                    mm(P, xu[p_, u * 64:(u + 1) * 64], Pm[p_, u, :], RH[p_, u, :], True, True, r=['Pm' + h, 'RH' + h],
                       w=[ku])
            for hh in HH:
                p_ = rg(hh)
                (xu, ku) = bank(hh, 1)
                src_ = xu[p_, 0:256].rearrange("p (u c) -> p u c", u=4)
                if hh == 0:
                    P.op('dve', lambda e: e.tensor_scalar(out=nU[p_], in0=src_, scalar1=-1.0, scalar2=None, op0=ALU.mult),
                         r=[ku], w=['nU0'])
                else:
                    P.op('act', lambda e: e.activation(out=nU[p_], in_=src_, func=AF.Copy, scale=-1.0), r=[ku], w=['nU1'])
            for hh in HH:
                p_ = rg(hh)
                h = str(hh)
                (xy, ky) = bank(hh, 2)
                (xt, kx) = bank(hh, 3)
                xtb = xt[:].bitcast(BF16)
                for u in range(4):
                    o_ = xy[p_, u * 64:(u + 1) * 64]
                    mm(P, o_, Mb[p_, u, :], QR[p_, u, n, 64:128], True, False, r=['Mb' + h, 'QR'], w=[ky])
                    mm(P, o_, nU[p_, u, :], W1[p_, u, 64:128], False, False, r=['nU' + h, 'W1' + h], w=[ky])
                    mm(P, o_, Vtm[p_, u, :], W2[p_, u, 64:128], False, True, r=['Vtm' + h, 'W2' + h], w=[ky])
                for u in range(4):
                    P.op('pe', lambda e: e.transpose(xtb[p_, u * 64:(u + 1) * 64], BK[p_, u, n, 0:64], idb[p_, :]),
                         r=['BK', 'idb'], w=[kx])
                    P.op('pe', lambda e: e.transpose(xtb[p_, 256 + u * 64:256 + (u + 1) * 64], BK[p_, u, n, 64:128],
                                                     idb[p_, :]), r=['BK', 'idb'], w=[kx])
            for hh in HH:
                p_ = rg(hh)
                h = str(hh)
                (xy, ky) = bank(hh, 2)
                (xt, kx) = bank(hh, 3)
                xtb = xt[:].bitcast(BF16)
                if d == 0:
                    yo = yacc[p_, :, n * 64:(n + 1) * 64]
                else:
                    if n < 4:
                        hi = NC_ - 1 - 64 * n
                    else:
                        hi = 2559 - 64 * n
                    lo = hi - 63
                    yo = yacc[p_, :, hi::-1] if lo == 0 else yacc[p_, :, hi:lo - 1:-1]
                src_ = xy[p_, 0:256].rearrange("p (u c) -> p u c", u=4)
                if d == 0:
                    P.op('dve', lambda e: e.tensor_copy(yo, src_), r=[ky], w=['yacc' + h])
                else:
                    P.op('dve', lambda e: e.tensor_tensor(out=yo, in0=src_, in1=yo, op=ALU.add), r=[ky, 'yacc' + h],
                         w=['yacc' + h])
                P.op('act', lambda e: e.activation(out=bT[p_], in_=xtb[p_, 0:256].rearrange("p (u c) -> p u c", u=4),
                                                   func=AF.Copy), r=[kx], w=['bT' + h])
                P.op('act', lambda e: e.activation(out=kT[p_], in_=xtb[p_, 256:512].rearrange("p (u c) -> p u c", u=4),
                                                   func=AF.Copy), r=[kx], w=['kT' + h])
            for hh in HH:
                p_ = rg(hh)
                h = str(hh)
                (xm, km) = bank(hh, 0)
                for u in range(4):
                    o_ = xm[p_, u * 64:(u + 1) * 64]
                    mm(P, o_, bT[p_, u, :], nU[p_, u, :], True, False, r=['bT' + h, 'nU' + h], w=[km])
                    mm(P, o_, kT[p_, u, :], Vtm[p_, u, :], False, True, r=['kT' + h, 'Vtm' + h], w=[km])
            for hh in HH:
                p_ = rg(hh)
                h = str(hh)
                (xm, km) = bank(hh, 0)
                for u in range(4):
                    g_ = gC[p_, u, n:n + 1]
                    P.op('dve', lambda e: e.tensor_scalar(out=Mg[p_, u, :], in0=Mf[p_, u, :], scalar1=g_, scalar2=None,
                                                          op0=ALU.mult), r=['Mf' + h, 'gC'], w=['Mg' + h])
                    P.op('dve', lambda e: e.scalar_tensor_tensor(out=Mf[p_, u, :], in0=xm[p_, u * 64:(u + 1) * 64],
                                                                 scalar=g_, in1=Mg[p_, u, :], op0=ALU.mult, op1=ALU.add),
                         r=[km, 'Mg' + h, 'gC'], w=['Mf' + h])
                P.op('act', lambda e: e.activation(out=Mb[p_], in_=Mf[p_], func=AF.Copy), r=['Mf' + h], w=['Mb' + h])
        P.barrier()


def rwkv_post(K, l, yacc):
    i = l // 2
    P = K.P
    with ExitStack() as es:
        sb = K.mk(es)
        cols = sb('cols', [128, 4, 5])
        blk = sb('blk', [128, 128], BF16)
        glo = sb('glo', [128, NT])
        gup = sb('gup', [128, 512])
        rr = sb('rr', [128, NT])
        vv = sb('vv', [128, NT])
        k0 = sb('k0', [128, NT])
        k1 = sb('k1', [128, NT])
        ybf = sb('ybf', [128, 512], BF16)
        yc = sb('yc', [128, 512])
        sq = sb('sq', [128, 512], BF16)
        rs = sb('rs', [128, 512])
        bon = sb('bon', [128, 512])
        eps = sb('eps', [128, 1])
        ostg = [sb('ostg%d' % j, [128, NT], BF16) for j in range(2)]
        P.dma('sp', cols[:], K.rw_cols[:, i], w=['cols'])
        P.dma('sp', blk[:], K.rw_blk, w=['blk'])
        P.dma('sp', glo[:], K.pfm[1664:1792, :], r=['pfm'], w=['glo'])
        P.dma('sp', gup[:], K.rw_gup[i], w=['gup'])
        P.op('dve', lambda e: e.memset(eps[:], 64e-5), w=['eps'])
        P.op('act', lambda e: e.activation(out=glo[:], in_=glo[:], func=AF.Sigmoid), r=['glo'], w=['glo'])
        for cq in range(4):
            os_ = cq % 2
            rows = slice(cq * 128, (cq + 1) * 128)
            P.dma('sp', rr[:], K.rw_rv[0, rows, :], r=['rw_rv'], w=['rr'])
            P.dma('sp', vv[:], K.rw_rv[1, rows, :], r=['rw_rv'], w=['vv'])
            P.dma('sp', k0[:], K.rw_keff[0, rows, :], r=['rw_keff'], w=['k0'])
            P.dma('sp', k1[:], K.rw_keff[1, rows, :], r=['rw_keff'], w=['k1'])
            P.op('pool', lambda e: e.tensor_tensor(out=k0[:], in0=k0[:], in1=k1[:], op=ALU.add), r=['k0', 'k1'], w=['k0'])
            P.op('pool', lambda e: e.tensor_tensor(out=k0[:], in0=k0[:], in1=rr[:], op=ALU.mult), r=['k0', 'rr'], w=['k0'])
            for (a, n) in TILES:
                y = yacc[:, cq, a:a + n]
                P.op('act', lambda e: e.activation(out=ybf[:, :n], in_=y, func=AF.Copy), r=['yacc0', 'yacc1'], w=['ybf'])
                pb = P.ps()
                mm(P, K.psum[pb][:, :n], blk[:], ybf[:, :n], True, True, r=['blk', 'ybf'], w=['ps%d' % pb])
                P.op('dve', lambda e: e.scalar_tensor_tensor(out=yc[:, :n], in0=K.psum[pb][:, :n], scalar=-1.0 / 64, in1=y,
                                                             op0=ALU.mult, op1=ALU.add), r=['ps%d' % pb, 'yacc0', 'yacc1'],
                     w=['yc'])
                P.op('act', lambda e: e.activation(out=sq[:, :n], in_=yc[:, :n], func=AF.Square), r=['yc'], w=['sq'])
                pb = P.ps()
                mm(P, K.psum[pb][:, :n], blk[:], sq[:, :n], True, True, r=['blk', 'sq'], w=['ps%d' % pb])
                P.op('act', lambda e: e.activation(out=rs[:, :n], in_=K.psum[pb][:, :n], func=AF.Sqrt, scale=1.0 / 64,
                                                   bias=eps[:, 0:1]), r=['ps%d' % pb, 'eps'], w=['rs'])
                P.op('dve', lambda e: e.reciprocal(rs[:, :n], rs[:, :n]), r=['rs'], w=['rs'])
                P.op('dve', lambda e: e.tensor_tensor(out=yc[:, :n], in0=yc[:, :n], in1=rs[:, :n], op=ALU.mult),
                     r=['yc', 'rs'], w=['yc'])
                P.op('dve', lambda e: e.tensor_scalar(out=yc[:, :n], in0=yc[:, :n], scalar1=cols[:, cq, 3:4],
                                                      scalar2=cols[:, cq, 4:5], op0=ALU.mult, op1=ALU.add),
                     r=['yc', 'cols'], w=['yc'])
                P.op('act', lambda e: e.activation(out=sq[:, :n], in_=k0[:, a:a + n], func=AF.Identity,
                                                   scale=cols[:, cq, 2:3]), r=['k0', 'cols'], w=['sq'])
                pb = P.ps()
                mm(P, K.psum[pb][:, :n], blk[:], sq[:, :n], True, True, r=['blk', 'sq'], w=['ps%d' % pb])
                P.op('dve', lambda e: e.tensor_tensor(out=bon[:, :n], in0=K.psum[pb][:, :n], in1=vv[:, a:a + n],
                                                      op=ALU.mult), r=['ps%d' % pb, 'vv'], w=['bon'])
                P.op('dve', lambda e: e.tensor_tensor(out=yc[:, :n], in0=yc[:, :n], in1=bon[:, :n], op=ALU.add),
                     r=['yc', 'bon'], w=['yc'])
                pb = P.ps()
                mm(P, K.psum[pb][:, :n], gup[:, cq * 128:(cq + 1) * 128], glo[:, a:a + n], True, True, r=['gup', 'glo'],
                   w=['ps%d' % pb])
                P.op('dve', lambda e: e.tensor_tensor(out=ostg[os_][:, a:a + n], in0=K.psum[pb][:, :n], in1=yc[:, :n],
                                                      op=ALU.mult), r=['ps%d' % pb, 'yc'], w=['ostg%d' % os_])
            P.dma('sp', K.mixd[l][rows, :], ostg[os_][:], r=['ostg%d' % os_], w=['mix'])
        P.barrier()


def mixer_rwkv(K, l):
    i = l // 2
    with ExitStack() as es:
        yacc = K.mk(es)('yacc', [128, 4, NT])
        for d in range(2):
            rwkv_pre(K, i, d)
            rwkv_scan(K, d, yacc)
        rwkv_post(K, l, yacc)


_CACHE = {}


def kernel(**inputs):
    inp = {k: np.asarray(v) for k, v in inputs.items()}
    if 'nc' not in _CACHE:
        _CACHE['nc'] = build({})
    nc = _CACHE['nc']
    hm = host_mixer(inp)
    in_maps = []
    for b in range(8):
        m = host_common(inp, b)
        m.update(hm)
        in_maps.append(m)
    res = run_bass_kernel_spmd(nc, in_maps, core_ids=list(range(8)))
    out = np.stack([np.asarray(r["out"], dtype=np.float32) for r in res.results], axis=0)
    return out
```

```python
import math
from contextlib import ExitStack
import numpy as np
import ml_dtypes
import concourse.bass as bass
import concourse.mybir as mybir
from concourse.bass_utils import run_bass_kernel_spmd

F32 = mybir.dt.float32
BF16 = mybir.dt.bfloat16
AF = mybir.ActivationFunctionType
ALU = mybir.AluOpType
AX = mybir.AxisListType

D = 1024
NT = 2304
NC_ = 256
NL = 2048
DEPTH = 4
EPS = 1e-6
TILES = [(0, 512), (512, 512), (1024, 512), (1536, 512), (2048, 256)]


class Prog:
    KD = 12

    def __init__(self, nc, es):
        self.nc = nc
        self.eng = {'pe': nc.tensor, 'act': nc.scalar, 'dve': nc.vector, 'pool': nc.gpsimd, 'sp': nc.sync}
        self.csem = {e: es.enter_context(nc.semaphore('c_' + e)) for e in ['pe', 'act', 'dve', 'pool']}
        self.ccnt = {e: 0 for e in self.csem}
        self.dsem = {q: [es.enter_context(nc.semaphore('d_%s%d' % (q, i))) for i in range(self.KD)]
                     for q in ['sp', 'pool']}
        self.dcnt = {q: [0] * self.KD for q in self.dsem}
        self.dnext = {q: 0 for q in self.dsem}
        self.waited = {e: {} for e in self.eng}
        self.res = {}
        self.psn = 0

    def _wait(self, e, tok):
        sid, sem, val = tok
        if self.waited[e].get(sid, 0) >= val:
            return
        self.eng[e].wait_ge(sem, val)
        self.waited[e][sid] = val

    def _deps(self, e, r, w):
        toks = []
        for k in r:
            st = self.res.get(k)
            if st and st['w'] is not None:
                toks.append(st['w'])
            if st and k.startswith('ps'):
                for t in st['r'].values():
                    if t[0] != 'c_' + e:
                        toks.append(t)
        for k in w:
            st = self.res.get(k)
            if st:
                if st['w'] is not None and st['w'][0] != 'c_' + e:
                    toks.append(st['w'])
                for t in st['r'].values():
                    if t[0] != 'c_' + e:
                        toks.append(t)
        for t in toks:
            self._wait(e, t)

    def _rec(self, r, w, tok):
        for k in r:
            st = self.res.setdefault(k, {'w': None, 'r': {}})
            st['r'][tok[0]] = tok
        for k in w:
            self.res[k] = {'w': tok, 'r': {}}

    def op(self, e, fn, r=(), w=()):
        self._deps(e, r, w)
        inst = fn(self.eng[e])
        self.ccnt[e] += 1
        inst.then_inc(self.csem[e], 1)
        tok = ('c_' + e, self.csem[e], self.ccnt[e])
        self._rec(r, w, tok)
        return tok

    def dma(self, q, out, in_, r=(), w=(), **kw):
        k = self.dnext[q]
        self.dnext[q] = (k + 1) % self.KD
        sid = 'd_%s%d' % (q, k)
        if self.dcnt[q][k] > 0:
            self._wait(q, (sid, self.dsem[q][k], 16 * self.dcnt[q][k]))
        self._deps(q, r, w)
        inst = self.eng[q].dma_start(out=out, in_=in_, **kw)
        self.dcnt[q][k] += 1
        inst.then_inc(self.dsem[q][k], 16)
        tok = (sid, self.dsem[q][k], 16 * self.dcnt[q][k])
        self._rec(r, w, tok)
        return tok

    def barrier(self):
        toks = []
        for e in self.csem:
            if self.ccnt[e]:
                toks.append(('c_' + e, self.csem[e], self.ccnt[e]))
        for q in self.dsem:
            for k in range(self.KD):
                if self.dcnt[q][k]:
                    toks.append(('d_%s%d' % (q, k), self.dsem[q][k], 16 * self.dcnt[q][k]))
        for e in self.eng:
            for t in toks:
                self._wait(e, t)
        self.res = {}

    def ps(self):
        self.psn = (self.psn + 1) % 8
        return self.psn


class Ctx:
    pass


def mm(P, out, lhsT, rhs, start, stop, r, w):
    return P.op('pe', lambda e: e.matmul(out, lhsT, rhs, start=start, stop=stop), r=r, w=w)


def load_w_block(K, wdram, kc, c0, ncols, slot):
    P = K.P
    src = wdram.rearrange("(kc p) m -> p kc m", p=128)
    for k0 in range(0, kc, 8):
        kn = min(8, kc - k0)
        sl = K.wstn
        K.wstn = (K.wstn + 1) % 2
        P.dma('sp', K.wst[sl][:, :kn, :ncols], src[:, k0:k0 + kn, c0:c0 + ncols], w=['wst%d' % sl])
        eng = 'pool' if (K.wcast % 2 == 0) else 'act'
        K.wcast += 1
        if eng == 'pool':
            P.op('pool', lambda e: e.tensor_copy(K.wb[slot][:, k0:k0 + kn, :ncols], K.wst[sl][:, :kn, :ncols]),
                 r=['wst%d' % sl], w=['wb%d' % slot])
        else:
            P.op('act', lambda e: e.activation(out=K.wb[slot][:, k0:k0 + kn, :ncols], in_=K.wst[sl][:, :kn, :ncols],
                                               func=AF.Copy), r=['wst%d' % sl], w=['wb%d' % slot])


def stage_mods(K, l):
    P = K.P
    psb = P.ps()
    for blk in range(4):
        sl = blk % 2
        P.dma('sp', K.adst[sl][:], K.ada_w[l].rearrange("(kc p) m -> p kc m", p=128)[:, :, blk * 1536:(blk + 1) * 1536],
              w=['adst%d' % sl])
        for mc in range(12):
            m = blk * 12 + mc
            for k in range(8):
                mm(P, K.psum[psb][:, 2 * m:2 * m + 2], K.adst[sl][:, k, mc * 128:(mc + 1) * 128], K.sc[:, k, :],
                   k == 0, k == 7, r=['adst%d' % sl, 'sc'], w=['ps%d' % psb])
    mv = K.modv[l]
    P.op('dve', lambda e: e.tensor_tensor(out=mv[:].rearrange("p a b -> p (a b)"), in0=K.psum[psb][:, 0:96],
                                          in1=K.adab[:, l].rearrange("p a b -> p (a b)"), op=ALU.add),
         r=['ps%d' % psb], w=['modv%d' % l])
    for j, (sci, g) in enumerate([(1, K.g1), (4, K.g2)]):
        P.op('dve', lambda e: e.tensor_scalar(out=K.gs[l][:, j], in0=mv[:, sci * 8:(sci + 1) * 8, :], scalar1=1.0,
                                              scalar2=None, op0=ALU.add), r=['modv%d' % l], w=['gs%d' % l])
        P.op('dve', lambda e: e.tensor_tensor(out=K.gs[l][:, j], in0=K.gs[l][:, j], in1=g[:, l], op=ALU.mult),
             r=['gs%d' % l], w=['gs%d' % l])


def stage_norm(K, l, j, gs_ap=None, shift_ap=None, out_dt_bf=True):
    P = K.P
    for ti in range(9):
        t0 = ti * 256
        col = 1 if ti == 0 else 0
        P.op('act', lambda e: e.activation(out=K.sq[:], in_=K.X[:, :, t0:t0 + 256], func=AF.Square),
             r=['X'], w=['sq'])
        pb = P.ps()
        for k in range(8):
            mm(P, K.psum[pb][:, 0:256], K.ones[:], K.sq[:, k, :], k == 0, k == 7, r=['sq'], w=['ps%d' % pb])
        P.op('act', lambda e: e.activation(out=K.rstd[:], in_=K.psum[pb][:, 0:256], func=AF.Sqrt, scale=1.0 / D,
                                           bias=K.epsc[:, 0:1]), r=['ps%d' % pb], w=['rstd'])
        P.op('dve', lambda e: e.reciprocal(K.rstd[:], K.rstd[:]), r=['rstd'], w=['rstd'])
        for k in range(8):
            if gs_ap is None:
                g_ap = K.gs[l][:, j, k, col:col + 1]
                s_ap = K.modv[l][:, (3 * j) * 8 + k, col:col + 1]
            else:
                g_ap = gs_ap[:, k:k + 1]
                s_ap = None
            P.op('dve', lambda e: e.scalar_tensor_tensor(out=K.ntmp[:, k, :], in0=K.X[:, k, t0:t0 + 256], scalar=g_ap,
                                                         in1=K.rstd[:], op0=ALU.mult, op1=ALU.mult),
                 r=['X', 'rstd', 'gs%d' % l], w=['ntmp%d' % k])
            if s_ap is not None:
                P.op('act', lambda e: e.activation(out=K.Hn[:, k, t0:t0 + 256], in_=K.ntmp[:, k, :], func=AF.Identity,
                                                   bias=s_ap, scale=1.0),
                     r=['ntmp%d' % k, 'modv%d' % l], w=['Hn'])
            else:
                P.op('act', lambda e: e.activation(out=K.Yf[:, k, t0:t0 + 256], in_=K.ntmp[:, k, :], func=AF.Copy),
                     r=['ntmp%d' % k], w=['Yf'])


def stage_proj(K, wdram, fin, out_dram):
    P = K.P
    nb = (fin + 511) // 512
    for b in range(nb):
        c0 = b * 512
        ncols = min(512, fin - c0)
        slot = b % 2
        load_w_block(K, wdram, 8, c0, ncols, slot)
        for mc in range(ncols // 128):
            ss = K.stn
            K.stn = (K.stn + 1) % 2
            for (t0, tn) in TILES:
                pb = P.ps()
                for k in range(8):
                    mm(P, K.psum[pb][:, :tn], K.wb[slot][:, k, mc * 128:(mc + 1) * 128], K.Hn[:, k, t0:t0 + tn],
                       k == 0, k == 7, r=['wb%d' % slot, 'Hn'], w=['ps%d' % pb])
                P.op('act', lambda e: e.activation(out=K.stg[ss][:, t0:t0 + tn], in_=K.psum[pb][:, :tn], func=AF.Copy),
                     r=['ps%d' % pb], w=['stg%d' % ss])
            P.dma('sp', out_dram[c0 + mc * 128:c0 + (mc + 1) * 128, :], K.stg[ss][:], r=['stg%d' % ss],
                  w=['pfm'])


def resid_evac(K, l, gi, pb, m, t0, tn):
    P = K.P
    segs = []
    if t0 < NC_:
        segs.append((t0, NC_ - t0, 1))
        segs.append((NC_, t0 + tn - NC_, 0))
    else:
        segs.append((t0, tn, 0))
    for (a, n, col) in segs:
        gate = K.modv[l][:, gi * 8 + m, col:col + 1]
        P.op('dve', lambda e: e.scalar_tensor_tensor(out=K.X[:, m, a:a + n], in0=K.psum[pb][:, a - t0:a - t0 + n],
                                                     scalar=gate, in1=K.X[:, m, a:a + n], op0=ALU.mult, op1=ALU.add),
             r=['ps%d' % pb, 'modv%d' % l, 'X'], w=['X'])


def stage_outproj(K, l, wdram, mix_dram):
    P = K.P
    P.dma('sp', K.Hn[:], mix_dram.rearrange("(kc p) t -> p kc t", p=128), r=['mix'], w=['Hn'])
    for b in range(2):
        slot = b % 2
        load_w_block(K, wdram, 8, b * 512, 512, slot)
        for mc in range(4):
            m = b * 4 + mc
            for (t0, tn) in TILES:
                pb = P.ps()
                for k in range(8):
                    mm(P, K.psum[pb][:, :tn], K.wb[slot][:, k, mc * 128:(mc + 1) * 128], K.Hn[:, k, t0:t0 + tn],
                       k == 0, k == 7, r=['wb%d' % slot, 'Hn'], w=['ps%d' % pb])
                resid_evac(K, l, 2, pb, m, t0, tn)


def stage_mlp(K, l):
    P = K.P
    w1 = K.mlp_w1[l]
    w2 = K.mlp_w2[l]
    for b in range(8):
        slot = b % 2
        load_w_block(K, w1, 8, b * 512, 512, slot)
        for mc in range(4):
            ss = K.stn
            K.stn = (K.stn + 1) % 2
            for (t0, tn) in TILES:
                pb = P.ps()
                for k in range(8):
                    mm(P, K.psum[pb][:, :tn], K.wb[slot][:, k, mc * 128:(mc + 1) * 128], K.Hn[:, k, t0:t0 + tn],
                       k == 0, k == 7, r=['wb%d' % slot, 'Hn'], w=['ps%d' % pb])
                P.op('act', lambda e: e.activation(out=K.rl[:, :tn], in_=K.psum[pb][:, :tn], func=AF.Relu),
                     r=['ps%d' % pb], w=['rl'])
                P.op('dve', lambda e: e.tensor_tensor(out=K.stgb[ss][:, t0:t0 + tn], in0=K.rl[:, :tn], in1=K.rl[:, :tn],
                                                      op=ALU.mult), r=['rl'], w=['stgb%d' % ss])
            r0 = b * 512 + mc * 128
            P.dma('sp', K.hid[r0:r0 + 128, :], K.stgb[ss][:], r=['stgb%d' % ss], w=['hid'])


def stage_mlp2(K, l):
    P = K.P
    w2 = K.mlp_w2[l]
    hsrc = K.hid.rearrange("(kc p) t -> p kc t", p=128)
    for b in range(4):
        load_w_block_to(K, w2, 32, b * 256, 256, K.wb2, 'wb2')
        for ti, (t0, tn) in enumerate(TILES):
            hs = ti % 2
            for kq in range(4):
                P.dma('sp', K.hb[hs][:, kq * 8:(kq + 1) * 8, :tn], hsrc[:, kq * 8:(kq + 1) * 8, t0:t0 + tn], r=['hid'],
                      w=['hb%d' % hs])
            for mc in range(2):
                m = b * 2 + mc
                pb = P.ps()
                for k in range(32):
                    mm(P, K.psum[pb][:, :tn], K.wb2[:, k, mc * 128:(mc + 1) * 128], K.hb[hs][:, k, :tn],
                       k == 0, k == 31, r=['wb2', 'hb%d' % hs], w=['ps%d' % pb])
                resid_evac(K, l, 5, pb, m, t0, tn)


def load_w_block_to(K, wdram, kc, c0, ncols, dst, key):
    P = K.P
    src = wdram.rearrange("(kc p) m -> p kc m", p=128)
    for k0 in range(0, kc, 8):
        sl = K.wstn
        K.wstn = (K.wstn + 1) % 2
        P.dma('sp', K.wst[sl][:, :8, :ncols], src[:, k0:k0 + 8, c0:c0 + ncols], w=['wst%d' % sl])
        eng = 'pool' if (K.wcast % 2 == 0) else 'act'
        K.wcast += 1
        if eng == 'pool':
            P.op('pool', lambda e: e.tensor_copy(dst[:, k0:k0 + 8, :ncols], K.wst[sl][:, :8, :ncols]),
                 r=['wst%d' % sl], w=[key])
        else:
            P.op('act', lambda e: e.activation(out=dst[:, k0:k0 + 8, :ncols], in_=K.wst[sl][:, :8, :ncols],
                                               func=AF.Copy), r=['wst%d' % sl], w=[key])


def stage_final(K):
    P = K.P
    stage_norm(K, 0, 0, gs_ap=K.fg)
    for tc in range(16):
        t0 = NC_ + tc * 128
        ss = tc % 2
        for half in range(2):
            pb = P.ps()
            for kk in range(4):
                k = half * 4 + kk
                P.op('pe', lambda e: e.transpose(K.psum[pb][:, kk * 128:(kk + 1) * 128], K.Yf[:, k, t0:t0 + 128],
                                                 K.ident[:]), r=['Yf'], w=['ps%d' % pb])
            P.op('act' if half else 'dve',
                 (lambda e: e.activation(out=K.ot[ss][:, half * 512:(half + 1) * 512], in_=K.psum[pb][:], func=AF.Copy))
                 if half else
                 (lambda e: e.tensor_copy(K.ot[ss][:, half * 512:(half + 1) * 512], K.psum[pb][:])),
                 r=['ps%d' % pb], w=['ot%d' % ss])
        P.dma('sp', K.out[tc * 128:(tc + 1) * 128, :], K.ot[ss][:], r=['ot%d' % ss], w=['out'])


def build(cfg):
    nc = bass.Bass("TRN2", target_bir_lowering=False)
    K = Ctx()
    K.nc = nc
    K.cfg = cfg
    K.uid = 0

    def mk(es_):
        def f(name, shape, dt=F32):
            K.uid += 1
            return es_.enter_context(nc.sbuf_tensor(name + '_u%d' % K.uid, list(shape), dt))
        return f
    K.mk = mk

    def din(name, shape, dt=F32):
        return nc.dram_tensor(name, list(shape), dt, kind="ExternalInput").ap()

    def dscr(name, shape, dt=F32):
        kind = "ExternalOutput" if name in cfg.get('dbg', ()) else "Internal"
        return nc.dram_tensor(name, list(shape), dt, kind=kind).ap()

    if cfg.get('mixer_test') is not None:
        _din = din

        def din(name, shape, dt=F32):
            if name in ('ada_w', 'mlp_w1', 'mlp_w2', 'ev_w_in', 'ev_w_out', 'od_w_in', 'od_w_out', 'xin'):
                return None
            return _din(name, shape, dt)
    K.xin = din("xin", [8, 128, NT])
    K.cc = din("cc", [128, 8, 2])
    K.ada_w = din("ada_w", [DEPTH, D, 6 * D])
    adab_d = din("adab", [128, DEPTH, 48, 2])
    g1_d = din("g1", [128, DEPTH, 8, 2])
    g2_d = din("g2", [128, DEPTH, 8, 2])
    fg_d = din("fg", [128, 8])
    ident_d = din("ident", [128, 128])
    K.mlp_w1 = din("mlp_w1", [DEPTH, D, 4 * D])
    K.mlp_w2 = din("mlp_w2", [DEPTH, 4 * D, D])
    K.ev_w_in = din("ev_w_in", [2, D, 2048])
    K.ev_w_out = din("ev_w_out", [2, D, D])
    K.od_w_in = din("od_w_in", [2, D, 3328])
    K.od_w_out = din("od_w_out", [2, D, D])
    K.out = nc.dram_tensor("out", [NL, D], F32, kind="ExternalOutput").ap()
    K.hid = dscr("hid", [4 * D, NT], BF16)
    K.pfm = din("pfm", [3328, NT]) if cfg.get('mixer_test') is not None else dscr("pfm", [3328, NT], F32)
    declare_mixer_inputs(K, din)
    if cfg.get('mix_in'):
        K.mixd = [din("mix%d" % l, [D, NT], BF16) for l in range(DEPTH)]
    else:
        K.mixd = [dscr("mix", [D, NT], BF16)] * DEPTH
    if cfg.get('mixer_test') is not None:
        with ExitStack() as es:
            P = K.P = Prog(nc, es)
            K.psum = [es.enter_context(nc.psum_tensor("psb%d" % i, [128, 512], F32)) for i in range(8)]
            K.ones = es.enter_context(nc.sbuf_tensor("ones", [128, 128], BF16))
            K.ident = es.enter_context(nc.sbuf_tensor("ident_s", [128, 128], F32))
            P.op('dve', lambda e: e.memset(K.ones[:], 1.0), w=['ones'])
            P.dma('sp', K.ident[:], ident_d, w=['g'])
            P.barrier()
            run_mixer(K, cfg['mixer_test'], cfg.get('which', 'ab'))
            P.barrier()
        return nc
    K.xdbg = [dscr("xdbg%d" % l, [8, 128, NT]) for l in range(DEPTH)] if cfg.get('xdbg') else None

    with ExitStack() as es:
        P = K.P = Prog(nc, es)

        def sb(name, shape, dt=F32):
            return es.enter_context(nc.sbuf_tensor(name, list(shape), dt))
        K.psum = [es.enter_context(nc.psum_tensor("psb%d" % i, [128, 512], F32)) for i in range(8)]
        K.sc = sb("sc", [128, 8, 2])
        K.adab = sb("adab_s", [128, DEPTH, 48, 2])
        K.g1 = sb("g1_s", [128, DEPTH, 8, 2])
        K.g2 = sb("g2_s", [128, DEPTH, 8, 2])
        K.fg = sb("fg_s", [128, 8])
        K.ident = sb("ident_s", [128, 128])
        K.ones = sb("ones", [128, 128], BF16)
        K.epsc = sb("epsc", [128, 1])
        K.modv = [sb("modv%d" % l, [128, 48, 2]) for l in range(DEPTH)]
        K.gs = [sb("gs%d" % l, [128, 2, 8, 2]) for l in range(DEPTH)]
        K.rstd = sb("rstd", [128, 256])
        K.wstn = 0
        K.wcast = 0
        K.stn = 0
        P.dma('sp', K.sc[:], K.cc, w=['sc'])
        P.dma('sp', K.adab[:], adab_d, w=['adab'])
        P.dma('sp', K.g1[:], g1_d, w=['g'])
        P.dma('sp', K.g2[:], g2_d, w=['g'])
        P.dma('sp', K.fg[:], fg_d, w=['g'])
        P.dma('sp', K.ident[:], ident_d, w=['g'])
        P.op('dve', lambda e: e.memset(K.ones[:], 1.0), w=['ones'])
        P.op('dve', lambda e: e.memset(K.epsc[:], EPS), w=['ones'])
        P.op('act', lambda e: e.activation(out=K.sc[:], in_=K.sc[:], func=AF.Silu), r=['sc'], w=['sc'])
        P.barrier()
        with ExitStack() as es2:
            K.uid = 0
            K.adst = [es2.enter_context(nc.sbuf_tensor("adst%d" % i, [128, 8, 1536], F32)) for i in range(2)]
            for l in range(DEPTH):
                stage_mods(K, l)
            P.barrier()
        K.xres = nc.dram_tensor("xres", [8, 128, NT], F32).ap()

        def load_X(src):
            P.dma('sp', K.X[:, 0:4], src.rearrange("k p t -> p k t")[:, 0:4], r=['xres'], w=['X'])
            P.dma('sp', K.X[:, 4:8], src.rearrange("k p t -> p k t")[:, 4:8], r=['xres'], w=['X'])
        for l in range(cfg.get('nlayers', DEPTH)):
            i = l // 2
            with ExitStack() as es2:
                sb2 = K.mk(es2)
                K.X = sb2("X", [128, 8, NT])
                K.Hn = sb2("Hn", [128, 8, NT], BF16)
                K.sq = sb2("sq", [128, 8, 256], BF16)
                K.ntmp = sb2("ntmp", [128, 8, 256])
                K.wst = [sb2("wst%d" % j, [128, 8, 512]) for j in range(2)]
                K.wb = [sb2("wb%d" % j, [128, 8, 512], BF16) for j in range(2)]
                K.stg = [sb2("stg%d" % j, [128, NT]) for j in range(2)]
                load_X(K.xin if l == 0 else K.xres)
                stage_norm(K, l, 0)
                if not cfg.get('mix_in'):
                    if l % 2 == 0:
                        stage_proj(K, K.ev_w_in[i], 2048, K.pfm)
                    else:
                        stage_proj(K, K.od_w_in[i], 3328, K.pfm)
                P.barrier()
            if not cfg.get('mix_in'):
                run_mixer(K, l)
                P.barrier()
            with ExitStack() as es1:
                K.X = K.mk(es1)("X", [128, 8, NT])
                load_X(K.xin if l == 0 else K.xres)
                with ExitStack() as es2:
                    sb2 = K.mk(es2)
                    K.Hn = sb2("Hn", [128, 8, NT], BF16)
                    K.sq = sb2("sq", [128, 8, 256], BF16)
                    K.ntmp = sb2("ntmp", [128, 8, 256])
                    K.wst = [sb2("wst%d" % j, [128, 8, 512]) for j in range(2)]
                    K.wb = [sb2("wb%d" % j, [128, 8, 512], BF16) for j in range(2)]
                    K.stgb = [sb2("stgb%d" % j, [128, NT], BF16) for j in range(2)]
                    K.rl = sb2("rl", [128, 512])
                    stage_outproj(K, l, (K.ev_w_out if l % 2 == 0 else K.od_w_out)[i], K.mixd[l])
                    stage_norm(K, l, 1)
                    stage_mlp(K, l)
                    P.barrier()
                with ExitStack() as es2:
                    sb2 = K.mk(es2)
                    K.wst = [sb2("wst%d" % j, [128, 8, 512]) for j in range(2)]
                    K.wb2 = sb2("wb2", [128, 32, 256], BF16)
                    K.hb = [sb2("hb%d" % j, [128, 32, 512], BF16) for j in range(2)]
                    stage_mlp2(K, l)
                    P.barrier()
                P.dma('sp', K.xres.rearrange("k p t -> p k t"), K.X[:], r=['X'], w=['xres'])
                if K.xdbg is not None:
                    P.dma('sp', K.xdbg[l].rearrange("k p t -> p k t"), K.X[:], r=['X'], w=['xdbg'])
                P.barrier()
        with ExitStack() as es2:
            sb2 = K.mk(es2)
            K.X = sb2("X", [128, 8, NT])
            K.sq = sb2("sq", [128, 8, 256], BF16)
            K.ntmp = sb2("ntmp", [128, 8, 256])
            K.Yf = sb2("Yf", [128, 8, NT])
            K.ot = [sb2("ot%d" % j, [128, D]) for j in range(2)]
            load_X(K.xres)
            stage_final(K)
            P.barrier()
    return nc


def host_common(inp, b):
    f = np.float32
    x = np.concatenate([inp['ctx'][b], inp['x'][b]], axis=0)
    m = {}
    m['xin'] = np.ascontiguousarray(x.T.reshape(8, 128, NT)).astype(f)
    cc = np.stack([inp['c'][b], inp['c_ctx']], axis=-1)
    m['cc'] = np.ascontiguousarray(cc.reshape(8, 128, 2).transpose(1, 0, 2)).astype(f)
    m['ada_w'] = inp['ada_w']
    ab = inp['ada_b'].reshape(DEPTH, 48, 128).transpose(2, 0, 1)
    m['adab'] = np.ascontiguousarray(np.repeat(ab[..., None], 2, axis=-1)).astype(f)
    for nm, key in [('g1', 'norm1_g'), ('g2', 'norm2_g')]:
        g = inp[key].reshape(DEPTH, 8, 128).transpose(2, 0, 1)
        m[nm] = np.ascontiguousarray(np.repeat(g[..., None], 2, axis=-1)).astype(f)
    m['fg'] = np.ascontiguousarray(inp['final_g'].reshape(8, 128).T).astype(f)
    m['ident'] = np.eye(128, dtype=f)
    for k in ['mlp_w1', 'mlp_w2', 'ev_w_in', 'ev_w_out', 'od_w_in', 'od_w_out']:
        m[k] = inp[k]
    return m


RW_IN = 1792


def declare_mixer_inputs(K, din):
    K.rw_cw = din("rw_cw", [128, 2, 12, 3])
    K.rw_w0a0 = din("rw_w0a0", [128, 2, 2, 4, 2])
    K.rw_wup = din("rw_wup", [2, 2, 64, 512])
    K.rw_aup = din("rw_aup", [2, 2, 64, 512])
    K.rw_gup = din("rw_gup", [2, 128, 512])
    K.rw_cols = din("rw_cols", [128, 2, 4, 5])
    K.rw_blk = din("rw_blk", [128, 128], BF16)
    K.rw_mk = din("rw_mk", [128, 2, 128])
    K.rw_mk3 = din("rw_mk3", [128, 2, 64])
    K.rw_idb = din("rw_idb", [128, 64], BF16)
    if K.cfg.get('mixer_test') is not None or True:
        dt_ = lambda nm, shp, d=F32: K.nc.dram_tensor(nm, list(shp), d).ap()
        K.rw_QR = dt_("rw_QR", [512, 36, 128], BF16)
        K.rw_BK = dt_("rw_BK", [512, 36, 128], BF16)
        K.rw_V = dt_("rw_V", [512, NT], BF16)
        K.rw_gC = dt_("rw_gC", [512, 36])
        K.rw_keff = dt_("rw_keff", [2, 512, NT])
        K.rw_rv = dt_("rw_rv", [2, 512, NT])
    K.dft = {2048: (din("dftc", [16, 128, 16, 128], BF16), din("dfts", [16, 128, 16, 128], BF16)),
             256: (din("dcc", [2, 128, 2, 128], BF16), din("dcs", [2, 128, 2, 128], BF16))}
    K.hy_zT = {2048: din("hy_zT_l", [33, 2048]), 256: din("hy_zT_c", [33, 256])}
    K.hy_tcol = {2048: din("hy_tcol_l", [128, 16]), 256: din("hy_tcol_c", [128, 2])}
    K.hy_w1 = din("hy_w1", [2, 33, 64])
    K.hy_w2 = din("hy_w2", [2, 64, 64])
    K.hy_w3 = din("hy_w3", [2, 64, 2048])
    K.hy_cols = din("hy_cols", [64, 2, 4])
    K.hy_ld = din("hy_ld", [128, 2, 2048])
    K.hy_biasb = din("hy_biasb", [128, 2, 1024])
    K.hy_cw = din("hy_cw", [128, 2, 12, 4])
    K.altc = din("altc", [128, 1], BF16)
    K.altr = din("altr", [1, 128], BF16)
    K.identb = din("identb", [128, 128], BF16)
    K.s5_par = din("s5_par", [128, 2, 32, 3])
    K.s5_B = din("s5_B", [2, 32, 128, 2, 128])
    K.s5_C = din("s5_C", [2, 32, 128, 2, 128])
    K.s5_dg = din("s5_dg", [128, 2, 2, 4])
    K.s5_glu_w = din("s5_glu_w", [2, 512, 512])
    K.iot = din("iot", [128, 96])
    K.da_lamb = din("da_lamb", [128, 2, 256])
    K.da_g = din("da_g", [2, 128, 1])
    K.ropec = din("ropec", [128, NL])
    K.ropes = din("ropes", [128, NL])


def host_mixer(inp):
    f = np.float32
    m = {}
    cwr = inp['rw_conv_w']
    m['rw_cw'] = np.ascontiguousarray(cwr.reshape(2, 3, 12, 128).transpose(3, 0, 2, 1)).astype(f)
    wa = np.stack([inp['rw_w0'], inp['rw_a0']], axis=-1)
    m['rw_w0a0'] = np.ascontiguousarray(wa.reshape(2, 2, 4, 128, 2).transpose(3, 0, 1, 2, 4)).astype(f)
    m['rw_wup'] = inp['rw_w_up']
    m['rw_aup'] = inp['rw_a_up']
    m['rw_gup'] = inp['rw_g_up']
    cl = np.stack([inp['rw_k_k'], inp['rw_k_a'], inp['rw_r_k'].reshape(2, 512), inp['rw_ln_g'], inp['rw_ln_b']], axis=-1)
    m['rw_cols'] = np.ascontiguousarray(cl.reshape(2, 4, 128, 5).transpose(2, 0, 1, 3)).astype(f)
    pp = np.arange(128)
    m['rw_blk'] = (pp[:, None] // 64 == pp[None, :] // 64).astype(ml_dtypes.bfloat16)
    s_ = (pp % 64)[:, None]
    t_ = np.arange(64)[None, :]
    su = (s_ < t_).astype(f)
    iu = (s_ <= t_).astype(f)
    mk = np.zeros((128, 2, 128), f)
    mk[:, 0, 0:64] = -su
    mk[:, 0, 64:128] = iu
    mk[:, 1, 0:64] = su
    mk[:, 1, 64:128] = iu
    m['rw_mk'] = mk
    mk3 = np.zeros((128, 2, 64), f)
    mk3[:, 0] = -(t_ < s_).astype(f)
    mk3[:, 1] = (s_ == t_).astype(f)
    m['rw_mk3'] = mk3
    m['rw_idb'] = (s_ == t_).astype(ml_dtypes.bfloat16)
    bf = ml_dtypes.bfloat16
    for n, (nc_, ns_) in [(2048, ('dftc', 'dfts')), (256, ('dcc', 'dcs'))]:
        N = 2 * n
        a = np.arange(n, dtype=np.int64)
        ph = (np.outer(a, a) % N).astype(np.float64) * (2 * np.pi / N)
        nb = n // 128
        for nm, fn in [(nc_, np.cos), (ns_, np.sin)]:
            T = fn(ph).reshape(nb, 128, nb, 128)
            m[nm] = np.ascontiguousarray(T.transpose(2, 1, 0, 3)).astype(bf)
        t = np.linspace(0.0, 1.0, n, dtype=f)[:, None]
        w = (f(2.0 * math.pi) * np.arange(n, dtype=f)[:, None] / f(n)).astype(f)
        bands = np.linspace(1e-4, 15, 16, dtype=f)[None, :]
        z = np.concatenate([t, np.cos(bands * w), -np.sin(bands * w)], axis=-1).astype(f)
        sfx = 'l' if n == 2048 else 'c'
        m['hy_zT_' + sfx] = np.ascontiguousarray(z.T)
        m['hy_tcol_' + sfx] = np.ascontiguousarray(t[:, 0].reshape(nb, 128).T)
    m['hy_w1'] = inp['hy_f_w1']
    m['hy_w2'] = inp['hy_f_w2']
    m['hy_w3'] = inp['hy_f_w3']
    m['hy_cols'] = np.ascontiguousarray(np.stack([inp['hy_f_b1'], inp['hy_f_b2'], inp['hy_f_freq'][:, 0],
                                                  inp['hy_f_freq'][:, 1]], axis=-1).transpose(1, 0, 2)).astype(f)
    m['hy_ld'] = np.ascontiguousarray(np.broadcast_to(inp['hy_log_decay'].reshape(1, 2, 2048), (128, 2, 2048))).astype(f)
    m['hy_biasb'] = np.ascontiguousarray(np.broadcast_to(inp['hy_bias'].reshape(1, 2, 1024), (128, 2, 1024))).astype(f)
    cw = np.concatenate([inp['hy_conv_w'], inp['hy_conv_b'][:, None, :]], axis=1)
    m['hy_cw'] = np.ascontiguousarray(cw.reshape(2, 4, 12, 128).transpose(3, 0, 2, 1)).astype(f)
    alt = np.where(np.arange(128) % 2 == 0, 1.0, -1.0)
    m['altc'] = alt.reshape(128, 1).astype(bf)
    m['altr'] = alt.reshape(1, 128).astype(bf)
    m['identb'] = np.eye(128).astype(bf)
    par = np.zeros((128, 2, 32, 3), f)
    Bp = np.zeros((2, 32, 128, 2, 128), f)
    Cp = np.zeros((2, 32, 128, 2, 128), f)
    for i in range(2):
        for d in range(2):
            for gp in range(16):
                idx = d * 16 + gp
                for g2 in range(2):
                    g = 2 * gp + g2
                    st = slice(g2 * 64, g2 * 64 + 64)
                    par[st, i, idx, 0] = inp['s5_lam_re'][i, d, g]
                    par[st, i, idx, 1] = inp['s5_lam_im'][i, d, g]
                    par[st, i, idx, 2] = inp['s5_log_dt'][i, d, g]
                    ch = slice((gp % 4) * 32 + g2 * 16, (gp % 4) * 32 + g2 * 16 + 16)
                    Bp[i, idx, ch, 0, st] = inp['s5_b_re'][i, d, g].T
                    Bp[i, idx, ch, 1, st] = inp['s5_b_im'][i, d, g].T
                    Cp[i, idx, st, 0, ch] = inp['s5_c_re'][i, d, g].T
                    Cp[i, idx, st, 1, ch] = inp['s5_c_im'][i, d, g].T
    m['s5_par'] = par
    m['s5_B'] = Bp
    m['s5_C'] = Cp
    dg = np.zeros((128, 2, 2, 4), f)
    for i in range(2):
        dg[:, i, 0, :] = inp['s5_d'][i].reshape(4, 128).T
        dg[:, i, 1, :] = inp['s5_glu_b'][i].reshape(4, 128).T
    m['s5_dg'] = dg
    m['s5_glu_w'] = inp['s5_glu_w']
    m['iot'] = np.ascontiguousarray(np.broadcast_to(
        np.concatenate([np.arange(48), 48 * np.arange(48)]).astype(f)[None], (128, 96)))
    m['da_lamb'] = np.ascontiguousarray(np.broadcast_to(inp['da_lam'].reshape(1, 2, 256), (128, 2, 256))).astype(f)
    m['da_g'] = np.ascontiguousarray(inp['da_subln_g'].reshape(2, 128, 1)).astype(f)
    p = np.arange(128)
    d = p % 64
    a = d // 32
    fr = d % 16
    half = (d % 32) // 16
    t = np.arange(NL)
    row = (t // 64).astype(f)
    col = (t % 64).astype(f)
    inv = (f(10000.0) ** (-(np.arange(16, dtype=f)) / f(16))).astype(f)
    pos = np.where(a[:, None] == 0, row[None, :], col[None, :]).astype(f)
    ang = (pos * inv[fr][:, None]).astype(f)
    m['ropec'] = np.cos(ang).astype(f)
    m['ropes'] = (np.sin(ang) * np.where(half == 0, -1.0, 1.0)[:, None]).astype(f)
    return m


def run_mixer(K, l, which='ab'):
    if l % 2 == 0:
        if 'a' in which:
            mixer_s5(K, l)
            K.P.barrier()
        if 'b' in which:
            mixer_hyena(K, l)
            K.P.barrier()
    if l % 2 == 1:
        if 'a' in which:
            mixer_rwkv(K, l)
            K.P.barrier()
        if 'b' in which:
            mixer_attn(K, l)
            K.P.barrier()


def mixer_attn(K, l):
    i = l // 2
    P = K.P
    cnt = 0
    lam_init = 0.8 - 0.6 * math.exp(-0.3 * l)
    QR = RW_IN
    with ExitStack() as es:
        sb = K.mk(es)
        xf = [sb('xf%d' % j, [128, NT]) for j in range(2)]
        xsw = [sb('xsw%d' % j, [128, NL]) for j in range(2)]
        t1 = sb('t1', [128, NL])
        t2 = sb('t2', [128, NL])
        cosT = sb('cosT', [128, NL])
        sinT = sb('sinT', [128, NL])
        qb = sb('qb', [128, 4, NT], BF16)
        kb = sb('kb', [128, 4, NT], BF16)
        vtm = sb('vtm', [128, 18, 512], BF16)
        lamt = sb('lamt', [128, 256])
        pr = sb('pr', [128, 128])
        sv = sb('sv', [128, 2])
        ev = sb('ev', [128, 2])
        nlam = sb('nlam', [128, 1])
        gsub = sb('gsub', [128, 1])
        eps5 = sb('eps5', [128, 1])
        E = [sb('E%d' % j, [128, 512], BF16) for j in range(3)]
        rz = [sb('rz%d' % j, [128, 512]) for j in range(2)]
        tt = [sb('tt%d' % j, [128, 512]) for j in range(2)]
        osb = sb('osb', [128, 512])
        sqb = sb('sqb', [128, 512], BF16)
        rs = sb('rs', [128, 512])
        ostg = [sb('ostg%d' % j, [128, NT], BF16) for j in range(2)]
        P.dma('sp', cosT[:], K.ropec, w=['cosT'])
        P.dma('sp', sinT[:], K.ropes, w=['sinT'])
        P.dma('sp', lamt[:], K.da_lamb[:, i], w=['lamt'])
        P.dma('sp', gsub[:], K.da_g[i], w=['gsub'])
        P.op('dve', lambda e: e.memset(eps5[:], 1e-5), w=['eps5'])
        P.op('dve', lambda e: e.tensor_tensor(out=pr[:, 0:64], in0=lamt[:, 0:64], in1=lamt[:, 64:128], op=ALU.mult),
             r=['lamt'], w=['pr'])
        P.op('dve', lambda e: e.tensor_tensor(out=pr[:, 64:128], in0=lamt[:, 128:192], in1=lamt[:, 192:256], op=ALU.mult),
             r=['lamt'], w=['pr'])
        P.op('dve', lambda e: e.tensor_reduce(out=sv[:], in_=pr[:].rearrange("p (a b) -> p a b", a=2), axis=AX.X,
                                              op=ALU.add), r=['pr'], w=['sv'])
        P.op('act', lambda e: e.activation(out=ev[:], in_=sv[:], func=AF.Exp), r=['sv'], w=['ev'])
        P.op('dve', lambda e: e.tensor_tensor(out=nlam[:], in0=ev[:, 1:2], in1=ev[:, 0:1], op=ALU.subtract),
             r=['ev'], w=['nlam'])
        P.op('dve', lambda e: e.tensor_scalar(out=nlam[:], in0=nlam[:], scalar1=-lam_init, scalar2=None, op0=ALU.add),
             r=['nlam'], w=['nlam'])
        P.op('dve', lambda e: e.tensor_scalar(out=gsub[:], in0=gsub[:], scalar1=1.0 - lam_init, scalar2=None,
                                              op0=ALU.mult), r=['gsub'], w=['gsub'])
        n = 0
        for (dst, off, key) in [(qb, 0, 'qb'), (kb, 512, 'kb')]:
            for h in range(4):
                s_ = n % 2
                n += 1
                r0 = QR + off + h * 128
                P.dma('sp', xf[s_][:], K.pfm[r0:r0 + 128, :], r=['pfm'], w=['xf%d' % s_])
                for j in range(8):
                    P.dma('sp', xsw[s_][16 * j:16 * j + 16, :], K.pfm[r0 + 16 * (j ^ 1):r0 + 16 * (j ^ 1) + 16, NC_:NT],
                          r=['pfm'], w=['xsw%d' % s_])
                P.op('act', lambda e: e.activation(out=dst[:, h, 0:NC_], in_=xf[s_][:, 0:NC_], func=AF.Copy),
                     r=['xf%d' % s_], w=[key])
                P.op('dve', lambda e: e.tensor_tensor(out=t1[:], in0=xf[s_][:, NC_:NT], in1=cosT[:], op=ALU.mult),
                     r=['xf%d' % s_, 'cosT'], w=['t1'])
                P.op('pool', lambda e: e.tensor_tensor(out=t2[:], in0=xsw[s_][:], in1=sinT[:], op=ALU.mult),
                     r=['xsw%d' % s_, 'sinT'], w=['t2'])
                P.op('dve', lambda e: e.tensor_tensor(out=dst[:, h, NC_:NT], in0=t1[:], in1=t2[:], op=ALU.add),
                     r=['t1', 't2'], w=[key])
        for h in range(4):
            s_ = n % 2
            n += 1
            r0 = QR + 1024 + h * 128
            P.dma('sp', xf[s_][:], K.pfm[r0:r0 + 128, :], r=['pfm'], w=['xf%d' % s_])
            for c0 in range(0, 18, 4):
                cn = min(4, 18 - c0)
                pb = 4 + (c0 // 4) % 3
                for cc in range(cn):
                    c = c0 + cc
                    P.op('pe', lambda e: e.transpose(K.psum[pb][:, cc * 128:(cc + 1) * 128],
                                                     xf[s_][:, c * 128:(c + 1) * 128], K.ident[:]),
                         r=['xf%d' % s_], w=['ps%d' % pb])
                P.op('act', lambda e: e.activation(
                    out=vtm[:, c0:c0 + cn, h * 128:(h + 1) * 128],
                    in_=K.psum[pb][:, 0:cn * 128].rearrange("p (c d) -> p c d", c=cn), func=AF.Copy),
                    r=['ps%d' % pb], w=['vtm'])
        for h in range(4):
            os_ = h % 2
            qtiles = [(0, 256, [0, 1])] + [(256 + 512 * j, 512, list(range(18))) for j in range(4)]
            for (q0, qn, kch) in qtiles:
                items = [(m_, ci, c) for m_ in range(2) for ci, c in enumerate(kch)]

                def emit_qk(m_, ci, c):
                    nonlocal cnt
                    pbs = m_ * 64
                    sbk = 4 + cnt % 3
                    eb = cnt % 3
                    cnt += 1
                    mm(P, K.psum[sbk][:, :qn], kb[pbs:pbs + 64, h, c * 128:(c + 1) * 128],
                       qb[pbs:pbs + 64, h, q0:q0 + qn], True, True, r=['kb', 'qb'], w=['ps%d' % sbk])
                    P.op('act', lambda e: e.activation(out=E[eb][:, :qn], in_=K.psum[sbk][:, :qn], func=AF.Exp,
                                                       scale=0.125), r=['ps%d' % sbk], w=['E%d' % eb])
                    return eb

                def emit_pv(m_, ci, c, eb):
                    mm(P, K.psum[m_][:, :qn], vtm[:, c, h * 128:(h + 1) * 128], E[eb][:, :qn], ci == 0,
                       ci == len(kch) - 1, r=['vtm', 'E%d' % eb], w=['ps%d' % m_])
                    mm(P, K.psum[2 + m_][:, :qn], K.ones[:], E[eb][:, :qn], ci == 0, ci == len(kch) - 1,
                       r=['E%d' % eb], w=['ps%d' % (2 + m_)])
                prev = None
                for it in items:
                    eb = emit_qk(*it)
                    if prev is not None:
                        emit_pv(*prev)
                    prev = it + (eb,)
                emit_pv(*prev)
                for m_ in range(2):
                    P.op('dve', lambda e: e.reciprocal(rz[m_][:, :qn], K.psum[2 + m_][:, :qn]), r=['ps%d' % (2 + m_)],
                         w=['rz%d' % m_])
                    P.op('dve', lambda e: e.tensor_tensor(out=tt[m_][:, :qn], in0=K.psum[m_][:, :qn], in1=rz[m_][:, :qn],
                                                          op=ALU.mult), r=['ps%d' % m_, 'rz%d' % m_], w=['tt%d' % m_])
                P.op('dve', lambda e: e.scalar_tensor_tensor(out=osb[:, :qn], in0=tt[1][:, :qn], scalar=nlam[:, 0:1],
                                                             in1=tt[0][:, :qn], op0=ALU.mult, op1=ALU.add),
                     r=['tt0', 'tt1', 'nlam'], w=['osb'])
                P.op('act', lambda e: e.activation(out=sqb[:, :qn], in_=osb[:, :qn], func=AF.Square), r=['osb'],
                     w=['sqb'])
                mm(P, K.psum[7][:, :qn], K.ones[:], sqb[:, :qn], True, True, r=['sqb'], w=['ps7'])
                P.op('act', lambda e: e.activation(out=rs[:, :qn], in_=K.psum[7][:, :qn], func=AF.Sqrt, scale=1.0 / 128,
                                                   bias=eps5[:, 0:1]), r=['ps7', 'eps5'], w=['rs'])
                P.op('dve', lambda e: e.reciprocal(rs[:, :qn], rs[:, :qn]), r=['rs'], w=['rs'])
                P.op('dve', lambda e: e.scalar_tensor_tensor(out=ostg[os_][:, q0:q0 + qn], in0=osb[:, :qn],
                                                             scalar=gsub[:, 0:1], in1=rs[:, :qn], op0=ALU.mult,
                                                             op1=ALU.mult), r=['osb', 'rs', 'gsub'], w=['ostg%d' % os_])
            P.dma('sp', K.mixd[l][512 + h * 128:512 + (h + 1) * 128, :], ostg[os_][:], r=['ostg%d' % os_], w=['mix'])


TS5 = [(0, 256), (256, 512), (768, 512), (1280, 512), (1792, 512)]
TWO_PI = 2.0 * math.pi


def wrap_sin(P, sb_t, dst, src, shift, key, n, np_=128, sin=True):
    tmp, msk = sb_t
    tmp = tmp[:np_]
    msk = msk[:np_]
    P.op('dve', lambda e: e.tensor_scalar(out=tmp[:, :n], in0=src, scalar1=shift, scalar2=None, op0=ALU.add),
         r=[key], w=[key + 't'])
    for (cmp, thr, add) in [(ALU.is_gt, math.pi, -TWO_PI), (ALU.is_lt, -math.pi, TWO_PI)] * 2:
        P.op('dve', lambda e: e.tensor_scalar(out=msk[:, :n], in0=tmp[:, :n], scalar1=thr, scalar2=None, op0=cmp),
             r=[key + 't'], w=[key + 'm'])
        P.op('dve', lambda e: e.scalar_tensor_tensor(out=tmp[:, :n], in0=msk[:, :n], scalar=add, in1=tmp[:, :n],
                                                     op0=ALU.mult, op1=ALU.add), r=[key + 'm', key + 't'],
             w=[key + 't'])
    if sin:
        P.op('act', lambda e: e.activation(out=dst, in_=tmp[:, :n], func=AF.Sin), r=[key + 't'], w=[key + 'o'])
    else:
        P.op('dve', lambda e: e.tensor_copy(dst, tmp[:, :n]), r=[key + 't', key], w=[key, key + 'o'])


def reduce_angle(P, tiles, ang, n, key, np_=128):
    kf, ki = tiles
    kf = kf[:np_]
    ki = ki[:np_]
    ang = ang[:np_]
    P.op('dve', lambda e: e.tensor_scalar(out=ki[:, :n], in0=ang[:, :n], scalar1=1.0 / TWO_PI, scalar2=None,
                                          op0=ALU.mult), r=[key], w=[key + 'ki'])
    P.op('dve', lambda e: e.tensor_copy(kf[:, :n], ki[:, :n]), r=[key + 'ki'], w=[key + 'kf'])
    P.op('dve', lambda e: e.scalar_tensor_tensor(out=ang[:, :n], in0=kf[:, :n], scalar=-TWO_PI, in1=ang[:, :n],
                                                 op0=ALU.mult, op1=ALU.add), r=[key + 'kf', key], w=[key])


def mixer_s5(K, l):
    i = l // 2
    P = K.P
    I32 = mybir.dt.int32
    with ExitStack() as es:
        sb = K.mk(es)
        par = sb('par', [128, 32, 3])
        c_ = {nm: sb('c_' + nm, [128, 32]) for nm in
              ['dt', 'lre', 'th', 'a', 'rho', 'cth', 'sth', 'lbr', 'lbi', 'den', 'nr', 'ni', 'fr', 'fi', 'nfi', 't1',
               't2']}
        wt = (sb('wtmp', [128, 96]), sb('wmsk', [128, 96]))
        rt = (sb('rkf', [128, 96]), sb('rki', [128, 96], I32))
        iot = sb('iot', [128, 96])
        ang = sb('ang', [128, 96])
        stc = sb('stc', [128, 96])
        sts = sb('sts', [128, 96])
        dg = sb('dg', [128, 2, 4])
        uf = sb('uf', [128, NT])
        ub = sb('ub', [128, NT], BF16)
        cosF2 = [sb('cosF%d' % j, [128, NT], BF16) for j in range(2)]
        sinF2 = [sb('sinF%d' % j, [128, NT], BF16) for j in range(2)]
        tq = [sb('tq%d' % j, [128, NT]) for j in range(3)]
        cst = sb('cst', [128, 2])
        zre2 = [sb('zre%d' % j, [128, NT]) for j in range(2)]
        zim2 = [sb('zim%d' % j, [128, NT]) for j in range(2)]
        wre = sb('wre', [128, NT], BF16)
        wim = sb('wim', [128, NT], BF16)
        prd = [sb('prd%d' % j, [128, NT], BF16) for j in range(4)]
        tm8 = [sb('tm%d' % j, [128, 512]) for j in range(8)]
        rhoT2 = [sb('rhoT%d' % j, [128, NL]) for j in range(2)]
        ygb = sb('ygb', [128, 4, NT], BF16)
        Bst2 = [sb('Bst%d' % j, [128, 2, 128]) for j in range(2)]
        Bb2 = [sb('Bb%d' % j, [128, 2, 128], BF16) for j in range(2)]
        Cst2 = [sb('Cst%d' % j, [128, 2, 128]) for j in range(2)]
        Cb2 = [sb('Cb%d' % j, [128, 3, 128], BF16) for j in range(2)]
        ct = sb('ct', [128, 128])
        gst = sb('gst', [128, 4, 512])
        gwb = sb('gwb', [128, 4, 512], BF16)
        sig = tm8[1]
        ostg = [sb('ostg%d' % j, [128, NT], BF16) for j in range(1)] * 2
        P.dma('sp', par[:], K.s5_par[:, i], w=['par'])
        P.dma('sp', iot[:], K.iot, w=['iot'])
        P.dma('sp', dg[:], K.s5_dg[:, i], w=['dg'])
        P.dma('sp', gst[:], K.s5_glu_w[i].rearrange("(kc p) m -> p kc m", p=128), w=['gst'])
        P.op('act', lambda e: e.activation(out=gwb[:], in_=gst[:], func=AF.Copy), r=['gst'], w=['gwb'])
        P.op('dve', lambda e: e.memset(cst[:, 0:1], 1.0), w=['cst'])
        P.op('dve', lambda e: e.memset(cst[:, 1:2], 2.0), w=['cst'])
        C = c_
        lamre, lamim, logdt = par[:, :, 0], par[:, :, 1], par[:, :, 2]

        def tsc(out, in0, s1, op0, s2=None, op1=None, r=(), w=()):
            if op1 is None:
                P.op('dve', lambda e: e.tensor_scalar(out=out, in0=in0, scalar1=s1, scalar2=None, op0=op0), r=r, w=w)
            else:
                P.op('dve', lambda e: e.tensor_scalar(out=out, in0=in0, scalar1=s1, scalar2=s2, op0=op0, op1=op1),
                     r=r, w=w)

        def ttn(out, in0, in1, op, r=(), w=(), eng='dve'):
            P.op(eng, lambda e: e.tensor_tensor(out=out, in0=in0, in1=in1, op=op), r=r, w=w)
        kc = ['cs']
        P.op('act', lambda e: e.activation(out=C['dt'][:], in_=logdt, func=AF.Exp), r=['par'], w=kc)
        tsc(C['lre'][:], lamre, -1e-4, ALU.min, r=['par'], w=kc)
        ttn(C['th'][:], lamim, C['dt'][:], ALU.mult, r=kc + ['par'], w=kc)
        ttn(C['a'][:], C['lre'][:], C['dt'][:], ALU.mult, r=kc, w=kc)
        P.op('act', lambda e: e.activation(out=C['rho'][:], in_=C['a'][:], func=AF.Exp), r=kc, w=kc)
        P.op('dve', lambda e: e.tensor_copy(ang[:, :32], C['th'][:]), r=kc, w=['ang'])
        reduce_angle(P, rt, ang, 32, 'ang')
        wrap_sin(P, wt, C['sth'][:], ang[:, :32], 0.0, 'ang', 32)
        wrap_sin(P, wt, C['cth'][:], ang[:, :32], math.pi / 2, 'ang', 32)
        kc2 = ['cs', 'ango']
        ttn(C['lbr'][:], C['rho'][:], C['cth'][:], ALU.mult, r=kc2, w=kc)
        ttn(C['lbi'][:], C['rho'][:], C['sth'][:], ALU.mult, r=kc2, w=kc)
        ttn(C['den'][:], C['lre'][:], C['lre'][:], ALU.mult, r=kc, w=kc)
        ttn(C['t1'][:], lamim, lamim, ALU.mult, r=kc + ['par'], w=kc)
        ttn(C['den'][:], C['den'][:], C['t1'][:], ALU.add, r=kc, w=kc)
        P.op('dve', lambda e: e.reciprocal(C['den'][:], C['den'][:]), r=kc, w=kc)
        tsc(C['t1'][:], C['lbr'][:], -1.0, ALU.add, r=kc, w=kc)
        ttn(C['nr'][:], C['t1'][:], C['lre'][:], ALU.mult, r=kc, w=kc)
        ttn(C['t2'][:], C['lbi'][:], lamim, ALU.mult, r=kc + ['par'], w=kc)
        ttn(C['nr'][:], C['nr'][:], C['t2'][:], ALU.add, r=kc, w=kc)
        ttn(C['ni'][:], C['lbi'][:], C['lre'][:], ALU.mult, r=kc, w=kc)
        ttn(C['t2'][:], C['t1'][:], lamim, ALU.mult, r=kc + ['par'], w=kc)
        ttn(C['ni'][:], C['ni'][:], C['t2'][:], ALU.subtract, r=kc, w=kc)
        ttn(C['fr'][:], C['nr'][:], C['den'][:], ALU.mult, r=kc, w=kc)
        ttn(C['fi'][:], C['ni'][:], C['den'][:], ALU.mult, r=kc, w=kc)
        tsc(C['nfi'][:], C['fi'][:], -1.0, ALU.mult, r=kc, w=kc)

        ITS = [(cq, d, g4) for cq in range(4) for d in range(2) for g4 in range(4)]

        def tabv(T_, a, n):
            return T_[:, a:a + n]

        def gen(k_):
            cq, d, g4 = ITS[k_]
            gp = cq * 4 + g4
            idx = d * 16 + gp
            par = k_ % 2
            sp_ = str(par)
            cosF, sinF, zre, zim = cosF2[par], sinF2[par], zre2[par], zim2[par]
            Bst, Bb, Cst, Cb = Bst2[par], Bb2[par], Cst2[par], Cb2[par]
            rhoT = rhoT2[par]
            first = (d == 0 and g4 == 0)
            last = (d == 1 and g4 == 3)
            tm = tm8[0:4]
            P.dma('sp', Bst[:], K.s5_B[i, idx], w=['Bst' + sp_])
            P.op('act', lambda e: e.activation(out=Bb[:], in_=Bst[:], func=AF.Copy), r=['Bst' + sp_], w=['Bb' + sp_])
            P.dma('sp', Cst[:], K.s5_C[i, idx], w=['Cst' + sp_])
            fr, fi, nfi = C['fr'][:, idx:idx + 1], C['fi'][:, idx:idx + 1], C['nfi'][:, idx:idx + 1]
            tsc(ct[:], Cst[:, 1, :], fi, ALU.mult, r=['Cst' + sp_, 'cs'], w=['ct'])
            P.op('dve', lambda e: e.scalar_tensor_tensor(out=Cb[:, 0, :], in0=Cst[:, 0, :], scalar=fr, in1=ct[:],
                                                         op0=ALU.mult, op1=ALU.subtract),
                 r=['Cst' + sp_, 'ct', 'cs'], w=['Cb' + sp_])
            tsc(ct[:], Cst[:, 1, :], fr, ALU.mult, r=['Cst' + sp_, 'cs'], w=['ct'])
            P.op('dve', lambda e: e.scalar_tensor_tensor(out=Cb[:, 1, :], in0=Cst[:, 0, :], scalar=nfi, in1=ct[:],
                                                         op0=ALU.mult, op1=ALU.subtract),
                 r=['Cst' + sp_, 'ct', 'cs'], w=['Cb' + sp_])
            P.op('act', lambda e: e.activation(out=Cb[:, 2, :], in_=Cb[:, 0, :], func=AF.Copy, scale=-1.0),
                 r=['Cb' + sp_], w=['Cb' + sp_])
            tsc(ang[:], iot[:], C['th'][:, idx:idx + 1], ALU.mult, r=['iot', 'cs'], w=['ang'])
            reduce_angle(P, rt, ang, 96, 'ang')
            wrap_sin(P, wt, ang[:], ang[:], 0.0, 'ang', 96, sin=False)
            bA = lambda t_: t_[:, 48:96].unsqueeze(2).to_broadcast([128, 48, 48])
            bB = lambda t_: t_[:, 0:48].unsqueeze(1).to_broadcast([128, 48, 48])
            v3 = lambda t_: t_[:].rearrange("p (a b) -> p a b", a=48)
            ttn(v3(tq[0]), bA(ang), bB(ang), ALU.add, r=['ango'], w=['tq0'], eng='pool')

            def tseg(T_):
                if d == 0:
                    return [(T_[:, 0:NT], slice(0, NT))]
                return [(T_[:, NC_ - 1::-1], slice(0, NC_)), (T_[:, NT - 1:NC_ - 1:-1], slice(NC_, NT))]
            P.op('act', lambda e: e.activation(out=tq[1][:], in_=tq[0][:], func=AF.Sin, scale=0.5), r=['tq0'], w=['tq1'])
            P.op('act', lambda e: e.activation(out=tq[2][:], in_=tq[0][:], func=AF.Sin, scale=0.25), r=['tq0'], w=['tq2'])
            P.op('act', lambda e: e.activation(out=tq[0][:], in_=tq[1][:], func=AF.Square), r=['tq1'], w=['tq0'])
            for (o_, sl_) in tseg(cosF):
                P.op('act', lambda e: e.activation(out=o_, in_=tq[0][:, sl_], func=AF.Identity, scale=-2.0,
                                                   bias=cst[:, 0:1]), r=['tq0', 'cst'], w=['cosF' + sp_])
            P.op('act', lambda e: e.activation(out=tq[0][:], in_=tq[2][:], func=AF.Square), r=['tq2', 'cosF' + sp_],
                 w=['tq0'])
            P.op('act', lambda e: e.activation(out=tq[2][:], in_=tq[0][:], func=AF.Identity, scale=-4.0,
                                               bias=cst[:, 1:2]), r=['tq0', 'cst'], w=['tq2'])
            for (o_, sl_) in tseg(sinF):
                ttn(o_, tq[1][:, sl_], tq[2][:, sl_], ALU.mult, r=['tq1', 'tq2'], w=['sinF' + sp_], eng='pool')
            P.op('act', lambda e: e.activation(out=rhoT[:], in_=tq[1][:, 0:NL], func=AF.Identity, scale=0.0,
                                               bias=C['rho'][:, idx:idx + 1]), r=['tq1', 'cs'], w=['rhoT' + sp_])

        def use1(k_):
            cq, d, g4 = ITS[k_]
            gp = cq * 4 + g4
            idx = d * 16 + gp
            par = k_ % 2
            sp_ = str(par)
            cosF, sinF, zre, zim = cosF2[par], sinF2[par], zre2[par], zim2[par]
            Bst, Bb, Cst, Cb = Bst2[par], Bb2[par], Cst2[par], Cb2[par]
            rhoT = rhoT2[par]
            first = (d == 0 and g4 == 0)
            last = (d == 1 and g4 == 3)
            tm = tm8[0:4]
            for ti, (a, n) in enumerate(TS5):
                tm = tm8[4 * (ti % 2):4 * (ti % 2) + 4]
                tk = 4 * (ti % 2)
                mm(P, K.psum[5][:, :n], Bb[:, 0, :], ub[:, a:a + n], True, True, r=['Bb' + sp_, 'ub'], w=['ps5'])
                mm(P, K.psum[6][:, :n], Bb[:, 1, :], ub[:, a:a + n], True, True, r=['Bb' + sp_, 'ub'], w=['ps6'])
                cv, sv_ = tabv(cosF, a, n), tabv(sinF, a, n)
                ttn(tm[0][:, :n], K.psum[5][:, :n], cv, ALU.mult, r=['ps5', 'cosF' + sp_], w=['tm%d' % (tk + 0)])
                ttn(tm[1][:, :n], K.psum[6][:, :n], sv_, ALU.mult, r=['ps6', 'sinF' + sp_], w=['tm%d' % (tk + 1)])
                ttn(zre[:, a:a + n], tm[0][:, :n], tm[1][:, :n], ALU.add, r=['tm%d' % (tk + 0), 'tm%d' % (tk + 1)], w=['zre' + sp_], eng='pool')
                ttn(tm[2][:, :n], K.psum[6][:, :n], cv, ALU.mult, r=['ps6', 'cosF' + sp_], w=['tm%d' % (tk + 2)])
                ttn(tm[3][:, :n], K.psum[5][:, :n], sv_, ALU.mult, r=['ps5', 'sinF' + sp_], w=['tm%d' % (tk + 3)])
                ttn(zim[:, a:a + n], tm[2][:, :n], tm[3][:, :n], ALU.subtract, r=['tm%d' % (tk + 2), 'tm%d' % (tk + 3)], w=['zim' + sp_],
                    eng='pool')

        def use2(k_):
            cq, d, g4 = ITS[k_]
            gp = cq * 4 + g4
            idx = d * 16 + gp
            par = k_ % 2
            sp_ = str(par)
            cosF, sinF, zre, zim = cosF2[par], sinF2[par], zre2[par], zim2[par]
            Bst, Bb, Cst, Cb = Bst2[par], Bb2[par], Cst2[par], Cb2[par]
            rhoT = rhoT2[par]
            first = (d == 0 and g4 == 0)
            last = (d == 1 and g4 == 3)
            tm = tm8[0:4]
            for (zz, ww, kz, kw) in [(zre, wre, 'zre' + sp_, 'wre'), (zim, wim, 'zim' + sp_, 'wim')]:
                if d == 0:
                    segs = [(zz[:, 0:NC_], ww[:, 0:NC_], rhoT[:, 0:NC_], 0.0),
                            (zz[:, NC_:NT], ww[:, NC_:NT], rhoT[:, 0:NL], ww[:, NC_ - 1:NC_])]
                else:
                    segs = [(zz[:, NC_ - 1::-1], ww[:, NC_ - 1::-1], rhoT[:, 0:NC_], 0.0),
                            (zz[:, NT - 1:NC_ - 1:-1], ww[:, NT - 1:NC_ - 1:-1], rhoT[:, 0:NL], ww[:, 0:1])]
                for (zi, wo, rh, init) in segs:
                    P.op('dve', lambda e: e.tensor_tensor_scan(wo, rh, zi, init, ALU.mult, ALU.add),
                         r=[kz, 'rhoT' + sp_, kw], w=[kw])
            ttn(prd[0][:], wre[:], cosF[:], ALU.mult, r=['wre', 'cosF' + sp_], w=['prd0'])
            ttn(prd[1][:], wim[:], sinF[:], ALU.mult, r=['wim', 'sinF' + sp_], w=['prd1'], eng='pool')
            ttn(prd[2][:], wre[:], sinF[:], ALU.mult, r=['wre', 'sinF' + sp_], w=['prd2'])
            ttn(prd[3][:], wim[:], cosF[:], ALU.mult, r=['wim', 'cosF' + sp_], w=['prd3'], eng='pool')
            for ti, (a, n) in enumerate(TS5):
                for j_, cs_i in enumerate([0, 2, 1, 1]):
                    mm(P, K.psum[ti][:, :n], Cb[:, cs_i, :], prd[j_][:, a:a + n], first and j_ == 0,
                       last and j_ == 3, r=['Cb' + sp_, 'prd%d' % j_], w=['ps%d' % ti])

        def epilogue(cq):
            tm = tm8[0:4]
            for ti, (a, n) in enumerate(TS5):
                P.op('dve', lambda e: e.scalar_tensor_tensor(out=tm[0][:, :n], in0=uf[:, a:a + n], scalar=dg[:, 0, cq:cq + 1],
                                                             in1=K.psum[ti][:, :n], op0=ALU.mult, op1=ALU.add),
                     r=['uf', 'dg', 'ps%d' % ti], w=['tm0'])
                P.op('act', lambda e: e.activation(out=ygb[:, cq, a:a + n], in_=tm[0][:, :n], func=AF.Gelu),
                     r=['tm0'], w=['ygb'])

        gen(0)
        for k_ in range(32):
            cq, d, g4 = ITS[k_]
            if d == 0 and g4 == 0:
                P.dma('sp', uf[:], K.pfm[cq * 128:(cq + 1) * 128, :], r=['pfm'], w=['uf'])
                P.op('act', lambda e: e.activation(out=ub[:], in_=uf[:], func=AF.Copy), r=['uf'], w=['ub'])
            use1(k_)
            if k_ + 1 < 32:
                gen(k_ + 1)
            use2(k_)
            if d == 1 and g4 == 3:
                epilogue(cq)
        cnt = 0
        for mc in range(4):
            os_ = 0
            for (a, n) in TILES:
                pb = 5 + cnt % 3
                cnt += 1
                for k in range(4):
                    mm(P, K.psum[pb][:, :n], gwb[:, k, mc * 128:(mc + 1) * 128], ygb[:, k, a:a + n], k == 0, k == 3,
                       r=['gwb', 'ygb'], w=['ps%d' % pb])
                P.op('act', lambda e: e.activation(out=sig[:, :n], in_=K.psum[pb][:, :n], func=AF.Sigmoid,
                                                   bias=dg[:, 1, mc:mc + 1], scale=1.0), r=['ps%d' % pb, 'dg'], w=['sig'])
                ttn(ostg[os_][:, a:a + n], ygb[:, mc, a:a + n], sig[:, :n], ALU.mult, r=['ygb', 'sig'],
                    w=['ostg%d' % os_])
            P.dma('sp', K.mixd[l][mc * 128:(mc + 1) * 128, :], ostg[os_][:], r=['ostg%d' % os_], w=['mix'])


def hyena_filters(K, i, n, Kr, Ki, Kn):
    P = K.P
    I32 = mybir.dt.int32
    nb = n // 128
    dc, ds = K.dft[n]
    with ExitStack() as es0:
      h2 = K.mk(es0)('h2', [64, n])
      with ExitStack() as es:
        sb = K.mk(es)
        zT = sb('zT', [33, n])
        w1 = sb('w1', [33, 64])
        w2 = sb('w2', [64, 64])
        cols = sb('cols', [64, 4])
        h1 = sb('h1', [64, n])
        arg = sb('arg', [128, 512])
        wt = (sb('wtmp', [128, 512]), sb('wmsk', [128, 512]))
        rt = (sb('rkf', [128, 512]), sb('rki', [128, 512], I32))
        P.dma('sp', zT[:], K.hy_zT[n], w=['zT'])
        P.dma('sp', w1[:], K.hy_w1[i], w=['w1'])
        P.dma('sp', w2[:], K.hy_w2[i], w=['w2'])
        P.dma('sp', cols[:], K.hy_cols[:, i], w=['cols'])
        tl = [(a, min(512, n - a)) for a in range(0, n, 512)]
        for (src, wmat, dst, bcol, fcol, kin, kout) in [(zT, w1, h1, 0, 2, 'zT', 'h1'), (h1, w2, h2, 1, 3, 'h1', 'h2')]:
            for (a, tn) in tl:
                pb = P.ps()
                mm(P, K.psum[pb][:64, :tn], wmat[:], src[:, a:a + tn], True, True, r=[kin, 'w1', 'w2'], w=['ps%d' % pb])
                P.op('dve', lambda e: e.tensor_scalar(out=arg[:64, :tn], in0=K.psum[pb][:64, :tn],
                                                      scalar1=cols[:, bcol:bcol + 1], scalar2=cols[:, fcol:fcol + 1],
                                                      op0=ALU.add, op1=ALU.mult), r=['ps%d' % pb, 'cols'], w=['ang'])
                reduce_angle(P, rt, arg, tn, 'ang', 64)
                wrap_sin(P, wt, dst[:, a:a + tn], arg[:64, :tn], 0.0, 'ang', tn, 64)
                P.op('dve', lambda e: e.tensor_copy(dst[:, a:a + tn], dst[:, a:a + tn]), r=['ango'], w=[kout])
        P.barrier()
      with ExitStack() as es:
        sb = K.mk(es)
        w3 = sb('w3', [64, 2048])
        rate = sb('rate', [128, 2048])
        win = sb('win', [128, 2048])
        ntc = sb('ntc', [128, nb])
        hw = sb('hw', [128, 2048])
        hs = sb('hs', [128, nb, 1024], BF16)
        hd = sb('hd', [128, nb, 1024], BF16)
        tc = [sb('tc%d' % j, [128, nb, 128], BF16) for j in range(1)] * 2
        ts = [sb('ts%d' % j, [128, nb, 128], BF16) for j in range(1)] * 2
        biasb = sb('biasb', [128, 1024])
        altc = sb('altc', [128, 1], BF16)
        kt = sb('kt', [128, 512])
        P.dma('sp', w3[:], K.hy_w3[i], w=['w3'])
        P.dma('sp', rate[:], K.hy_ld[:, i], w=['rate'])
        P.dma('sp', ntc[:], K.hy_tcol[n], w=['ntc'])
        P.dma('sp', biasb[:], K.hy_biasb[:, i], w=['biasb'])
        P.dma('sp', altc[:], K.altc, w=['altc'])
        P.op('act', lambda e: e.activation(out=rate[:], in_=rate[:], func=AF.Exp), r=['rate'], w=['rate'])
        P.op('dve', lambda e: e.tensor_scalar(out=ntc[:], in0=ntc[:], scalar1=-1.0, scalar2=None, op0=ALU.mult),
             r=['ntc'], w=['ntc'])
        for c in range(nb):
            P.op('act', lambda e: e.activation(out=win[:], in_=rate[:], func=AF.Exp, scale=ntc[:, c:c + 1]),
                 r=['rate', 'ntc'], w=['win'])
            for q in range(4):
                pb = P.ps()
                mm(P, K.psum[pb][:], h2[:, c * 128:(c + 1) * 128], w3[:, q * 512:(q + 1) * 512], True, True,
                   r=['h2', 'w3'], w=['ps%d' % pb])
                P.op('dve', lambda e: e.tensor_tensor(out=hw[:, q * 512:(q + 1) * 512], in0=K.psum[pb][:],
                                                      in1=win[:, q * 512:(q + 1) * 512], op=ALU.mult),
                     r=['ps%d' % pb, 'win'], w=['hw'])
            if c == 0:
                P.op('dve', lambda e: e.memset(hw[0:1, 1024:2048], 0.0), r=['hw'], w=['hw'])
            P.op('dve', lambda e: e.tensor_tensor(out=hs[:, c, :], in0=hw[:, 0:1024], in1=hw[:, 1024:2048], op=ALU.add),
                 r=['hw'], w=['hs'])
            P.op('pool', lambda e: e.tensor_tensor(out=hd[:, c, :], in0=hw[:, 1024:2048], in1=hw[:, 0:1024],
                                                   op=ALU.subtract), r=['hw'], w=['hd'])
        sc = 1.0 / n
        for fc in range(nb):
            s_ = 0
            P.dma('sp', tc[s_][:], dc[fc], w=['tc%d' % s_])
            P.dma('sp', ts[s_][:], ds[fc], w=['ts%d' % s_])
            for q in range(2):
                pb = P.ps()
                for jc in range(nb):
                    mm(P, K.psum[pb][:], tc[s_][:, jc, :], hs[:, jc, q * 512:(q + 1) * 512], jc == 0, jc == nb - 1,
                       r=['tc%d' % s_, 'hs'], w=['ps%d' % pb])
                P.op('dve', lambda e: e.tensor_tensor(out=kt[:], in0=K.psum[pb][:], in1=biasb[:, q * 512:(q + 1) * 512],
                                                      op=ALU.add), r=['ps%d' % pb, 'biasb'], w=['kt'])
                if fc == 0:
                    P.op('dve', lambda e: e.tensor_scalar(out=kt[0:1, :], in0=kt[0:1, :], scalar1=0.5, scalar2=None,
                                                          op0=ALU.mult), r=['kt'], w=['kt'])
                P.op('act', lambda e: e.activation(out=Kr[:, fc, q * 512:(q + 1) * 512], in_=kt[:], func=AF.Copy,
                                                   scale=sc), r=['kt'], w=['Kr'])
                pb = P.ps()
                for jc in range(nb):
                    mm(P, K.psum[pb][:], ts[s_][:, jc, :], hd[:, jc, q * 512:(q + 1) * 512], jc == 0, jc == nb - 1,
                       r=['ts%d' % s_, 'hd'], w=['ps%d' % pb])
                P.op('act', lambda e: e.activation(out=Ki[:, fc, q * 512:(q + 1) * 512], in_=K.psum[pb][:], func=AF.Copy,
                                                   scale=sc), r=['ps%d' % pb], w=['Ki'])
        for q in range(2):
            pb = P.ps()
            for jc in range(nb):
                mm(P, K.psum[pb][0:1, :], altc[:, 0:1], hs[:, jc, q * 512:(q + 1) * 512], jc == 0, jc == nb - 1,
                   r=['altc', 'hs'], w=['ps%d' % pb])
            P.op('dve', lambda e: e.tensor_tensor(out=kt[0:1, :], in0=K.psum[pb][0:1, :],
                                                  in1=biasb[0:1, q * 512:(q + 1) * 512], op=ALU.add),
                 r=['ps%d' % pb, 'biasb'], w=['kt'])
            P.op('act', lambda e: e.activation(out=Kn[0:1, q * 512:(q + 1) * 512], in_=kt[0:1, :], func=AF.Copy,
                                               scale=0.5 * sc), r=['kt'], w=['Kn'])
        P.barrier()


def hyena_conv(K, n, stm, Kr, Ki, Kn, emit_out):
    P = K.P
    nb = n // 128
    dc, ds = K.dft[n]
    with ExitStack() as es:
        sb = K.mk(es)
        Yr = sb('Yr', [128, nb, 512], BF16)
        Yi = sb('Yi', [128, nb, 512], BF16)
        Yn = sb('Yn', [1, 512], BF16)
        z1 = stm[0]
        tc = [sb('tc%d' % j, [128, nb, 128], BF16) for j in range(1)] * 2
        ts = [sb('ts%d' % j, [128, nb, 128], BF16) for j in range(1)] * 2
        tm = [sb('tm%d' % j, [128, 512]) for j in range(4)]
        altc = sb('altc', [128, 1], BF16)
        altr = sb('altr', [1, 128], BF16)
        zo = sb('zo', [128, 512], BF16)
        P.dma('sp', altc[:], K.altc, w=['altc'])
        P.dma('sp', altr[:], K.altr, w=['altr'])
        for o in range(2):
            u = stm[0] if o == 0 else z1
            ku = 'stm0'
            for fc in range(nb):
                s_ = 0
                P.dma('sp', tc[s_][:], dc[fc], w=['tc%d' % s_])
                P.dma('sp', ts[s_][:], ds[fc], w=['ts%d' % s_])
                pa = P.ps()
                for jc in range(nb):
                    mm(P, K.psum[pa][:], tc[s_][:, jc, :], u[:, jc, :], jc == 0, jc == nb - 1, r=['tc%d' % s_, ku],
                       w=['ps%d' % pa])
                pbb = P.ps()
                for jc in range(nb):
                    mm(P, K.psum[pbb][:], ts[s_][:, jc, :], u[:, jc, :], jc == 0, jc == nb - 1, r=['ts%d' % s_, ku],
                       w=['ps%d' % pbb])
                kr = Kr[:, fc, o * 512:(o + 1) * 512]
                ki = Ki[:, fc, o * 512:(o + 1) * 512]
                xr, xs_ = K.psum[pa][:], K.psum[pbb][:]
                rk = ['ps%d' % pa, 'ps%d' % pbb, 'Kr', 'Ki']
                P.op('dve', lambda e: e.tensor_tensor(out=tm[0][:], in0=xr, in1=kr, op=ALU.mult), r=rk, w=['tm0'])
                P.op('dve', lambda e: e.tensor_tensor(out=tm[1][:], in0=xs_, in1=ki, op=ALU.mult), r=rk, w=['tm1'])
                P.op('pool', lambda e: e.tensor_tensor(out=Yr[:, fc, :], in0=tm[0][:], in1=tm[1][:], op=ALU.add),
                     r=['tm0', 'tm1'], w=['Yr'])
                P.op('dve', lambda e: e.tensor_tensor(out=tm[2][:], in0=xs_, in1=kr, op=ALU.mult), r=rk, w=['tm2'])
                P.op('dve', lambda e: e.tensor_tensor(out=tm[3][:], in0=xr, in1=ki, op=ALU.mult), r=rk, w=['tm3'])
                P.op('pool', lambda e: e.tensor_tensor(out=Yi[:, fc, :], in0=tm[2][:], in1=tm[3][:], op=ALU.subtract),
                     r=['tm2', 'tm3'], w=['Yi'])
            pa = P.ps()
            for jc in range(nb):
                mm(P, K.psum[pa][0:1, :], altc[:, 0:1], u[:, jc, :], jc == 0, jc == nb - 1, r=['altc', ku], w=['ps%d' % pa])
            P.op('dve', lambda e: e.tensor_tensor(out=Yn[0:1, :], in0=K.psum[pa][0:1, :],
                                                  in1=Kn[0:1, o * 512:(o + 1) * 512], op=ALU.mult),
                 r=['ps%d' % pa, 'Kn'], w=['Yn'])
            for tci in range(nb):
                s_ = 0
                P.dma('sp', tc[s_][:], dc[tci], w=['tc%d' % s_])
                P.dma('sp', ts[s_][:], ds[tci], w=['ts%d' % s_])
                pa = P.ps()
                for fc in range(nb):
                    mm(P, K.psum[pa][:], tc[s_][:, fc, :], Yr[:, fc, :], fc == 0, False, r=['tc%d' % s_, 'Yr'],
                       w=['ps%d' % pa])
                    mm(P, K.psum[pa][:], ts[s_][:, fc, :], Yi[:, fc, :], False, False, r=['ts%d' % s_, 'Yi'],
                       w=['ps%d' % pa])
                mm(P, K.psum[pa][:], altr[0:1, :], Yn[0:1, :], False, True, r=['altr', 'Yn'], w=['ps%d' % pa])
                if o == 0:
                    P.op('dve', lambda e: e.tensor_tensor(out=z1[:, tci, :], in0=K.psum[pa][:], in1=stm[1][:, tci, :],
                                                          op=ALU.mult), r=['ps%d' % pa, 'stm1'], w=['stm0'])
                else:
                    P.op('dve', lambda e: e.tensor_tensor(out=zo[:], in0=K.psum[pa][:], in1=stm[2][:, tci, :],
                                                          op=ALU.mult), r=['ps%d' % pa, 'stm2'], w=['zo'])
                    emit_out(tci, zo)
        P.barrier()


def mixer_hyena(K, l):
    i = l // 2
    P = K.P
    with ExitStack() as es:
        sb = K.mk(es)
        Kr = {n: sb('Kr%d' % n, [128, n // 128, 1024], BF16) for n in (2048, 256)}
        Ki = {n: sb('Ki%d' % n, [128, n // 128, 1024], BF16) for n in (2048, 256)}
        Kn = {n: sb('Kn%d' % n, [1, 1024], BF16) for n in (2048, 256)}
        for n in (256, 2048):
            hyena_filters(K, i, n, Kr[n], Ki[n], Kn[n])
        stm = {2048: [sb('stl%d' % j, [128, 16, 512], BF16) for j in range(3)],
               256: [sb('stc%d' % j, [128, 2, 512], BF16) for j in range(3)]}
        identb = sb('identb', [128, 128], BF16)
        ostg = sb('ostg', [128, 4, NT], BF16)
        P.dma('sp', identb[:], K.identb, w=['identb'])
        with ExitStack() as es2:
            sb2 = K.mk(es2)
            xf = [sb2('xf%d' % j, [128, NT]) for j in range(2)]
            yf = sb2('yf', [128, NT])
            yb = sb2('yb', [128, NT], BF16)
            cw = sb2('cw', [128, 12, 4])
            P.dma('sp', cw[:], K.hy_cw[:, i], w=['cw'])
            for cc in range(12):
                s_ = cc % 2
                st_i, cs_ = cc // 4, cc % 4
                P.dma('sp', xf[s_][:], K.pfm[512 + cc * 128:512 + (cc + 1) * 128, :], r=['pfm'], w=['xf%d' % s_])
                kx = 'xf%d' % s_
                for (a, n_) in [(0, NC_), (NC_, NL)]:
                    P.op('act', lambda e: e.activation(out=yf[:, a:a + n_], in_=xf[s_][:, a:a + n_], func=AF.Identity,
                                                       scale=cw[:, cc, 1:2], bias=cw[:, cc, 3:4]), r=[kx, 'cw'], w=['yf'])
                    P.op('dve', lambda e: e.scalar_tensor_tensor(out=yf[:, a + 1:a + n_], in0=xf[s_][:, a:a + n_ - 1],
                                                                 scalar=cw[:, cc, 0:1], in1=yf[:, a + 1:a + n_],
                                                                 op0=ALU.mult, op1=ALU.add), r=[kx, 'cw', 'yf'], w=['yf'])
                    P.op('dve', lambda e: e.scalar_tensor_tensor(out=yb[:, a:a + n_ - 1], in0=xf[s_][:, a + 1:a + n_],
                                                                 scalar=cw[:, cc, 2:3], in1=yf[:, a:a + n_ - 1],
                                                                 op0=ALU.mult, op1=ALU.add), r=[kx, 'cw', 'yf'], w=['yb'])
                    P.op('act', lambda e: e.activation(out=yb[:, a + n_ - 1:a + n_], in_=yf[:, a + n_ - 1:a + n_],
                                                       func=AF.Copy), r=['yf'], w=['yb'])
                for gi_, (c0, cn) in enumerate([(0, 2), (2, 4), (6, 4), (10, 4), (14, 4)]):
                    pb = P.ps()
                    pst = K.psum[pb][:].bitcast(BF16)
                    for q in range(cn):
                        c = c0 + q
                        P.op('pe', lambda e: e.transpose(pst[:, q * 128:(q + 1) * 128], yb[:, c * 128:(c + 1) * 128],
                                                         identb[:]), r=['yb', 'identb'], w=['ps%d' % pb])
                    if c0 == 0:
                        dst = stm[256][st_i][:, 0:2, cs_ * 128:(cs_ + 1) * 128]
                        kd = 'stc'
                    else:
                        dst = stm[2048][st_i][:, c0 - 2:c0 - 2 + cn, cs_ * 128:(cs_ + 1) * 128]
                        kd = 'stl'
                    src_ = pst[:, 0:cn * 128].rearrange("p (q t) -> p q t", q=cn)
                    if gi_ % 2:
                        P.op('act', lambda e: e.activation(out=dst, in_=src_, func=AF.Copy), r=['ps%d' % pb], w=[kd])
                    else:
                        P.op('dve', lambda e: e.tensor_copy(dst, src_), r=['ps%d' % pb], w=[kd])
            P.barrier()
        for n, coff in [(256, 0), (2048, 2)]:
            def emit_out(tci, zo, coff=coff):
                pb = P.ps()
                pst = K.psum[pb][:].bitcast(BF16)
                for q in range(4):
                    P.op('pe', lambda e: e.transpose(pst[:, q * 128:(q + 1) * 128], zo[:, q * 128:(q + 1) * 128],
                                                     identb[:]), r=['zo', 'identb'], w=['ps%d' % pb])
                t0 = (coff + tci) * 128
                P.op('act', lambda e: e.activation(out=ostg[:, :, t0:t0 + 128],
                                                   in_=pst[:, 0:512].rearrange("p (q t) -> p q t", q=4), func=AF.Copy),
                     r=['ps%d' % pb], w=['ostg'])
            hyena_conv(K, n, stm[n], Kr[n], Ki[n], Kn[n], emit_out)
        for cq in range(4):
            P.dma('sp', K.mixd[l][512 + cq * 128:512 + (cq + 1) * 128, :], ostg[:, cq, :], r=['ostg'], w=['mix'])


def conv3(P, y, x, cw3, kx, ky):
    for (a, n_) in [(0, NC_), (NC_, NL)]:
        P.op('act', lambda e: e.activation(out=y[:, a:a + n_], in_=x[:, a:a + n_], func=AF.Identity,
                                           scale=cw3[:, 1:2]), r=[kx], w=[ky])
        P.op('dve', lambda e: e.scalar_tensor_tensor(out=y[:, a + 1:a + n_], in0=x[:, a:a + n_ - 1], scalar=cw3[:, 0:1],
                                                     in1=y[:, a + 1:a + n_], op0=ALU.mult, op1=ALU.add),
             r=[kx, ky], w=[ky])
        P.op('dve', lambda e: e.scalar_tensor_tensor(out=y[:, a:a + n_ - 1], in0=x[:, a + 1:a + n_], scalar=cw3[:, 2:3],
                                                     in1=y[:, a:a + n_ - 1], op0=ALU.mult, op1=ALU.add),
             r=[kx, ky], w=[ky])


def rw_vis(T_, d):
    if d == 0:
        return [(T_[:, 0:NC_].rearrange("p (c t) -> p c t", t=64), 0),
                (T_[:, NC_:NT].rearrange("p (c t) -> p c t", t=64), 4)]
    return [(T_[:, NC_ - 1::-1].rearrange("p (c t) -> p c t", t=64), 0),
            (T_[:, NT - 1:NC_ - 1:-1].rearrange("p (c t) -> p c t", t=64), 4)]


def rwkv_pre(K, i, d):
    P = K.P
    with ExitStack() as es:
        sb = K.mk(es)
        cw = sb('cw', [128, 12, 3])
        w0a0 = sb('w0a0', [128, 4, 2])
        cols = sb('cols', [128, 4, 5])
        omka = sb('omka', [128, 4])
        wlo = sb('wlo', [64, NT])
        alo = sb('alo', [64, NT])
        wup = sb('wup', [64, 512])
        aup = sb('aup', [64, 512])
        blk = sb('blk', [128, 128], BF16)
        ones64 = sb('ones64', [128, 64])
        xin = sb('xin', [128, NT])
        bufs = {nm: sb('b_' + nm, [128, NT]) for nm in ['r', 'k', 'v', 'lw', 'lam', 'gi', 'gv', 'ge', 'a', 'kk', 'ke', 'b']}
        sqb = sb('sqb', [128, 512], BF16)
        rs = sb('rs', [128, 512])
        QRs = sb('QRs', [128, 36, 128], BF16)
        BKs = sb('BKs', [128, 36, 128], BF16)
        Vs = sb('Vs', [128, NT], BF16)
        gCt = sb('gCt', [128, 36])
        B = bufs
        P.dma('sp', cw[:], K.rw_cw[:, i], w=['cw'])
        P.dma('sp', w0a0[:], K.rw_w0a0[:, i, d], w=['w0a0'])
        P.dma('sp', cols[:], K.rw_cols[:, i], w=['cols'])
        P.dma('sp', wlo[:], K.pfm[1536:1600, :], r=['pfm'], w=['wlo'])
        P.dma('sp', alo[:], K.pfm[1600:1664, :], r=['pfm'], w=['alo'])
        P.dma('sp', wup[:], K.rw_wup[i, d], w=['wup'])
        P.dma('sp', aup[:], K.rw_aup[i, d], w=['aup'])
        P.dma('sp', blk[:], K.rw_blk, w=['blk'])
        P.op('dve', lambda e: e.memset(ones64[:], 1.0), w=['ones64'])
        P.op('act', lambda e: e.activation(out=wlo[:], in_=wlo[:], func=AF.Tanh), r=['wlo'], w=['wlo'])
        P.op('dve', lambda e: e.tensor_scalar(out=omka[:], in0=cols[:, :, 1], scalar1=-1.0, scalar2=1.0, op0=ALU.mult,
                                              op1=ALU.add), r=['cols'], w=['omka'])

        def tt(out, in0, in1, op, r, w, eng='dve'):
            P.op(eng, lambda e: e.tensor_tensor(out=out, in0=in0, in1=in1, op=op), r=r, w=w)
        for cq in range(4):
            for j, nm in enumerate(['r', 'k', 'v']):
                P.dma('sp', xin[:], K.pfm[j * 512 + cq * 128:j * 512 + (cq + 1) * 128, :], r=['pfm'], w=['xin'])
                conv3(P, B[nm], xin, cw[:, j * 4 + cq, :], 'xin', nm)
            if d == 0:
                P.dma('sp', K.rw_rv[0, cq * 128:(cq + 1) * 128, :], B['r'][:], r=['r'], w=['rw_rv'])
                P.dma('sp', K.rw_rv[1, cq * 128:(cq + 1) * 128, :], B['v'][:], r=['v'], w=['rw_rv'])
            for (a, n) in TILES:
                pb = P.ps()
                mm(P, K.psum[pb][:, :n], wup[:, cq * 128:(cq + 1) * 128], wlo[:, a:a + n], True, True, r=['wup', 'wlo'],
                   w=['ps%d' % pb])
                P.op('act', lambda e: e.activation(out=B['lw'][:, a:a + n], in_=K.psum[pb][:, :n], func=AF.Sigmoid,
                                                   bias=w0a0[:, cq, 0:1], scale=1.0), r=['ps%d' % pb, 'w0a0'], w=['lw'])
                pb = P.ps()
                mm(P, K.psum[pb][:, :n], aup[:, cq * 128:(cq + 1) * 128], alo[:, a:a + n], True, True, r=['aup', 'alo'],
                   w=['ps%d' % pb])
                P.op('act', lambda e: e.activation(out=B['a'][:, a:a + n], in_=K.psum[pb][:, :n], func=AF.Sigmoid,
                                                   bias=w0a0[:, cq, 1:2], scale=1.0), r=['ps%d' % pb, 'w0a0'], w=['a'])
            P.op('dve', lambda e: e.tensor_scalar(out=B['lw'][:], in0=B['lw'][:], scalar1=-math.exp(-0.5), scalar2=None,
                                                  op0=ALU.mult), r=['lw'], w=['lw'])
            for c in range(36):
                sl = slice(c * 64, (c + 1) * 64)
                if d == 0:
                    o_, i_ = B['lam'][:, sl], B['lw'][:, sl]
                else:
                    lo = c * 64
                    hi = c * 64 + 63
                    o_ = B['lam'][:, hi::-1] if lo == 0 else B['lam'][:, hi:lo - 1:-1]
                    i_ = B['lw'][:, hi::-1] if lo == 0 else B['lw'][:, hi:lo - 1:-1]
                P.op('dve', lambda e: e.tensor_tensor_scan(o_, ones64[:], i_, 0.0, ALU.mult, ALU.add),
                     r=['lw', 'ones64'], w=['lam'])
            P.op('act', lambda e: e.activation(out=B['gi'][:], in_=B['lam'][:], func=AF.Exp), r=['lam'], w=['gi'])
            P.op('act', lambda e: e.activation(out=B['gv'][:], in_=B['lam'][:], func=AF.Exp, scale=-1.0), r=['lam'],
                 w=['gv'])
            tt(B['ge'][:], B['lam'][:], B['lw'][:], ALU.subtract, ['lam', 'lw'], ['ge'])
            P.op('act', lambda e: e.activation(out=B['ge'][:], in_=B['ge'][:], func=AF.Exp), r=['ge'], w=['ge'])
            gsrc = B['gi'][:, 63::64] if d == 0 else B['gi'][:, 0::64]
            P.op('dve', lambda e: e.tensor_copy(gCt[:], gsrc), r=['gi'], w=['gCt'])
            P.dma('sp', K.rw_gC[cq * 128:(cq + 1) * 128, :], gCt[:], r=['gCt'], w=['rw_gC'])
            P.op('dve', lambda e: e.tensor_scalar(out=B['kk'][:], in0=B['k'][:], scalar1=cols[:, cq, 0:1], scalar2=None,
                                                  op0=ALU.mult), r=['k', 'cols'], w=['kk'])
            for (a, n) in TILES:
                P.op('act', lambda e: e.activation(out=sqb[:, :n], in_=B['kk'][:, a:a + n], func=AF.Square), r=['kk'],
                     w=['sqb'])
                pb = P.ps()
                mm(P, K.psum[pb][:, :n], blk[:], sqb[:, :n], True, True, r=['blk', 'sqb'], w=['ps%d' % pb])
                P.op('act', lambda e: e.activation(out=rs[:, :n], in_=K.psum[pb][:, :n], func=AF.Sqrt), r=['ps%d' % pb],
                     w=['rs'])
                P.op('dve', lambda e: e.tensor_scalar(out=rs[:, :n], in0=rs[:, :n], scalar1=1e-12, scalar2=None,
                                                      op0=ALU.max), r=['rs'], w=['rs'])
                P.op('dve', lambda e: e.reciprocal(rs[:, :n], rs[:, :n]), r=['rs'], w=['rs'])
                tt(B['kk'][:, a:a + n], B['kk'][:, a:a + n], rs[:, :n], ALU.mult, ['kk', 'rs'], ['kk'])
            P.op('dve', lambda e: e.tensor_scalar(out=B['ke'][:], in0=B['a'][:], scalar1=cols[:, cq, 1:2],
                                                  scalar2=omka[:, cq:cq + 1], op0=ALU.mult, op1=ALU.add),
                 r=['a', 'cols', 'omka'], w=['ke'])
            tt(B['ke'][:], B['ke'][:], B['k'][:], ALU.mult, ['ke', 'k'], ['ke'])
            tt(B['b'][:], B['kk'][:], B['a'][:], ALU.mult, ['kk', 'a'], ['b'], eng='pool')
            P.dma('sp', K.rw_keff[d, cq * 128:(cq + 1) * 128, :], B['ke'][:], r=['ke'], w=['rw_keff'])
            for (dst, half, x_, g_, kd, eng) in [(QRs, 0, 'kk', 'ge', 'QRs', 'dve'), (QRs, 1, 'r', 'gi', 'QRs', 'pool'),
                                                 (BKs, 0, 'b', 'gv', 'BKs', 'dve'), (BKs, 1, 'ke', 'gv', 'BKs', 'pool')]:
                for (xv, c0), (gv_, _) in zip(rw_vis(B[x_], d), rw_vis(B[g_], d)):
                    ncn = xv.shape[1]
                    tt(dst[:, c0:c0 + ncn, half * 64:(half + 1) * 64], xv, gv_, ALU.mult, [x_, g_], [kd], eng=eng)
            for (xv, c0) in rw_vis(B['v'], d):
                ncn = xv.shape[1]
                P.op('act', lambda e: e.activation(out=Vs[:, c0 * 64:(c0 + ncn) * 64].rearrange("p (c t) -> p c t", t=64),
                                                   in_=xv, func=AF.Copy), r=['v'], w=['Vs'])
            P.dma('sp', K.rw_QR[cq * 128:(cq + 1) * 128], QRs[:], r=['QRs'], w=['rw_QR'])
            P.dma('sp', K.rw_BK[cq * 128:(cq + 1) * 128], BKs[:], r=['BKs'], w=['rw_BK'])
            P.dma('sp', K.rw_V[cq * 128:(cq + 1) * 128, :], Vs[:], r=['Vs'], w=['rw_V'])
        P.barrier()


def rwkv_scan(K, d, yacc):
    P = K.P
    with ExitStack() as es:
        sb = K.mk(es)
        QR = sb('QR', [128, 4, 36, 128], BF16)
        BK = sb('BK', [128, 4, 36, 128], BF16)
        V = sb('V', [128, 4, NT], BF16)
        gC = sb('gC', [128, 4, 36])
        mk = sb('mk', [128, 2, 128])
        mk3 = sb('mk3', [128, 2, 64])
        idb = sb('idb', [128, 64], BF16)
        Mf = sb('Mf', [128, 4, 64])
        Mg = sb('Mg', [128, 4, 64])
        Mb = sb('Mb', [128, 4, 64], BF16)
        W1 = sb('W1', [128, 4, 128], BF16)
        W2 = sb('W2', [128, 4, 128], BF16)
        Aa = [sb('Aa%d' % j, [128, 4, 64], BF16) for j in range(2)]
        Bb = [sb('Bb%d' % j, [128, 4, 64], BF16) for j in range(2)]
        Pm = sb('Pm', [128, 4, 64], BF16)
        Vtm = sb('Vtm', [128, 4, 64], BF16)
        RH = sb('RH', [128, 4, 64], BF16)
        nU = sb('nU', [128, 4, 64], BF16)
        bT = sb('bT', [128, 4, 64], BF16)
        kT = sb('kT', [128, 4, 64], BF16)
        for cq in range(4):
            P.dma('sp', QR[:, cq], K.rw_QR[cq * 128:(cq + 1) * 128], r=['rw_QR'], w=['QR'])
            P.dma('sp', BK[:, cq], K.rw_BK[cq * 128:(cq + 1) * 128], r=['rw_BK'], w=['BK'])
            P.dma('sp', V[:, cq], K.rw_V[cq * 128:(cq + 1) * 128, :], r=['rw_V'], w=['V'])
            P.dma('sp', gC[:, cq], K.rw_gC[cq * 128:(cq + 1) * 128, :], r=['rw_gC'], w=['gC'])
        P.dma('sp', mk[:], K.rw_mk, w=['mk'])
        P.dma('sp', mk3[:], K.rw_mk3, w=['mk3'])
        P.dma('sp', idb[:], K.rw_idb, w=['idb'])
        P.op('dve', lambda e: e.memset(Mf[:], 0.0), w=['Mf0', 'Mf1'])
        P.op('dve', lambda e: e.memset(Mb[:], 0.0), w=['Mb0', 'Mb1'])
        HH = (0, 1)

        def rg(hh):
            return slice(hh * 64, hh * 64 + 64)

        def bank(hh, j):
            return K.psum[hh * 4 + j], 'ps%d' % (hh * 4 + j)

        def bc(ap, shape):
            return ap.unsqueeze(1).to_broadcast(shape)
        for n in range(36):
            for hh in HH:
                p_ = rg(hh)
                (x1, k1), (x2, k2), (x3, k3) = bank(hh, 0), bank(hh, 1), bank(hh, 2)
                for u in range(4):
                    bt, kt_ = BK[p_, u, n, 0:64], BK[p_, u, n, 64:128]
                    qr, qt = QR[p_, u, n, :], QR[p_, u, n, 0:64]
                    mm(P, x1[p_, u * 128:(u + 1) * 128], bt, qr, True, True, r=['BK', 'QR'], w=[k1])
                    mm(P, x2[p_, u * 128:(u + 1) * 128], kt_, qr, True, True, r=['BK', 'QR'], w=[k2])
                    mm(P, x3[p_, u * 64:(u + 1) * 64], qt, bt, True, True, r=['BK', 'QR'], w=[k3])
            for hh in HH:
                p_ = rg(hh)
                h = str(hh)
                (x1, k1), (x2, k2), (x3, k3) = bank(hh, 0), bank(hh, 1), bank(hh, 2)
                v4 = lambda x, w_: x[p_, 0:4 * w_].rearrange("p (u c) -> p u c", u=4)
                P.op('dve', lambda e: e.tensor_tensor(out=W1[p_], in0=v4(x1, 128), in1=bc(mk[p_, 0, :], [64, 4, 128]),
                                                      op=ALU.mult), r=[k1, 'mk'], w=['W1' + h])
                P.op('dve', lambda e: e.tensor_tensor(out=W2[p_], in0=v4(x2, 128), in1=bc(mk[p_, 1, :], [64, 4, 128]),
                                                      op=ALU.mult), r=[k2, 'mk'], w=['W2' + h])
                P.op('dve', lambda e: e.tensor_tensor(out=Bb[0][p_], in0=v4(x3, 64), in1=bc(mk3[p_, 0, :], [64, 4, 64]),
                                                      op=ALU.mult), r=[k3, 'mk3'], w=['B0' + h])
                P.op('dve', lambda e: e.tensor_tensor(out=Pm[p_], in0=W1[p_, :, 0:64], in1=bc(mk3[p_, 1, :], [64, 4, 64]),
                                                      op=ALU.add), r=['W1' + h, 'mk3'], w=['Pm' + h])
                P.op('act', lambda e: e.activation(out=Aa[0][p_], in_=W1[p_, :, 0:64], func=AF.Copy), r=['W1' + h],
                     w=['A0' + h])
            cur = 0
            for lev in range(1, 6):
                nxt = 1 - cur
                for hh in HH:
                    p_ = rg(hh)
                    h = str(hh)
                    (xa, ka), (xb, kb) = bank(hh, 0), bank(hh, 1)
                    for u in range(4):
                        if lev < 5:
                            mm(P, xa[p_, u * 64:(u + 1) * 64], Bb[cur][p_, u, :], Aa[cur][p_, u, :], True, True,
                               r=['A%d' % cur + h, 'B%d' % cur + h], w=[ka])
                        mm(P, xb[p_, u * 64:(u + 1) * 64], Aa[cur][p_, u, :], Bb[cur][p_, u, :], True, True,
                           r=['A%d' % cur + h, 'B%d' % cur + h], w=[kb])
                for hh in HH:
                    p_ = rg(hh)
                    h = str(hh)
                    (xa, ka), (xb, kb) = bank(hh, 0), bank(hh, 1)
                    v4 = lambda x: x[p_, 0:256].rearrange("p (u c) -> p u c", u=4)
                    if lev < 5:
                        P.op('act', lambda e: e.activation(out=Aa[nxt][p_], in_=v4(xa), func=AF.Copy), r=[ka],
                             w=['A%d' % nxt + h])
                    P.op('dve', lambda e: e.tensor_copy(Bb[nxt][p_], v4(xb)), r=[kb], w=['B%d' % nxt + h])
                for hh in HH:
                    p_ = rg(hh)
                    h = str(hh)
                    (xp, kp) = bank(hh, 2)
                    for u in range(4):
                        mm(P, xp[p_, u * 64:(u + 1) * 64], Bb[nxt][p_, u, :], Pm[p_, u, :], True, True,
                           r=['B%d' % nxt + h, 'Pm' + h], w=[kp])
                for hh in HH:
                    p_ = rg(hh)
                    h = str(hh)
                    (xp, kp) = bank(hh, 2)
                    P.op('dve', lambda e: e.tensor_tensor(out=Pm[p_], in0=xp[p_, 0:256].rearrange("p (u c) -> p u c", u=4),
                                                          in1=Pm[p_], op=ALU.add), r=[kp, 'Pm' + h], w=['Pm' + h])
                cur = nxt
            for hh in HH:
                p_ = rg(hh)
                (xt, kx) = bank(hh, 3)
                xtb = xt[:].bitcast(BF16)
                for u in range(4):
                    P.op('pe', lambda e: e.transpose(xtb[p_, u * 64:(u + 1) * 64], V[p_, u, n * 64:(n + 1) * 64],
                                                     idb[p_, :]), r=['V', 'idb'], w=[kx])
            for hh in HH:
                p_ = rg(hh)
                (xt, kx) = bank(hh, 3)
                xtb = xt[:].bitcast(BF16)
                src_ = xtb[p_, 0:256].rearrange("p (u c) -> p u c", u=4)
                if hh == 0:
                    P.op('dve', lambda e: e.tensor_copy(Vtm[p_], src_), r=[kx], w=['Vtm0'])
                else:
                    P.op('act', lambda e: e.activation(out=Vtm[p_], in_=src_, func=AF.Copy), r=[kx], w=['Vtm1'])
            for hh in HH:
                p_ = rg(hh)
                h = str(hh)
                (xr, kr) = bank(hh, 0)
                for u in range(4):
                    mm(P, xr[p_, u * 64:(u + 1) * 64], QR[p_, u, n, 0:64], Mb[p_, u, :], True, False,
                       r=['QR', 'Mb' + h], w=[kr])
                    mm(P, xr[p_, u * 64:(u + 1) * 64], W2[p_, u, 0:64], Vtm[p_, u, :], False, True,
                       r=['W2' + h, 'Vtm' + h], w=[kr])
            for hh in HH:
                p_ = rg(hh)
                h = str(hh)
                (xr, kr) = bank(hh, 0)
                src_ = xr[p_, 0:256].rearrange("p (u c) -> p u c", u=4)
                if hh == 0:
                    P.op('dve', lambda e: e.tensor_copy(RH[p_], src_), r=[kr], w=['RH0'])
                else:
                    P.op('act', lambda e: e.activation(out=RH[p_], in_=src_, func=AF.Copy), r=[kr], w=['RH1'])
            for hh in HH:
                p_ = rg(hh)
                h = str(hh)
                (xu, ku) = bank(hh, 1)
                for u in range(4):
                    mm(P, xu[p_, u * 64:(u + 1) * 64], Pm[p_, u, :], RH[p_, u, :], True, True, r=['Pm' + h, 'RH' + h],
                       w=[ku])
            for hh in HH:
                p_ = rg(hh)
                (xu, ku) = bank(hh, 1)
                src_ = xu[p_, 0:256].rearrange("p (u c) -> p u c", u=4)
                if hh == 0:
                    P.op('dve', lambda e: e.tensor_scalar(out=nU[p_], in0=src_, scalar1=-1.0, scalar2=None, op0=ALU.mult),
                         r=[ku], w=['nU0'])
                else:
                    P.op('act', lambda e: e.activation(out=nU[p_], in_=src_, func=AF.Copy, scale=-1.0), r=[ku], w=['nU1'])
            for hh in HH:
                p_ = rg(hh)
                h = str(hh)
                (xy, ky) = bank(hh, 2)
                (xt, kx) = bank(hh, 3)
                xtb = xt[:].bitcast(BF16)
                for u in range(4):
                    o_ = xy[p_, u * 64:(u + 1) * 64]
                    mm(P, o_, Mb[p_, u, :], QR[p_, u, n, 64:128], True, False, r=['Mb' + h, 'QR'], w=[ky])
                    mm(P, o_, nU[p_, u, :], W1[p_, u, 64:128], False, False, r=['nU' + h, 'W1' + h], w=[ky])
                    mm(P, o_, Vtm[p_, u, :], W2[p_, u, 64:128], False, True, r=['Vtm' + h, 'W2' + h], w=[ky])
                for u in range(4):
                    P.op('pe', lambda e: e.transpose(xtb[p_, u * 64:(u + 1) * 64], BK[p_, u, n, 0:64], idb[p_, :]),
                         r=['BK', 'idb'], w=[kx])
                    P.op('pe', lambda e: e.transpose(xtb[p_, 256 + u * 64:256 + (u + 1) * 64], BK[p_, u, n, 64:128],
                                                     idb[p_, :]), r=['BK', 'idb'], w=[kx])
            for hh in HH:
                p_ = rg(hh)
                h = str(hh)
                (xy, ky) = bank(hh, 2)
                (xt, kx) = bank(hh, 3)
                xtb = xt[:].bitcast(BF16)
                if d == 0:
                    yo = yacc[p_, :, n * 64:(n + 1) * 64]
                else:
                    if n < 4:
                        hi = NC_ - 1 - 64 * n
                    else:
                        hi = 2559 - 64 * n
                    lo = hi - 63
                    yo = yacc[p_, :, hi::-1] if lo == 0 else yacc[p_, :, hi:lo - 1:-1]
                src_ = xy[p_, 0:256].rearrange("p (u c) -> p u c", u=4)
                if d == 0:
                    P.op('dve', lambda e: e.tensor_copy(yo, src_), r=[ky], w=['yacc' + h])
                else:
                    P.op('dve', lambda e: e.tensor_tensor(out=yo, in0=src_, in1=yo, op=ALU.add), r=[ky, 'yacc' + h],
                         w=['yacc' + h])
                P.op('act', lambda e: e.activation(out=bT[p_], in_=xtb[p_, 0:256].rearrange("p (u c) -> p u c", u=4),
                                                   func=AF.Copy), r=[kx], w=['bT' + h])
                P.op('act', lambda e: e.activation(out=kT[p_], in_=xtb[p_, 256:512].rearrange("p (u c) -> p u c", u=4),
                                                   func=AF.Copy), r=[kx], w=['kT' + h])
            for hh in HH:
                p_ = rg(hh)
                h = str(hh)
                (xm, km) = bank(hh, 0)
                for u in range(4):
                    o_ = xm[p_, u * 64:(u + 1) * 64]
                    mm(P, o_, bT[p_, u, :], nU[p_, u, :], True, False, r=['bT' + h, 'nU' + h], w=[km])
                    mm(P, o_, kT[p_, u, :], Vtm[p_, u, :], False, True, r=['kT' + h, 'Vtm' + h], w=[km])
            for hh in HH:
                p_ = rg(hh)
                h = str(hh)
                (xm, km) = bank(hh, 0)
                for u in range(4):
                    g_ = gC[p_, u, n:n + 1]
                    P.op('dve', lambda e: e.tensor_scalar(out=Mg[p_, u, :], in0=Mf[p_, u, :], scalar1=g_, scalar2=None,
                                                          op0=ALU.mult), r=['Mf' + h, 'gC'], w=['Mg' + h])
                    P.op('dve', lambda e: e.scalar_tensor_tensor(out=Mf[p_, u, :], in0=xm[p_, u * 64:(u + 1) * 64],
                                                                 scalar=g_, in1=Mg[p_, u, :], op0=ALU.mult, op1=ALU.add),
                         r=[km, 'Mg' + h, 'gC'], w=['Mf' + h])
                P.op('act', lambda e: e.activation(out=Mb[p_], in_=Mf[p_], func=AF.Copy), r=['Mf' + h], w=['Mb' + h])
        P.barrier()


def rwkv_post(K, l, yacc):
    i = l // 2
    P = K.P
    with ExitStack() as es:
        sb = K.mk(es)
        cols = sb('cols', [128, 4, 5])
        blk = sb('blk', [128, 128], BF16)
        glo = sb('glo', [128, NT])
        gup = sb('gup', [128, 512])
        rr = sb('rr', [128, NT])
        vv = sb('vv', [128, NT])
        k0 = sb('k0', [128, NT])
        k1 = sb('k1', [128, NT])
        ybf = sb('ybf', [128, 512], BF16)
        yc = sb('yc', [128, 512])
        sq = sb('sq', [128, 512], BF16)
        rs = sb('rs', [128, 512])
        bon = sb('bon', [128, 512])
        eps = sb('eps', [128, 1])
        ostg = [sb('ostg%d' % j, [128, NT], BF16) for j in range(2)]
        P.dma('sp', cols[:], K.rw_cols[:, i], w=['cols'])
        P.dma('sp', blk[:], K.rw_blk, w=['blk'])
        P.dma('sp', glo[:], K.pfm[1664:1792, :], r=['pfm'], w=['glo'])
        P.dma('sp', gup[:], K.rw_gup[i], w=['gup'])
        P.op('dve', lambda e: e.memset(eps[:], 64e-5), w=['eps'])
        P.op('act', lambda e: e.activation(out=glo[:], in_=glo[:], func=AF.Sigmoid), r=['glo'], w=['glo'])
        for cq in range(4):
            os_ = cq % 2
            rows = slice(cq * 128, (cq + 1) * 128)
            P.dma('sp', rr[:], K.rw_rv[0, rows, :], r=['rw_rv'], w=['rr'])
            P.dma('sp', vv[:], K.rw_rv[1, rows, :], r=['rw_rv'], w=['vv'])
            P.dma('sp', k0[:], K.rw_keff[0, rows, :], r=['rw_keff'], w=['k0'])
            P.dma('sp', k1[:], K.rw_keff[1, rows, :], r=['rw_keff'], w=['k1'])
            P.op('pool', lambda e: e.tensor_tensor(out=k0[:], in0=k0[:], in1=k1[:], op=ALU.add), r=['k0', 'k1'], w=['k0'])
            P.op('pool', lambda e: e.tensor_tensor(out=k0[:], in0=k0[:], in1=rr[:], op=ALU.mult), r=['k0', 'rr'], w=['k0'])
            for (a, n) in TILES:
                y = yacc[:, cq, a:a + n]
                P.op('act', lambda e: e.activation(out=ybf[:, :n], in_=y, func=AF.Copy), r=['yacc0', 'yacc1'], w=['ybf'])
                pb = P.ps()
                mm(P, K.psum[pb][:, :n], blk[:], ybf[:, :n], True, True, r=['blk', 'ybf'], w=['ps%d' % pb])
                P.op('dve', lambda e: e.scalar_tensor_tensor(out=yc[:, :n], in0=K.psum[pb][:, :n], scalar=-1.0 / 64, in1=y,
                                                             op0=ALU.mult, op1=ALU.add), r=['ps%d' % pb, 'yacc0', 'yacc1'],
                     w=['yc'])
                P.op('act', lambda e: e.activation(out=sq[:, :n], in_=yc[:, :n], func=AF.Square), r=['yc'], w=['sq'])
                pb = P.ps()
                mm(P, K.psum[pb][:, :n], blk[:], sq[:, :n], True, True, r=['blk', 'sq'], w=['ps%d' % pb])
                P.op('act', lambda e: e.activation(out=rs[:, :n], in_=K.psum[pb][:, :n], func=AF.Sqrt, scale=1.0 / 64,
                                                   bias=eps[:, 0:1]), r=['ps%d' % pb, 'eps'], w=['rs'])
                P.op('dve', lambda e: e.reciprocal(rs[:, :n], rs[:, :n]), r=['rs'], w=['rs'])
                P.op('dve', lambda e: e.tensor_tensor(out=yc[:, :n], in0=yc[:, :n], in1=rs[:, :n], op=ALU.mult),
                     r=['yc', 'rs'], w=['yc'])
                P.op('dve', lambda e: e.tensor_scalar(out=yc[:, :n], in0=yc[:, :n], scalar1=cols[:, cq, 3:4],
                                                      scalar2=cols[:, cq, 4:5], op0=ALU.mult, op1=ALU.add),
                     r=['yc', 'cols'], w=['yc'])
                P.op('act', lambda e: e.activation(out=sq[:, :n], in_=k0[:, a:a + n], func=AF.Identity,
                                                   scale=cols[:, cq, 2:3]), r=['k0', 'cols'], w=['sq'])
                pb = P.ps()
                mm(P, K.psum[pb][:, :n], blk[:], sq[:, :n], True, True, r=['blk', 'sq'], w=['ps%d' % pb])
                P.op('dve', lambda e: e.tensor_tensor(out=bon[:, :n], in0=K.psum[pb][:, :n], in1=vv[:, a:a + n],
                                                      op=ALU.mult), r=['ps%d' % pb, 'vv'], w=['bon'])
                P.op('dve', lambda e: e.tensor_tensor(out=yc[:, :n], in0=yc[:, :n], in1=bon[:, :n], op=ALU.add),
                     r=['yc', 'bon'], w=['yc'])
                pb = P.ps()
                mm(P, K.psum[pb][:, :n], gup[:, cq * 128:(cq + 1) * 128], glo[:, a:a + n], True, True, r=['gup', 'glo'],
                   w=['ps%d' % pb])
                P.op('dve', lambda e: e.tensor_tensor(out=ostg[os_][:, a:a + n], in0=K.psum[pb][:, :n], in1=yc[:, :n],
                                                      op=ALU.mult), r=['ps%d' % pb, 'yc'], w=['ostg%d' % os_])
            P.dma('sp', K.mixd[l][rows, :], ostg[os_][:], r=['ostg%d' % os_], w=['mix'])
        P.barrier()


def mixer_rwkv(K, l):
    i = l // 2
    with ExitStack() as es:
        yacc = K.mk(es)('yacc', [128, 4, NT])
        for d in range(2):
            rwkv_pre(K, i, d)
            rwkv_scan(K, d, yacc)
        rwkv_post(K, l, yacc)


_CACHE = {}


def kernel(**inputs):
    inp = {k: np.asarray(v) for k, v in inputs.items()}
    if 'nc' not in _CACHE:
        _CACHE['nc'] = build({})
    nc = _CACHE['nc']
    hm = host_mixer(inp)
    in_maps = []
    for b in range(8):
        m = host_common(inp, b)
        m.update(hm)
        in_maps.append(m)
    res = run_bass_kernel_spmd(nc, in_maps, core_ids=list(range(8)))
    out = np.stack([np.asarray(r["out"], dtype=np.float32) for r in res.results], axis=0)
    return out
```

```python
import math
from contextlib import ExitStack
import numpy as np
import ml_dtypes
import concourse.bass as bass
import concourse.mybir as mybir
from concourse.bass_utils import run_bass_kernel_spmd

F32 = mybir.dt.float32
BF16 = mybir.dt.bfloat16
AF = mybir.ActivationFunctionType
ALU = mybir.AluOpType
AX = mybir.AxisListType

D = 1024
NT = 2304
NC_ = 256
NL = 2048
DEPTH = 4
EPS = 1e-6
TILES = [(0, 512), (512, 512), (1024, 512), (1536, 512), (2048, 256)]


class Prog:
    KD = 12

    def __init__(self, nc, es):
        self.nc = nc
        self.eng = {'pe': nc.tensor, 'act': nc.scalar, 'dve': nc.vector, 'pool': nc.gpsimd, 'sp': nc.sync}
        self.csem = {e: es.enter_context(nc.semaphore('c_' + e)) for e in ['pe', 'act', 'dve', 'pool']}
        self.ccnt = {e: 0 for e in self.csem}
        self.dsem = {q: [es.enter_context(nc.semaphore('d_%s%d' % (q, i))) for i in range(self.KD)]
                     for q in ['sp', 'pool']}
        self.dcnt = {q: [0] * self.KD for q in self.dsem}
        self.dnext = {q: 0 for q in self.dsem}
        self.waited = {e: {} for e in self.eng}
        self.res = {}
        self.psn = 0

    def _wait(self, e, tok):
        sid, sem, val = tok
        if self.waited[e].get(sid, 0) >= val:
            return
        self.eng[e].wait_ge(sem, val)
        self.waited[e][sid] = val

    def _deps(self, e, r, w):
        toks = []
        for k in r:
            st = self.res.get(k)
            if st and st['w'] is not None:
                toks.append(st['w'])
            if st and k.startswith('ps'):
                for t in st['r'].values():
                    if t[0] != 'c_' + e:
                        toks.append(t)
        for k in w:
            st = self.res.get(k)
            if st:
                if st['w'] is not None and st['w'][0] != 'c_' + e:
                    toks.append(st['w'])
                for t in st['r'].values():
                    if t[0] != 'c_' + e:
                        toks.append(t)
        for t in toks:
            self._wait(e, t)

    def _rec(self, r, w, tok):
        for k in r:
            st = self.res.setdefault(k, {'w': None, 'r': {}})
            st['r'][tok[0]] = tok
        for k in w:
            self.res[k] = {'w': tok, 'r': {}}

    def op(self, e, fn, r=(), w=()):
        self._deps(e, r, w)
        inst = fn(self.eng[e])
        self.ccnt[e] += 1
        inst.then_inc(self.csem[e], 1)
        tok = ('c_' + e, self.csem[e], self.ccnt[e])
        self._rec(r, w, tok)
        return tok

    def dma(self, q, out, in_, r=(), w=(), **kw):
        k = self.dnext[q]
        self.dnext[q] = (k + 1) % self.KD
        sid = 'd_%s%d' % (q, k)
        if self.dcnt[q][k] > 0:
            self._wait(q, (sid, self.dsem[q][k], 16 * self.dcnt[q][k]))
        self._deps(q, r, w)
        inst = self.eng[q].dma_start(out=out, in_=in_, **kw)
        self.dcnt[q][k] += 1
        inst.then_inc(self.dsem[q][k], 16)
        tok = (sid, self.dsem[q][k], 16 * self.dcnt[q][k])
        self._rec(r, w, tok)
        return tok

    def barrier(self):
        toks = []
        for e in self.csem:
            if self.ccnt[e]:
                toks.append(('c_' + e, self.csem[e], self.ccnt[e]))
        for q in self.dsem:
            for k in range(self.KD):
                if self.dcnt[q][k]:
                    toks.append(('d_%s%d' % (q, k), self.dsem[q][k], 16 * self.dcnt[q][k]))
        for e in self.eng:
            for t in toks:
                self._wait(e, t)
        self.res = {}

    def ps(self):
        self.psn = (self.psn + 1) % 8
        return self.psn


class Ctx:
    pass


def mm(P, out, lhsT, rhs, start, stop, r, w):
    return P.op('pe', lambda e: e.matmul(out, lhsT, rhs, start=start, stop=stop), r=r, w=w)


def load_w_block(K, wdram, kc, c0, ncols, slot):
    P = K.P
    src = wdram.rearrange("(kc p) m -> p kc m", p=128)
    for k0 in range(0, kc, 8):
        kn = min(8, kc - k0)
        sl = K.wstn
        K.wstn = (K.wstn + 1) % 2
        P.dma('sp', K.wst[sl][:, :kn, :ncols], src[:, k0:k0 + kn, c0:c0 + ncols], w=['wst%d' % sl])
        eng = 'pool' if (K.wcast % 2 == 0) else 'act'
        K.wcast += 1
        if eng == 'pool':
            P.op('pool', lambda e: e.tensor_copy(K.wb[slot][:, k0:k0 + kn, :ncols], K.wst[sl][:, :kn, :ncols]),
                 r=['wst%d' % sl], w=['wb%d' % slot])
        else:
            P.op('act', lambda e: e.activation(out=K.wb[slot][:, k0:k0 + kn, :ncols], in_=K.wst[sl][:, :kn, :ncols],
                                               func=AF.Copy), r=['wst%d' % sl], w=['wb%d' % slot])


def stage_mods(K, l):
    P = K.P
    psb = P.ps()
    for blk in range(4):
        sl = blk % 2
        P.dma('sp', K.adst[sl][:], K.ada_w[l].rearrange("(kc p) m -> p kc m", p=128)[:, :, blk * 1536:(blk + 1) * 1536],
              w=['adst%d' % sl])
        for mc in range(12):
            m = blk * 12 + mc
            for k in range(8):
                mm(P, K.psum[psb][:, 2 * m:2 * m + 2], K.adst[sl][:, k, mc * 128:(mc + 1) * 128], K.sc[:, k, :],
                   k == 0, k == 7, r=['adst%d' % sl, 'sc'], w=['ps%d' % psb])
    mv = K.modv[l]
    P.op('dve', lambda e: e.tensor_tensor(out=mv[:].rearrange("p a b -> p (a b)"), in0=K.psum[psb][:, 0:96],
                                          in1=K.adab[:, l].rearrange("p a b -> p (a b)"), op=ALU.add),
         r=['ps%d' % psb], w=['modv%d' % l])
    for j, (sci, g) in enumerate([(1, K.g1), (4, K.g2)]):
        P.op('dve', lambda e: e.tensor_scalar(out=K.gs[l][:, j], in0=mv[:, sci * 8:(sci + 1) * 8, :], scalar1=1.0,
                                              scalar2=None, op0=ALU.add), r=['modv%d' % l], w=['gs%d' % l])
        P.op('dve', lambda e: e.tensor_tensor(out=K.gs[l][:, j], in0=K.gs[l][:, j], in1=g[:, l], op=ALU.mult),
             r=['gs%d' % l], w=['gs%d' % l])


def stage_norm(K, l, j, gs_ap=None, shift_ap=None, out_dt_bf=True):
    P = K.P
    for ti in range(9):
        t0 = ti * 256
        col = 1 if ti == 0 else 0
        P.op('act', lambda e: e.activation(out=K.sq[:], in_=K.X[:, :, t0:t0 + 256], func=AF.Square),
             r=['X'], w=['sq'])
        pb = P.ps()
        for k in range(8):
            mm(P, K.psum[pb][:, 0:256], K.ones[:], K.sq[:, k, :], k == 0, k == 7, r=['sq'], w=['ps%d' % pb])
        P.op('act', lambda e: e.activation(out=K.rstd[:], in_=K.psum[pb][:, 0:256], func=AF.Sqrt, scale=1.0 / D,
                                           bias=K.epsc[:, 0:1]), r=['ps%d' % pb], w=['rstd'])
        P.op('dve', lambda e: e.reciprocal(K.rstd[:], K.rstd[:]), r=['rstd'], w=['rstd'])
        for k in range(8):
            if gs_ap is None:
                g_ap = K.gs[l][:, j, k, col:col + 1]
                s_ap = K.modv[l][:, (3 * j) * 8 + k, col:col + 1]
            else:
                g_ap = gs_ap[:, k:k + 1]
                s_ap = None
            P.op('dve', lambda e: e.scalar_tensor_tensor(out=K.ntmp[:, k, :], in0=K.X[:, k, t0:t0 + 256], scalar=g_ap,
                                                         in1=K.rstd[:], op0=ALU.mult, op1=ALU.mult),
                 r=['X', 'rstd', 'gs%d' % l], w=['ntmp%d' % k])
            if s_ap is not None:
                P.op('act', lambda e: e.activation(out=K.Hn[:, k, t0:t0 + 256], in_=K.ntmp[:, k, :], func=AF.Identity,
                                                   bias=s_ap, scale=1.0),
                     r=['ntmp%d' % k, 'modv%d' % l], w=['Hn'])
            else:
                P.op('act', lambda e: e.activation(out=K.Yf[:, k, t0:t0 + 256], in_=K.ntmp[:, k, :], func=AF.Copy),
                     r=['ntmp%d' % k], w=['Yf'])


def stage_proj(K, wdram, fin, out_dram):
    P = K.P
    nb = (fin + 511) // 512
    for b in range(nb):
        c0 = b * 512
        ncols = min(512, fin - c0)
        slot = b % 2
        load_w_block(K, wdram, 8, c0, ncols, slot)
        for mc in range(ncols // 128):
            ss = K.stn
            K.stn = (K.stn + 1) % 2
            for (t0, tn) in TILES:
                pb = P.ps()
                for k in range(8):
                    mm(P, K.psum[pb][:, :tn], K.wb[slot][:, k, mc * 128:(mc + 1) * 128], K.Hn[:, k, t0:t0 + tn],
                       k == 0, k == 7, r=['wb%d' % slot, 'Hn'], w=['ps%d' % pb])
                P.op('act', lambda e: e.activation(out=K.stg[ss][:, t0:t0 + tn], in_=K.psum[pb][:, :tn], func=AF.Copy),
                     r=['ps%d' % pb], w=['stg%d' % ss])
            P.dma('sp', out_dram[c0 + mc * 128:c0 + (mc + 1) * 128, :], K.stg[ss][:], r=['stg%d' % ss],
                  w=['pfm'])


def resid_evac(K, l, gi, pb, m, t0, tn):
    P = K.P
    segs = []
    if t0 < NC_:
        segs.append((t0, NC_ - t0, 1))
        segs.append((NC_, t0 + tn - NC_, 0))
    else:
        segs.append((t0, tn, 0))
    for (a, n, col) in segs:
        gate = K.modv[l][:, gi * 8 + m, col:col + 1]
        P.op('dve', lambda e: e.scalar_tensor_tensor(out=K.X[:, m, a:a + n], in0=K.psum[pb][:, a - t0:a - t0 + n],
                                                     scalar=gate, in1=K.X[:, m, a:a + n], op0=ALU.mult, op1=ALU.add),
             r=['ps%d' % pb, 'modv%d' % l, 'X'], w=['X'])


def stage_outproj(K, l, wdram, mix_dram):
    P = K.P
    P.dma('sp', K.Hn[:], mix_dram.rearrange("(kc p) t -> p kc t", p=128), r=['mix'], w=['Hn'])
    for b in range(2):
        slot = b % 2
        load_w_block(K, wdram, 8, b * 512, 512, slot)
        for mc in range(4):
            m = b * 4 + mc
            for (t0, tn) in TILES:
                pb = P.ps()
                for k in range(8):
                    mm(P, K.psum[pb][:, :tn], K.wb[slot][:, k, mc * 128:(mc + 1) * 128], K.Hn[:, k, t0:t0 + tn],
                       k == 0, k == 7, r=['wb%d' % slot, 'Hn'], w=['ps%d' % pb])
                resid_evac(K, l, 2, pb, m, t0, tn)


def stage_mlp(K, l):
    P = K.P
    w1 = K.mlp_w1[l]
    w2 = K.mlp_w2[l]
    for b in range(8):
        slot = b % 2
        load_w_block(K, w1, 8, b * 512, 512, slot)
        for mc in range(4):
            ss = K.stn
            K.stn = (K.stn + 1) % 2
            for (t0, tn) in TILES:
                pb = P.ps()
                for k in range(8):
                    mm(P, K.psum[pb][:, :tn], K.wb[slot][:, k, mc * 128:(mc + 1) * 128], K.Hn[:, k, t0:t0 + tn],
                       k == 0, k == 7, r=['wb%d' % slot, 'Hn'], w=['ps%d' % pb])
                P.op('act', lambda e: e.activation(out=K.rl[:, :tn], in_=K.psum[pb][:, :tn], func=AF.Relu),
                     r=['ps%d' % pb], w=['rl'])
                P.op('dve', lambda e: e.tensor_tensor(out=K.stgb[ss][:, t0:t0 + tn], in0=K.rl[:, :tn], in1=K.rl[:, :tn],
                                                      op=ALU.mult), r=['rl'], w=['stgb%d' % ss])
            r0 = b * 512 + mc * 128
            P.dma('sp', K.hid[r0:r0 + 128, :], K.stgb[ss][:], r=['stgb%d' % ss], w=['hid'])


def stage_mlp2(K, l):
    P = K.P
    w2 = K.mlp_w2[l]
    hsrc = K.hid.rearrange("(kc p) t -> p kc t", p=128)
    for b in range(4):
        load_w_block_to(K, w2, 32, b * 256, 256, K.wb2, 'wb2')
        for ti, (t0, tn) in enumerate(TILES):
            hs = ti % 2
            for kq in range(4):
                P.dma('sp', K.hb[hs][:, kq * 8:(kq + 1) * 8, :tn], hsrc[:, kq * 8:(kq + 1) * 8, t0:t0 + tn], r=['hid'],
                      w=['hb%d' % hs])
            for mc in range(2):
                m = b * 2 + mc
                pb = P.ps()
                for k in range(32):
                    mm(P, K.psum[pb][:, :tn], K.wb2[:, k, mc * 128:(mc + 1) * 128], K.hb[hs][:, k, :tn],
                       k == 0, k == 31, r=['wb2', 'hb%d' % hs], w=['ps%d' % pb])
                resid_evac(K, l, 5, pb, m, t0, tn)


def load_w_block_to(K, wdram, kc, c0, ncols, dst, key):
    P = K.P
    src = wdram.rearrange("(kc p) m -> p kc m", p=128)
    for k0 in range(0, kc, 8):
        sl = K.wstn
        K.wstn = (K.wstn + 1) % 2
        P.dma('sp', K.wst[sl][:, :8, :ncols], src[:, k0:k0 + 8, c0:c0 + ncols], w=['wst%d' % sl])
        eng = 'pool' if (K.wcast % 2 == 0) else 'act'
        K.wcast += 1
        if eng == 'pool':
            P.op('pool', lambda e: e.tensor_copy(dst[:, k0:k0 + 8, :ncols], K.wst[sl][:, :8, :ncols]),
                 r=['wst%d' % sl], w=[key])
        else:
            P.op('act', lambda e: e.activation(out=dst[:, k0:k0 + 8, :ncols], in_=K.wst[sl][:, :8, :ncols],
                                               func=AF.Copy), r=['wst%d' % sl], w=[key])


def stage_final(K):
    P = K.P
    stage_norm(K, 0, 0, gs_ap=K.fg)
    for tc in range(16):
        t0 = NC_ + tc * 128
        ss = tc % 2
        for half in range(2):
            pb = P.ps()
            for kk in range(4):
                k = half * 4 + kk
                P.op('pe', lambda e: e.transpose(K.psum[pb][:, kk * 128:(kk + 1) * 128], K.Yf[:, k, t0:t0 + 128],
                                                 K.ident[:]), r=['Yf'], w=['ps%d' % pb])
            P.op('act' if half else 'dve',
                 (lambda e: e.activation(out=K.ot[ss][:, half * 512:(half + 1) * 512], in_=K.psum[pb][:], func=AF.Copy))
                 if half else
                 (lambda e: e.tensor_copy(K.ot[ss][:, half * 512:(half + 1) * 512], K.psum[pb][:])),
                 r=['ps%d' % pb], w=['ot%d' % ss])
        P.dma('sp', K.out[tc * 128:(tc + 1) * 128, :], K.ot[ss][:], r=['ot%d' % ss], w=['out'])


def build(cfg):
    nc = bass.Bass("TRN2", target_bir_lowering=False)
    K = Ctx()
    K.nc = nc
    K.cfg = cfg
    K.uid = 0

    def mk(es_):
        def f(name, shape, dt=F32):
            K.uid += 1
            return es_.enter_context(nc.sbuf_tensor(name + '_u%d' % K.uid, list(shape), dt))
        return f
    K.mk = mk

    def din(name, shape, dt=F32):
        return nc.dram_tensor(name, list(shape), dt, kind="ExternalInput").ap()

    def dscr(name, shape, dt=F32):
        kind = "ExternalOutput" if name in cfg.get('dbg', ()) else "Internal"
        return nc.dram_tensor(name, list(shape), dt, kind=kind).ap()

    if cfg.get('mixer_test') is not None:
        _din = din

        def din(name, shape, dt=F32):
            if name in ('ada_w', 'mlp_w1', 'mlp_w2', 'ev_w_in', 'ev_w_out', 'od_w_in', 'od_w_out', 'xin'):
                return None
            return _din(name, shape, dt)
    K.xin = din("xin", [8, 128, NT])
    K.cc = din("cc", [128, 8, 2])
    K.ada_w = din("ada_w", [DEPTH, D, 6 * D])
    adab_d = din("adab", [128, DEPTH, 48, 2])
    g1_d = din("g1", [128, DEPTH, 8, 2])
    g2_d = din("g2", [128, DEPTH, 8, 2])
    fg_d = din("fg", [128, 8])
    ident_d = din("ident", [128, 128])
    K.mlp_w1 = din("mlp_w1", [DEPTH, D, 4 * D])
    K.mlp_w2 = din("mlp_w2", [DEPTH, 4 * D, D])
    K.ev_w_in = din("ev_w_in", [2, D, 2048])
    K.ev_w_out = din("ev_w_out", [2, D, D])
    K.od_w_in = din("od_w_in", [2, D, 3328])
    K.od_w_out = din("od_w_out", [2, D, D])
    K.out = nc.dram_tensor("out", [NL, D], F32, kind="ExternalOutput").ap()
    K.hid = dscr("hid", [4 * D, NT], BF16)
    K.pfm = din("pfm", [3328, NT]) if cfg.get('mixer_test') is not None else dscr("pfm", [3328, NT], F32)
    declare_mixer_inputs(K, din)
    if cfg.get('mix_in'):
        K.mixd = [din("mix%d" % l, [D, NT], BF16) for l in range(DEPTH)]
    else:
        K.mixd = [dscr("mix", [D, NT], BF16)] * DEPTH
    if cfg.get('mixer_test') is not None:
        with ExitStack() as es:
            P = K.P = Prog(nc, es)
            K.psum = [es.enter_context(nc.psum_tensor("psb%d" % i, [128, 512], F32)) for i in range(8)]
            K.ones = es.enter_context(nc.sbuf_tensor("ones", [128, 128], BF16))
            K.ident = es.enter_context(nc.sbuf_tensor("ident_s", [128, 128], F32))
            P.op('dve', lambda e: e.memset(K.ones[:], 1.0), w=['ones'])
            P.dma('sp', K.ident[:], ident_d, w=['g'])
            P.barrier()
            run_mixer(K, cfg['mixer_test'], cfg.get('which', 'ab'))
            P.barrier()
        return nc
    K.xdbg = [dscr("xdbg%d" % l, [8, 128, NT]) for l in range(DEPTH)] if cfg.get('xdbg') else None

    with ExitStack() as es:
        P = K.P = Prog(nc, es)

        def sb(name, shape, dt=F32):
            return es.enter_context(nc.sbuf_tensor(name, list(shape), dt))
        K.psum = [es.enter_context(nc.psum_tensor("psb%d" % i, [128, 512], F32)) for i in range(8)]
        K.sc = sb("sc", [128, 8, 2])
        K.adab = sb("adab_s", [128, DEPTH, 48, 2])
        K.g1 = sb("g1_s", [128, DEPTH, 8, 2])
        K.g2 = sb("g2_s", [128, DEPTH, 8, 2])
        K.fg = sb("fg_s", [128, 8])
        K.ident = sb("ident_s", [128, 128])
        K.ones = sb("ones", [128, 128], BF16)
        K.epsc = sb("epsc", [128, 1])
        K.modv = [sb("modv%d" % l, [128, 48, 2]) for l in range(DEPTH)]
        K.gs = [sb("gs%d" % l, [128, 2, 8, 2]) for l in range(DEPTH)]
        K.rstd = sb("rstd", [128, 256])
        K.wstn = 0
        K.wcast = 0
        K.stn = 0
        P.dma('sp', K.sc[:], K.cc, w=['sc'])
        P.dma('sp', K.adab[:], adab_d, w=['adab'])
        P.dma('sp', K.g1[:], g1_d, w=['g'])
        P.dma('sp', K.g2[:], g2_d, w=['g'])
        P.dma('sp', K.fg[:], fg_d, w=['g'])
        P.dma('sp', K.ident[:], ident_d, w=['g'])
        P.op('dve', lambda e: e.memset(K.ones[:], 1.0), w=['ones'])
        P.op('dve', lambda e: e.memset(K.epsc[:], EPS), w=['ones'])
        P.op('act', lambda e: e.activation(out=K.sc[:], in_=K.sc[:], func=AF.Silu), r=['sc'], w=['sc'])
        P.barrier()
        with ExitStack() as es2:
            K.uid = 0
            K.adst = [es2.enter_context(nc.sbuf_tensor("adst%d" % i, [128, 8, 1536], F32)) for i in range(2)]
            for l in range(DEPTH):
                stage_mods(K, l)
            P.barrier()
        K.xres = nc.dram_tensor("xres", [8, 128, NT], F32).ap()

        def load_X(src):
            P.dma('sp', K.X[:, 0:4], src.rearrange("k p t -> p k t")[:, 0:4], r=['xres'], w=['X'])
            P.dma('sp', K.X[:, 4:8], src.rearrange("k p t -> p k t")[:, 4:8], r=['xres'], w=['X'])
        for l in range(cfg.get('nlayers', DEPTH)):
            i = l // 2
            with ExitStack() as es2:
                sb2 = K.mk(es2)
                K.X = sb2("X", [128, 8, NT])
                K.Hn = sb2("Hn", [128, 8, NT], BF16)
                K.sq = sb2("sq", [128, 8, 256], BF16)
                K.ntmp = sb2("ntmp", [128, 8, 256])
                K.wst = [sb2("wst%d" % j, [128, 8, 512]) for j in range(2)]
                K.wb = [sb2("wb%d" % j, [128, 8, 512], BF16) for j in range(2)]
                K.stg = [sb2("stg%d" % j, [128, NT]) for j in range(2)]
                load_X(K.xin if l == 0 else K.xres)
                stage_norm(K, l, 0)
                if not cfg.get('mix_in'):
                    if l % 2 == 0:
                        stage_proj(K, K.ev_w_in[i], 2048, K.pfm)
                    else:
                        stage_proj(K, K.od_w_in[i], 3328, K.pfm)
                P.barrier()
            if not cfg.get('mix_in'):
                run_mixer(K, l)
                P.barrier()
            with ExitStack() as es1:
                K.X = K.mk(es1)("X", [128, 8, NT])
                load_X(K.xin if l == 0 else K.xres)
                with ExitStack() as es2:
                    sb2 = K.mk(es2)
                    K.Hn = sb2("Hn", [128, 8, NT], BF16)
                    K.sq = sb2("sq", [128, 8, 256], BF16)
                    K.ntmp = sb2("ntmp", [128, 8, 256])
                    K.wst = [sb2("wst%d" % j, [128, 8, 512]) for j in range(2)]
                    K.wb = [sb2("wb%d" % j, [128, 8, 512], BF16) for j in range(2)]
                    K.stgb = [sb2("stgb%d" % j, [128, NT], BF16) for j in range(2)]
                    K.rl = sb2("rl", [128, 512])
                    stage_outproj(K, l, (K.ev_w_out if l % 2 == 0 else K.od_w_out)[i], K.mixd[l])
                    stage_norm(K, l, 1)
                    stage_mlp(K, l)
                    P.barrier()
                with ExitStack() as es2:
                    sb2 = K.mk(es2)
                    K.wst = [sb2("wst%d" % j, [128, 8, 512]) for j in range(2)]
                    K.wb2 = sb2("wb2", [128, 32, 256], BF16)
                    K.hb = [sb2("hb%d" % j, [128, 32, 512], BF16) for j in range(2)]
                    stage_mlp2(K, l)
                    P.barrier()
                P.dma('sp', K.xres.rearrange("k p t -> p k t"), K.X[:], r=['X'], w=['xres'])
                if K.xdbg is not None:
                    P.dma('sp', K.xdbg[l].rearrange("k p t -> p k t"), K.X[:], r=['X'], w=['xdbg'])
                P.barrier()
        with ExitStack() as es2:
            sb2 = K.mk(es2)
            K.X = sb2("X", [128, 8, NT])
            K.sq = sb2("sq", [128, 8, 256], BF16)
            K.ntmp = sb2("ntmp", [128, 8, 256])
            K.Yf = sb2("Yf", [128, 8, NT])
            K.ot = [sb2("ot%d" % j, [128, D]) for j in range(2)]
            load_X(K.xres)
            stage_final(K)
            P.barrier()
    return nc


def host_common(inp, b):
    f = np.float32
    x = np.concatenate([inp['ctx'][b], inp['x'][b]], axis=0)
    m = {}
    m['xin'] = np.ascontiguousarray(x.T.reshape(8, 128, NT)).astype(f)
    cc = np.stack([inp['c'][b], inp['c_ctx']], axis=-1)
    m['cc'] = np.ascontiguousarray(cc.reshape(8, 128, 2).transpose(1, 0, 2)).astype(f)
    m['ada_w'] = inp['ada_w']
    ab = inp['ada_b'].reshape(DEPTH, 48, 128).transpose(2, 0, 1)
    m['adab'] = np.ascontiguousarray(np.repeat(ab[..., None], 2, axis=-1)).astype(f)
    for nm, key in [('g1', 'norm1_g'), ('g2', 'norm2_g')]:
        g = inp[key].reshape(DEPTH, 8, 128).transpose(2, 0, 1)
        m[nm] = np.ascontiguousarray(np.repeat(g[..., None], 2, axis=-1)).astype(f)
    m['fg'] = np.ascontiguousarray(inp['final_g'].reshape(8, 128).T).astype(f)
    m['ident'] = np.eye(128, dtype=f)
    for k in ['mlp_w1', 'mlp_w2', 'ev_w_in', 'ev_w_out', 'od_w_in', 'od_w_out']:
        m[k] = inp[k]
    return m


RW_IN = 1792


def declare_mixer_inputs(K, din):
    K.rw_cw = din("rw_cw", [128, 2, 12, 3])
    K.rw_w0a0 = din("rw_w0a0", [128, 2, 2, 4, 2])
    K.rw_wup = din("rw_wup", [2, 2, 64, 512])
    K.rw_aup = din("rw_aup", [2, 2, 64, 512])
    K.rw_gup = din("rw_gup", [2, 128, 512])
    K.rw_cols = din("rw_cols", [128, 2, 4, 5])
    K.rw_blk = din("rw_blk", [128, 128], BF16)
    K.rw_mk = din("rw_mk", [128, 2, 128])
    K.rw_mk3 = din("rw_mk3", [128, 2, 64])
    K.rw_idb = din("rw_idb", [128, 64], BF16)
    if K.cfg.get('mixer_test') is not None or True:
        dt_ = lambda nm, shp, d=F32: K.nc.dram_tensor(nm, list(shp), d).ap()
        K.rw_QR = dt_("rw_QR", [512, 36, 128], BF16)
        K.rw_BK = dt_("rw_BK", [512, 36, 128], BF16)
        K.rw_V = dt_("rw_V", [512, NT], BF16)
        K.rw_gC = dt_("rw_gC", [512, 36])
        K.rw_keff = dt_("rw_keff", [2, 512, NT])
        K.rw_rv = dt_("rw_rv", [2, 512, NT])
    K.dft = {2048: (din("dftc", [16, 128, 16, 128], BF16), din("dfts", [16, 128, 16, 128], BF16)),
             256: (din("dcc", [2, 128, 2, 128], BF16), din("dcs", [2, 128, 2, 128], BF16))}
    K.hy_zT = {2048: din("hy_zT_l", [33, 2048]), 256: din("hy_zT_c", [33, 256])}
    K.hy_tcol = {2048: din("hy_tcol_l", [128, 16]), 256: din("hy_tcol_c", [128, 2])}
    K.hy_w1 = din("hy_w1", [2, 33, 64])
    K.hy_w2 = din("hy_w2", [2, 64, 64])
    K.hy_w3 = din("hy_w3", [2, 64, 2048])
    K.hy_cols = din("hy_cols", [64, 2, 4])
    K.hy_ld = din("hy_ld", [128, 2, 2048])
    K.hy_biasb = din("hy_biasb", [128, 2, 1024])
    K.hy_cw = din("hy_cw", [128, 2, 12, 4])
    K.altc = din("altc", [128, 1], BF16)
    K.altr = din("altr", [1, 128], BF16)
    K.identb = din("identb", [128, 128], BF16)
    K.s5_par = din("s5_par", [128, 2, 32, 3])
    K.s5_B = din("s5_B", [2, 32, 128, 2, 128])
    K.s5_C = din("s5_C", [2, 32, 128, 2, 128])
    K.s5_dg = din("s5_dg", [128, 2, 2, 4])
    K.s5_glu_w = din("s5_glu_w", [2, 512, 512])
    K.iot = din("iot", [128, 96])
    K.da_lamb = din("da_lamb", [128, 2, 256])
    K.da_g = din("da_g", [2, 128, 1])
    K.ropec = din("ropec", [128, NL])
    K.ropes = din("ropes", [128, NL])


def host_mixer(inp):
    f = np.float32
    m = {}
    cwr = inp['rw_conv_w']
    m['rw_cw'] = np.ascontiguousarray(cwr.reshape(2, 3, 12, 128).transpose(3, 0, 2, 1)).astype(f)
    wa = np.stack([inp['rw_w0'], inp['rw_a0']], axis=-1)
    m['rw_w0a0'] = np.ascontiguousarray(wa.reshape(2, 2, 4, 128, 2).transpose(3, 0, 1, 2, 4)).astype(f)
    m['rw_wup'] = inp['rw_w_up']
    m['rw_aup'] = inp['rw_a_up']
    m['rw_gup'] = inp['rw_g_up']
    cl = np.stack([inp['rw_k_k'], inp['rw_k_a'], inp['rw_r_k'].reshape(2, 512), inp['rw_ln_g'], inp['rw_ln_b']], axis=-1)
    m['rw_cols'] = np.ascontiguousarray(cl.reshape(2, 4, 128, 5).transpose(2, 0, 1, 3)).astype(f)
    pp = np.arange(128)
    m['rw_blk'] = (pp[:, None] // 64 == pp[None, :] // 64).astype(ml_dtypes.bfloat16)
    s_ = (pp % 64)[:, None]
    t_ = np.arange(64)[None, :]
    su = (s_ < t_).astype(f)
    iu = (s_ <= t_).astype(f)
    mk = np.zeros((128, 2, 128), f)
    mk[:, 0, 0:64] = -su
    mk[:, 0, 64:128] = iu
    mk[:, 1, 0:64] = su
    mk[:, 1, 64:128] = iu
    m['rw_mk'] = mk
    mk3 = np.zeros((128, 2, 64), f)
    mk3[:, 0] = -(t_ < s_).astype(f)
    mk3[:, 1] = (s_ == t_).astype(f)
    m['rw_mk3'] = mk3
    m['rw_idb'] = (s_ == t_).astype(ml_dtypes.bfloat16)
    bf = ml_dtypes.bfloat16
    for n, (nc_, ns_) in [(2048, ('dftc', 'dfts')), (256, ('dcc', 'dcs'))]:
        N = 2 * n
        a = np.arange(n, dtype=np.int64)
        ph = (np.outer(a, a) % N).astype(np.float64) * (2 * np.pi / N)
        nb = n // 128
        for nm, fn in [(nc_, np.cos), (ns_, np.sin)]:
            T = fn(ph).reshape(nb, 128, nb, 128)
            m[nm] = np.ascontiguousarray(T.transpose(2, 1, 0, 3)).astype(bf)
        t = np.linspace(0.0, 1.0, n, dtype=f)[:, None]
        w = (f(2.0 * math.pi) * np.arange(n, dtype=f)[:, None] / f(n)).astype(f)
        bands = np.linspace(1e-4, 15, 16, dtype=f)[None, :]
        z = np.concatenate([t, np.cos(bands * w), -np.sin(bands * w)], axis=-1).astype(f)
        sfx = 'l' if n == 2048 else 'c'
        m['hy_zT_' + sfx] = np.ascontiguousarray(z.T)
        m['hy_tcol_' + sfx] = np.ascontiguousarray(t[:, 0].reshape(nb, 128).T)
    m['hy_w1'] = inp['hy_f_w1']
    m['hy_w2'] = inp['hy_f_w2']
    m['hy_w3'] = inp['hy_f_w3']
    m['hy_cols'] = np.ascontiguousarray(np.stack([inp['hy_f_b1'], inp['hy_f_b2'], inp['hy_f_freq'][:, 0],
                                                  inp['hy_f_freq'][:, 1]], axis=-1).transpose(1, 0, 2)).astype(f)
    m['hy_ld'] = np.ascontiguousarray(np.broadcast_to(inp['hy_log_decay'].reshape(1, 2, 2048), (128, 2, 2048))).astype(f)
    m['hy_biasb'] = np.ascontiguousarray(np.broadcast_to(inp['hy_bias'].reshape(1, 2, 1024), (128, 2, 1024))).astype(f)
    cw = np.concatenate([inp['hy_conv_w'], inp['hy_conv_b'][:, None, :]], axis=1)
    m['hy_cw'] = np.ascontiguousarray(cw.reshape(2, 4, 12, 128).transpose(3, 0, 2, 1)).astype(f)
    alt = np.where(np.arange(128) % 2 == 0, 1.0, -1.0)
    m['altc'] = alt.reshape(128, 1).astype(bf)
    m['altr'] = alt.reshape(1, 128).astype(bf)
    m['identb'] = np.eye(128).astype(bf)
    par = np.zeros((128, 2, 32, 3), f)
    Bp = np.zeros((2, 32, 128, 2, 128), f)
    Cp = np.zeros((2, 32, 128, 2, 128), f)
    for i in range(2):
        for d in range(2):
            for gp in range(16):
                idx = d * 16 + gp
                for g2 in range(2):
                    g = 2 * gp + g2
                    st = slice(g2 * 64, g2 * 64 + 64)
                    par[st, i, idx, 0] = inp['s5_lam_re'][i, d, g]
                    par[st, i, idx, 1] = inp['s5_lam_im'][i, d, g]
                    par[st, i, idx, 2] = inp['s5_log_dt'][i, d, g]
                    ch = slice((gp % 4) * 32 + g2 * 16, (gp % 4) * 32 + g2 * 16 + 16)
                    Bp[i, idx, ch, 0, st] = inp['s5_b_re'][i, d, g].T
                    Bp[i, idx, ch, 1, st] = inp['s5_b_im'][i, d, g].T
                    Cp[i, idx, st, 0, ch] = inp['s5_c_re'][i, d, g].T
                    Cp[i, idx, st, 1, ch] = inp['s5_c_im'][i, d, g].T
    m['s5_par'] = par
    m['s5_B'] = Bp
    m['s5_C'] = Cp
    dg = np.zeros((128, 2, 2, 4), f)
    for i in range(2):
        dg[:, i, 0, :] = inp['s5_d'][i].reshape(4, 128).T
        dg[:, i, 1, :] = inp['s5_glu_b'][i].reshape(4, 128).T
    m['s5_dg'] = dg
    m['s5_glu_w'] = inp['s5_glu_w']
    m['iot'] = np.ascontiguousarray(np.broadcast_to(
        np.concatenate([np.arange(48), 48 * np.arange(48)]).astype(f)[None], (128, 96)))
    m['da_lamb'] = np.ascontiguousarray(np.broadcast_to(inp['da_lam'].reshape(1, 2, 256), (128, 2, 256))).astype(f)
    m['da_g'] = np.ascontiguousarray(inp['da_subln_g'].reshape(2, 128, 1)).astype(f)
    p = np.arange(128)
    d = p % 64
    a = d // 32
    fr = d % 16
    half = (d % 32) // 16
    t = np.arange(NL)
    row = (t // 64).astype(f)
    col = (t % 64).astype(f)
    inv = (f(10000.0) ** (-(np.arange(16, dtype=f)) / f(16))).astype(f)
    pos = np.where(a[:, None] == 0, row[None, :], col[None, :]).astype(f)
    ang = (pos * inv[fr][:, None]).astype(f)
    m['ropec'] = np.cos(ang).astype(f)
    m['ropes'] = (np.sin(ang) * np.where(half == 0, -1.0, 1.0)[:, None]).astype(f)
    return m


def run_mixer(K, l, which='ab'):
    if l % 2 == 0:
        if 'a' in which:
            mixer_s5(K, l)
            K.P.barrier()
        if 'b' in which:
            mixer_hyena(K, l)
            K.P.barrier()
    if l % 2 == 1:
        if 'a' in which:
            mixer_rwkv(K, l)
            K.P.barrier()
        if 'b' in which:
            mixer_attn(K, l)
            K.P.barrier()


def mixer_attn(K, l):
    i = l // 2
    P = K.P
    cnt = 0
    lam_init = 0.8 - 0.6 * math.exp(-0.3 * l)
    QR = RW_IN
    with ExitStack() as es:
        sb = K.mk(es)
        xf = [sb('xf%d' % j, [128, NT]) for j in range(2)]
        xsw = [sb('xsw%d' % j, [128, NL]) for j in range(2)]
        t1 = sb('t1', [128, NL])
        t2 = sb('t2', [128, NL])
        cosT = sb('cosT', [128, NL])
        sinT = sb('sinT', [128, NL])
        qb = sb('qb', [128, 4, NT], BF16)
        kb = sb('kb', [128, 4, NT], BF16)
        vtm = sb('vtm', [128, 18, 512], BF16)
        lamt = sb('lamt', [128, 256])
        pr = sb('pr', [128, 128])
        sv = sb('sv', [128, 2])
        ev = sb('ev', [128, 2])
        nlam = sb('nlam', [128, 1])
        gsub = sb('gsub', [128, 1])
        eps5 = sb('eps5', [128, 1])
        E = [sb('E%d' % j, [128, 512], BF16) for j in range(3)]
        rz = [sb('rz%d' % j, [128, 512]) for j in range(2)]
        tt = [sb('tt%d' % j, [128, 512]) for j in range(2)]
        osb = sb('osb', [128, 512])
        sqb = sb('sqb', [128, 512], BF16)
        rs = sb('rs', [128, 512])
        ostg = [sb('ostg%d' % j, [128, NT], BF16) for j in range(2)]
        P.dma('sp', cosT[:], K.ropec, w=['cosT'])
        P.dma('sp', sinT[:], K.ropes, w=['sinT'])
        P.dma('sp', lamt[:], K.da_lamb[:, i], w=['lamt'])
        P.dma('sp', gsub[:], K.da_g[i], w=['gsub'])
        P.op('dve', lambda e: e.memset(eps5[:], 1e-5), w=['eps5'])
        P.op('dve', lambda e: e.tensor_tensor(out=pr[:, 0:64], in0=lamt[:, 0:64], in1=lamt[:, 64:128], op=ALU.mult),
             r=['lamt'], w=['pr'])
        P.op('dve', lambda e: e.tensor_tensor(out=pr[:, 64:128], in0=lamt[:, 128:192], in1=lamt[:, 192:256], op=ALU.mult),
             r=['lamt'], w=['pr'])
        P.op('dve', lambda e: e.tensor_reduce(out=sv[:], in_=pr[:].rearrange("p (a b) -> p a b", a=2), axis=AX.X,
                                              op=ALU.add), r=['pr'], w=['sv'])
        P.op('act', lambda e: e.activation(out=ev[:], in_=sv[:], func=AF.Exp), r=['sv'], w=['ev'])
        P.op('dve', lambda e: e.tensor_tensor(out=nlam[:], in0=ev[:, 1:2], in1=ev[:, 0:1], op=ALU.subtract),
             r=['ev'], w=['nlam'])
        P.op('dve', lambda e: e.tensor_scalar(out=nlam[:], in0=nlam[:], scalar1=-lam_init, scalar2=None, op0=ALU.add),
             r=['nlam'], w=['nlam'])
        P.op('dve', lambda e: e.tensor_scalar(out=gsub[:], in0=gsub[:], scalar1=1.0 - lam_init, scalar2=None,
                                              op0=ALU.mult), r=['gsub'], w=['gsub'])
        n = 0
        for (dst, off, key) in [(qb, 0, 'qb'), (kb, 512, 'kb')]:
            for h in range(4):
                s_ = n % 2
                n += 1
                r0 = QR + off + h * 128
                P.dma('sp', xf[s_][:], K.pfm[r0:r0 + 128, :], r=['pfm'], w=['xf%d' % s_])
                for j in range(8):
                    P.dma('sp', xsw[s_][16 * j:16 * j + 16, :], K.pfm[r0 + 16 * (j ^ 1):r0 + 16 * (j ^ 1) + 16, NC_:NT],
                          r=['pfm'], w=['xsw%d' % s_])
                P.op('act', lambda e: e.activation(out=dst[:, h, 0:NC_], in_=xf[s_][:, 0:NC_], func=AF.Copy),
                     r=['xf%d' % s_], w=[key])
                P.op('dve', lambda e: e.tensor_tensor(out=t1[:], in0=xf[s_][:, NC_:NT], in1=cosT[:], op=ALU.mult),
                     r=['xf%d' % s_, 'cosT'], w=['t1'])
                P.op('pool', lambda e: e.tensor_tensor(out=t2[:], in0=xsw[s_][:], in1=sinT[:], op=ALU.mult),
                     r=['xsw%d' % s_, 'sinT'], w=['t2'])
                P.op('dve', lambda e: e.tensor_tensor(out=dst[:, h, NC_:NT], in0=t1[:], in1=t2[:], op=ALU.add),
                     r=['t1', 't2'], w=[key])
        for h in range(4):
            s_ = n % 2
            n += 1
            r0 = QR + 1024 + h * 128
            P.dma('sp', xf[s_][:], K.pfm[r0:r0 + 128, :], r=['pfm'], w=['xf%d' % s_])
            for c0 in range(0, 18, 4):
                cn = min(4, 18 - c0)
                pb = 4 + (c0 // 4) % 3
                for cc in range(cn):
                    c = c0 + cc
                    P.op('pe', lambda e: e.transpose(K.psum[pb][:, cc * 128:(cc + 1) * 128],
                                                     xf[s_][:, c * 128:(c + 1) * 128], K.ident[:]),
                         r=['xf%d' % s_], w=['ps%d' % pb])
                P.op('act', lambda e: e.activation(
                    out=vtm[:, c0:c0 + cn, h * 128:(h + 1) * 128],
                    in_=K.psum[pb][:, 0:cn * 128].rearrange("p (c d) -> p c d", c=cn), func=AF.Copy),
                    r=['ps%d' % pb], w=['vtm'])
        for h in range(4):
            os_ = h % 2
            qtiles = [(0, 256, [0, 1])] + [(256 + 512 * j, 512, list(range(18))) for j in range(4)]
            for (q0, qn, kch) in qtiles:
                items = [(m_, ci, c) for m_ in range(2) for ci, c in enumerate(kch)]

                def emit_qk(m_, ci, c):
                    nonlocal cnt
                    pbs = m_ * 64
                    sbk = 4 + cnt % 3
                    eb = cnt % 3
                    cnt += 1
                    mm(P, K.psum[sbk][:, :qn], kb[pbs:pbs + 64, h, c * 128:(c + 1) * 128],
                       qb[pbs:pbs + 64, h, q0:q0 + qn], True, True, r=['kb', 'qb'], w=['ps%d' % sbk])
                    P.op('act', lambda e: e.activation(out=E[eb][:, :qn], in_=K.psum[sbk][:, :qn], func=AF.Exp,
                                                       scale=0.125), r=['ps%d' % sbk], w=['E%d' % eb])
                    return eb

                def emit_pv(m_, ci, c, eb):
                    mm(P, K.psum[m_][:, :qn], vtm[:, c, h * 128:(h + 1) * 128], E[eb][:, :qn], ci == 0,
                       ci == len(kch) - 1, r=['vtm', 'E%d' % eb], w=['ps%d' % m_])
                    mm(P, K.psum[2 + m_][:, :qn], K.ones[:], E[eb][:, :qn], ci == 0, ci == len(kch) - 1,
                       r=['E%d' % eb], w=['ps%d' % (2 + m_)])
                prev = None
                for it in items:
                    eb = emit_qk(*it)
                    if prev is not None:
                        emit_pv(*prev)
                    prev = it + (eb,)
                emit_pv(*prev)
                for m_ in range(2):
                    P.op('act', lambda e: e.activation(out=rz[m_][:, :qn], in_=K.psum[2 + m_][:, :qn], func=AF.Ln),
                         r=['ps%d' % (2 + m_)], w=['rz%d' % m_])
                    P.op('act', lambda e: e.activation(out=rz[m_][:, :qn], in_=rz[m_][:, :qn], func=AF.Exp, scale=-1.0),
                         r=['rz%d' % m_], w=['rz%d' % m_])
                    P.op('dve', lambda e: e.tensor_tensor(out=tt[m_][:, :qn], in0=K.psum[m_][:, :qn], in1=rz[m_][:, :qn],
                                                          op=ALU.mult), r=['ps%d' % m_, 'rz%d' % m_], w=['tt%d' % m_])
                P.op('dve', lambda e: e.scalar_tensor_tensor(out=osb[:, :qn], in0=tt[1][:, :qn], scalar=nlam[:, 0:1],
                                                             in1=tt[0][:, :qn], op0=ALU.mult, op1=ALU.add),
                     r=['tt0', 'tt1', 'nlam'], w=['osb'])
                P.op('act', lambda e: e.activation(out=sqb[:, :qn], in_=osb[:, :qn], func=AF.Square), r=['osb'],
                     w=['sqb'])
                mm(P, K.psum[7][:, :qn], K.ones[:], sqb[:, :qn], True, True, r=['sqb'], w=['ps7'])
                P.op('act', lambda e: e.activation(out=rs[:, :qn], in_=K.psum[7][:, :qn], func=AF.Ln, scale=1.0 / 128,
                                                   bias=eps5[:, 0:1]), r=['ps7', 'eps5'], w=['rs'])
                P.op('act', lambda e: e.activation(out=rs[:, :qn], in_=rs[:, :qn], func=AF.Exp, scale=-0.5), r=['rs'],
                     w=['rs'])
                P.op('dve', lambda e: e.scalar_tensor_tensor(out=ostg[os_][:, q0:q0 + qn], in0=osb[:, :qn],
                                                             scalar=gsub[:, 0:1], in1=rs[:, :qn], op0=ALU.mult,
                                                             op1=ALU.mult), r=['osb', 'rs', 'gsub'], w=['ostg%d' % os_])
            P.dma('sp', K.mixd[l][512 + h * 128:512 + (h + 1) * 128, :], ostg[os_][:], r=['ostg%d' % os_], w=['mix'])


TS5 = [(0, 256), (256, 512), (768, 512), (1280, 512), (1792, 512)]
TWO_PI = 2.0 * math.pi


def wrap_sin(P, sb_t, dst, src, shift, key, n, np_=128, sin=True):
    tmp, msk = sb_t
    tmp = tmp[:np_]
    msk = msk[:np_]
    P.op('dve', lambda e: e.tensor_scalar(out=tmp[:, :n], in0=src, scalar1=shift, scalar2=None, op0=ALU.add),
         r=[key], w=[key + 't'])
    for (cmp, thr, add) in [(ALU.is_gt, math.pi, -TWO_PI), (ALU.is_lt, -math.pi, TWO_PI)] * 2:
        P.op('dve', lambda e: e.tensor_scalar(out=msk[:, :n], in0=tmp[:, :n], scalar1=thr, scalar2=None, op0=cmp),
             r=[key + 't'], w=[key + 'm'])
        P.op('dve', lambda e: e.scalar_tensor_tensor(out=tmp[:, :n], in0=msk[:, :n], scalar=add, in1=tmp[:, :n],
                                                     op0=ALU.mult, op1=ALU.add), r=[key + 'm', key + 't'],
             w=[key + 't'])
    if sin:
        P.op('act', lambda e: e.activation(out=dst, in_=tmp[:, :n], func=AF.Sin), r=[key + 't'], w=[key + 'o'])
    else:
        P.op('dve', lambda e: e.tensor_copy(dst, tmp[:, :n]), r=[key + 't', key], w=[key, key + 'o'])


def reduce_angle(P, tiles, ang, n, key, np_=128):
    kf, ki = tiles
    kf = kf[:np_]
    ki = ki[:np_]
    ang = ang[:np_]
    P.op('dve', lambda e: e.tensor_scalar(out=ki[:, :n], in0=ang[:, :n], scalar1=1.0 / TWO_PI, scalar2=None,
                                          op0=ALU.mult), r=[key], w=[key + 'ki'])
    P.op('dve', lambda e: e.tensor_copy(kf[:, :n], ki[:, :n]), r=[key + 'ki'], w=[key + 'kf'])
    P.op('dve', lambda e: e.scalar_tensor_tensor(out=ang[:, :n], in0=kf[:, :n], scalar=-TWO_PI, in1=ang[:, :n],
                                                 op0=ALU.mult, op1=ALU.add), r=[key + 'kf', key], w=[key])


def mixer_s5(K, l):
    i = l // 2
    P = K.P
    I32 = mybir.dt.int32
    with ExitStack() as es:
        sb = K.mk(es)
        par = sb('par', [128, 32, 3])
        c_ = {nm: sb('c_' + nm, [128, 32]) for nm in
              ['dt', 'lre', 'th', 'a', 'rho', 'cth', 'sth', 'lbr', 'lbi', 'den', 'nr', 'ni', 'fr', 'fi', 'nfi', 't1',
               't2']}
        wt = (sb('wtmp', [128, 96]), sb('wmsk', [128, 96]))
        rt = (sb('rkf', [128, 96]), sb('rki', [128, 96], I32))
        iot = sb('iot', [128, 96])
        ang = sb('ang', [128, 96])
        stc = sb('stc', [128, 96])
        sts = sb('sts', [128, 96])
        dg = sb('dg', [128, 2, 4])
        uf = sb('uf', [128, NT])
        ub = sb('ub', [128, NT], BF16)
        cosF2 = [sb('cosF%d' % j, [128, NT], BF16) for j in range(2)]
        sinF2 = [sb('sinF%d' % j, [128, NT], BF16) for j in range(2)]
        tq = [sb('tq%d' % j, [128, NT]) for j in range(3)]
        cst = sb('cst', [128, 2])
        zre2 = [sb('zre%d' % j, [128, NT]) for j in range(2)]
        zim2 = [sb('zim%d' % j, [128, NT]) for j in range(2)]
        wre = sb('wre', [128, NT], BF16)
        wim = sb('wim', [128, NT], BF16)
        prd = [sb('prd%d' % j, [128, NT], BF16) for j in range(4)]
        tm8 = [sb('tm%d' % j, [128, 512]) for j in range(8)]
        rhoT2 = [sb('rhoT%d' % j, [128, NL]) for j in range(2)]
        ygb = sb('ygb', [128, 4, NT], BF16)
        Bst2 = [sb('Bst%d' % j, [128, 2, 128]) for j in range(2)]
        Bb2 = [sb('Bb%d' % j, [128, 2, 128], BF16) for j in range(2)]
        Cst2 = [sb('Cst%d' % j, [128, 2, 128]) for j in range(2)]
        Cb2 = [sb('Cb%d' % j, [128, 3, 128], BF16) for j in range(2)]
        ct = sb('ct', [128, 128])
        gst = sb('gst', [128, 4, 512])
        gwb = sb('gwb', [128, 4, 512], BF16)
        sig = tm8[1]
        ostg = [sb('ostg%d' % j, [128, NT], BF16) for j in range(1)] * 2
        P.dma('sp', par[:], K.s5_par[:, i], w=['par'])
        P.dma('sp', iot[:], K.iot, w=['iot'])
        P.dma('sp', dg[:], K.s5_dg[:, i], w=['dg'])
        P.dma('sp', gst[:], K.s5_glu_w[i].rearrange("(kc p) m -> p kc m", p=128), w=['gst'])
        P.op('act', lambda e: e.activation(out=gwb[:], in_=gst[:], func=AF.Copy), r=['gst'], w=['gwb'])
        P.op('dve', lambda e: e.memset(cst[:, 0:1], 1.0), w=['cst'])
        P.op('dve', lambda e: e.memset(cst[:, 1:2], 2.0), w=['cst'])
        C = c_
        lamre, lamim, logdt = par[:, :, 0], par[:, :, 1], par[:, :, 2]

        def tsc(out, in0, s1, op0, s2=None, op1=None, r=(), w=()):
            if op1 is None:
                P.op('dve', lambda e: e.tensor_scalar(out=out, in0=in0, scalar1=s1, scalar2=None, op0=op0), r=r, w=w)
            else:
                P.op('dve', lambda e: e.tensor_scalar(out=out, in0=in0, scalar1=s1, scalar2=s2, op0=op0, op1=op1),
                     r=r, w=w)

        def ttn(out, in0, in1, op, r=(), w=(), eng='dve'):
            P.op(eng, lambda e: e.tensor_tensor(out=out, in0=in0, in1=in1, op=op), r=r, w=w)
        kc = ['cs']
        P.op('act', lambda e: e.activation(out=C['dt'][:], in_=logdt, func=AF.Exp), r=['par'], w=kc)
        tsc(C['lre'][:], lamre, -1e-4, ALU.min, r=['par'], w=kc)
        ttn(C['th'][:], lamim, C['dt'][:], ALU.mult, r=kc + ['par'], w=kc)
        ttn(C['a'][:], C['lre'][:], C['dt'][:], ALU.mult, r=kc, w=kc)
        P.op('act', lambda e: e.activation(out=C['rho'][:], in_=C['a'][:], func=AF.Exp), r=kc, w=kc)
        P.op('dve', lambda e: e.tensor_copy(ang[:, :32], C['th'][:]), r=kc, w=['ang'])
        reduce_angle(P, rt, ang, 32, 'ang')
        wrap_sin(P, wt, C['sth'][:], ang[:, :32], 0.0, 'ang', 32)
        wrap_sin(P, wt, C['cth'][:], ang[:, :32], math.pi / 2, 'ang', 32)
        kc2 = ['cs', 'ango']
        ttn(C['lbr'][:], C['rho'][:], C['cth'][:], ALU.mult, r=kc2, w=kc)
        ttn(C['lbi'][:], C['rho'][:], C['sth'][:], ALU.mult, r=kc2, w=kc)
        ttn(C['den'][:], C['lre'][:], C['lre'][:], ALU.mult, r=kc, w=kc)
        ttn(C['t1'][:], lamim, lamim, ALU.mult, r=kc + ['par'], w=kc)
        ttn(C['den'][:], C['den'][:], C['t1'][:], ALU.add, r=kc, w=kc)
        P.op('dve', lambda e: e.reciprocal(C['den'][:], C['den'][:]), r=kc, w=kc)
        tsc(C['t1'][:], C['lbr'][:], -1.0, ALU.add, r=kc, w=kc)
        ttn(C['nr'][:], C['t1'][:], C['lre'][:], ALU.mult, r=kc, w=kc)
        ttn(C['t2'][:], C['lbi'][:], lamim, ALU.mult, r=kc + ['par'], w=kc)
        ttn(C['nr'][:], C['nr'][:], C['t2'][:], ALU.add, r=kc, w=kc)
        ttn(C['ni'][:], C['lbi'][:], C['lre'][:], ALU.mult, r=kc, w=kc)
        ttn(C['t2'][:], C['t1'][:], lamim, ALU.mult, r=kc + ['par'], w=kc)
        ttn(C['ni'][:], C['ni'][:], C['t2'][:], ALU.subtract, r=kc, w=kc)
        ttn(C['fr'][:], C['nr'][:], C['den'][:], ALU.mult, r=kc, w=kc)
        ttn(C['fi'][:], C['ni'][:], C['den'][:], ALU.mult, r=kc, w=kc)
        tsc(C['nfi'][:], C['fi'][:], -1.0, ALU.mult, r=kc, w=kc)

        ITS = [(cq, d, g4) for cq in range(4) for d in range(2) for g4 in range(4)]

        def tabv(T_, a, n):
            return T_[:, a:a + n]

        def gen(k_):
            cq, d, g4 = ITS[k_]
            gp = cq * 4 + g4
            idx = d * 16 + gp
            par = k_ % 2
            sp_ = str(par)
            cosF, sinF, zre, zim = cosF2[par], sinF2[par], zre2[par], zim2[par]
            Bst, Bb, Cst, Cb = Bst2[par], Bb2[par], Cst2[par], Cb2[par]
            rhoT = rhoT2[par]
            first = (d == 0 and g4 == 0)
            last = (d == 1 and g4 == 3)
            tm = tm8[0:4]
            P.dma('sp', Bst[:], K.s5_B[i, idx], w=['Bst' + sp_])
            P.op('act', lambda e: e.activation(out=Bb[:], in_=Bst[:], func=AF.Copy), r=['Bst' + sp_], w=['Bb' + sp_])
            P.dma('sp', Cst[:], K.s5_C[i, idx], w=['Cst' + sp_])
            fr, fi, nfi = C['fr'][:, idx:idx + 1], C['fi'][:, idx:idx + 1], C['nfi'][:, idx:idx + 1]
            tsc(ct[:], Cst[:, 1, :], fi, ALU.mult, r=['Cst' + sp_, 'cs'], w=['ct'])
            P.op('dve', lambda e: e.scalar_tensor_tensor(out=Cb[:, 0, :], in0=Cst[:, 0, :], scalar=fr, in1=ct[:],
                                                         op0=ALU.mult, op1=ALU.subtract),
                 r=['Cst' + sp_, 'ct', 'cs'], w=['Cb' + sp_])
            tsc(ct[:], Cst[:, 1, :], fr, ALU.mult, r=['Cst' + sp_, 'cs'], w=['ct'])
            P.op('dve', lambda e: e.scalar_tensor_tensor(out=Cb[:, 1, :], in0=Cst[:, 0, :], scalar=nfi, in1=ct[:],
                                                         op0=ALU.mult, op1=ALU.subtract),
                 r=['Cst' + sp_, 'ct', 'cs'], w=['Cb' + sp_])
            P.op('act', lambda e: e.activation(out=Cb[:, 2, :], in_=Cb[:, 0, :], func=AF.Copy, scale=-1.0),
                 r=['Cb' + sp_], w=['Cb' + sp_])
            tsc(ang[:], iot[:], C['th'][:, idx:idx + 1], ALU.mult, r=['iot', 'cs'], w=['ang'])
            reduce_angle(P, rt, ang, 96, 'ang')
            wrap_sin(P, wt, ang[:], ang[:], 0.0, 'ang', 96, sin=False)
            bA = lambda t_: t_[:, 48:96].unsqueeze(2).to_broadcast([128, 48, 48])
            bB = lambda t_: t_[:, 0:48].unsqueeze(1).to_broadcast([128, 48, 48])
            v3 = lambda t_: t_[:].rearrange("p (a b) -> p a b", a=48)
            ttn(v3(tq[0]), bA(ang), bB(ang), ALU.add, r=['ango'], w=['tq0'], eng='pool')

            def tseg(T_):
                if d == 0:
                    return [(T_[:, 0:NT], slice(0, NT))]
                return [(T_[:, NC_ - 1::-1], slice(0, NC_)), (T_[:, NT - 1:NC_ - 1:-1], slice(NC_, NT))]
            P.op('act', lambda e: e.activation(out=tq[1][:], in_=tq[0][:], func=AF.Sin, scale=0.5), r=['tq0'], w=['tq1'])
            P.op('act', lambda e: e.activation(out=tq[2][:], in_=tq[0][:], func=AF.Sin, scale=0.25), r=['tq0'], w=['tq2'])
            P.op('act', lambda e: e.activation(out=tq[0][:], in_=tq[1][:], func=AF.Square), r=['tq1'], w=['tq0'])
            for (o_, sl_) in tseg(cosF):
                P.op('act', lambda e: e.activation(out=o_, in_=tq[0][:, sl_], func=AF.Identity, scale=-2.0,
                                                   bias=cst[:, 0:1]), r=['tq0', 'cst'], w=['cosF' + sp_])
            P.op('act', lambda e: e.activation(out=tq[0][:], in_=tq[2][:], func=AF.Square), r=['tq2', 'cosF' + sp_],
                 w=['tq0'])
            P.op('act', lambda e: e.activation(out=tq[2][:], in_=tq[0][:], func=AF.Identity, scale=-4.0,
                                               bias=cst[:, 1:2]), r=['tq0', 'cst'], w=['tq2'])
            for (o_, sl_) in tseg(sinF):
                ttn(o_, tq[1][:, sl_], tq[2][:, sl_], ALU.mult, r=['tq1', 'tq2'], w=['sinF' + sp_], eng='pool')
            P.op('act', lambda e: e.activation(out=rhoT[:], in_=tq[1][:, 0:NL], func=AF.Identity, scale=0.0,
                                               bias=C['rho'][:, idx:idx + 1]), r=['tq1', 'cs'], w=['rhoT' + sp_])

        def use1(k_):
            cq, d, g4 = ITS[k_]
            gp = cq * 4 + g4
            idx = d * 16 + gp
            par = k_ % 2
            sp_ = str(par)
            cosF, sinF, zre, zim = cosF2[par], sinF2[par], zre2[par], zim2[par]
            Bst, Bb, Cst, Cb = Bst2[par], Bb2[par], Cst2[par], Cb2[par]
            rhoT = rhoT2[par]
            first = (d == 0 and g4 == 0)
            last = (d == 1 and g4 == 3)
            tm = tm8[0:4]
            for ti, (a, n) in enumerate(TS5):
                tm = tm8[4 * (ti % 2):4 * (ti % 2) + 4]
                tk = 4 * (ti % 2)
                mm(P, K.psum[5][:, :n], Bb[:, 0, :], ub[:, a:a + n], True, True, r=['Bb' + sp_, 'ub'], w=['ps5'])
                mm(P, K.psum[6][:, :n], Bb[:, 1, :], ub[:, a:a + n], True, True, r=['Bb' + sp_, 'ub'], w=['ps6'])
                cv, sv_ = tabv(cosF, a, n), tabv(sinF, a, n)
                ttn(tm[0][:, :n], K.psum[5][:, :n], cv, ALU.mult, r=['ps5', 'cosF' + sp_], w=['tm%d' % (tk + 0)])
                ttn(tm[1][:, :n], K.psum[6][:, :n], sv_, ALU.mult, r=['ps6', 'sinF' + sp_], w=['tm%d' % (tk + 1)])
                ttn(zre[:, a:a + n], tm[0][:, :n], tm[1][:, :n], ALU.add, r=['tm%d' % (tk + 0), 'tm%d' % (tk + 1)], w=['zre' + sp_], eng='pool')
                ttn(tm[2][:, :n], K.psum[6][:, :n], cv, ALU.mult, r=['ps6', 'cosF' + sp_], w=['tm%d' % (tk + 2)])
                ttn(tm[3][:, :n], K.psum[5][:, :n], sv_, ALU.mult, r=['ps5', 'sinF' + sp_], w=['tm%d' % (tk + 3)])
                ttn(zim[:, a:a + n], tm[2][:, :n], tm[3][:, :n], ALU.subtract, r=['tm%d' % (tk + 2), 'tm%d' % (tk + 3)], w=['zim' + sp_],
                    eng='pool')

        def use2(k_):
            cq, d, g4 = ITS[k_]
            gp = cq * 4 + g4
            idx = d * 16 + gp
            par = k_ % 2
            sp_ = str(par)
            cosF, sinF, zre, zim = cosF2[par], sinF2[par], zre2[par], zim2[par]
            Bst, Bb, Cst, Cb = Bst2[par], Bb2[par], Cst2[par], Cb2[par]
            rhoT = rhoT2[par]
            first = (d == 0 and g4 == 0)
            last = (d == 1 and g4 == 3)
            tm = tm8[0:4]
            for (zz, ww, kz, kw) in [(zre, wre, 'zre' + sp_, 'wre'), (zim, wim, 'zim' + sp_, 'wim')]:
                if d == 0:
                    segs = [(zz[:, 0:NC_], ww[:, 0:NC_], rhoT[:, 0:NC_], 0.0),
                            (zz[:, NC_:NT], ww[:, NC_:NT], rhoT[:, 0:NL], ww[:, NC_ - 1:NC_])]
                else:
                    segs = [(zz[:, NC_ - 1::-1], ww[:, NC_ - 1::-1], rhoT[:, 0:NC_], 0.0),
                            (zz[:, NT - 1:NC_ - 1:-1], ww[:, NT - 1:NC_ - 1:-1], rhoT[:, 0:NL], ww[:, 0:1])]
                for (zi, wo, rh, init) in segs:
                    P.op('dve', lambda e: e.tensor_tensor_scan(wo, rh, zi, init, ALU.mult, ALU.add),
                         r=[kz, 'rhoT' + sp_, kw], w=[kw])
            ttn(prd[0][:], wre[:], cosF[:], ALU.mult, r=['wre', 'cosF' + sp_], w=['prd0'])
            ttn(prd[1][:], wim[:], sinF[:], ALU.mult, r=['wim', 'sinF' + sp_], w=['prd1'], eng='pool')
            ttn(prd[2][:], wre[:], sinF[:], ALU.mult, r=['wre', 'sinF' + sp_], w=['prd2'])
            ttn(prd[3][:], wim[:], cosF[:], ALU.mult, r=['wim', 'cosF' + sp_], w=['prd3'], eng='pool')
            for ti, (a, n) in enumerate(TS5):
                for j_, cs_i in enumerate([0, 2, 1, 1]):
                    mm(P, K.psum[ti][:, :n], Cb[:, cs_i, :], prd[j_][:, a:a + n], first and j_ == 0,
                       last and j_ == 3, r=['Cb' + sp_, 'prd%d' % j_], w=['ps%d' % ti])

        def epilogue(cq):
            tm = tm8[0:4]
            for ti, (a, n) in enumerate(TS5):
                P.op('dve', lambda e: e.scalar_tensor_tensor(out=tm[0][:, :n], in0=uf[:, a:a + n], scalar=dg[:, 0, cq:cq + 1],
                                                             in1=K.psum[ti][:, :n], op0=ALU.mult, op1=ALU.add),
                     r=['uf', 'dg', 'ps%d' % ti], w=['tm0'])
                P.op('act', lambda e: e.activation(out=ygb[:, cq, a:a + n], in_=tm[0][:, :n], func=AF.Gelu),
                     r=['tm0'], w=['ygb'])

        gen(0)
        for k_ in range(32):
            cq, d, g4 = ITS[k_]
            if d == 0 and g4 == 0:
                P.dma('sp', uf[:], K.pfm[cq * 128:(cq + 1) * 128, :], r=['pfm'], w=['uf'])
                P.op('act', lambda e: e.activation(out=ub[:], in_=uf[:], func=AF.Copy), r=['uf'], w=['ub'])
            use1(k_)
            if k_ + 1 < 32:
                gen(k_ + 1)
            use2(k_)
            if d == 1 and g4 == 3:
                epilogue(cq)
        cnt = 0
        for mc in range(4):
            os_ = 0
            for (a, n) in TILES:
                pb = 5 + cnt % 3
                cnt += 1
                for k in range(4):
                    mm(P, K.psum[pb][:, :n], gwb[:, k, mc * 128:(mc + 1) * 128], ygb[:, k, a:a + n], k == 0, k == 3,
                       r=['gwb', 'ygb'], w=['ps%d' % pb])
                P.op('act', lambda e: e.activation(out=sig[:, :n], in_=K.psum[pb][:, :n], func=AF.Sigmoid,
                                                   bias=dg[:, 1, mc:mc + 1], scale=1.0), r=['ps%d' % pb, 'dg'], w=['sig'])
                ttn(ostg[os_][:, a:a + n], ygb[:, mc, a:a + n], sig[:, :n], ALU.mult, r=['ygb', 'sig'],
                    w=['ostg%d' % os_])
            P.dma('sp', K.mixd[l][mc * 128:(mc + 1) * 128, :], ostg[os_][:], r=['ostg%d' % os_], w=['mix'])


def hyena_filters(K, i, n, Kr, Ki, Kn):
    P = K.P
    I32 = mybir.dt.int32
    nb = n // 128
    dc, ds = K.dft[n]
    with ExitStack() as es0:
      h2 = K.mk(es0)('h2', [64, n])
      with ExitStack() as es:
        sb = K.mk(es)
        zT = sb('zT', [33, n])
        w1 = sb('w1', [33, 64])
        w2 = sb('w2', [64, 64])
        cols = sb('cols', [64, 4])
        h1 = sb('h1', [64, n])
        arg = sb('arg', [128, 512])
        wt = (sb('wtmp', [128, 512]), sb('wmsk', [128, 512]))
        rt = (sb('rkf', [128, 512]), sb('rki', [128, 512], I32))
        P.dma('sp', zT[:], K.hy_zT[n], w=['zT'])
        P.dma('sp', w1[:], K.hy_w1[i], w=['w1'])
        P.dma('sp', w2[:], K.hy_w2[i], w=['w2'])
        P.dma('sp', cols[:], K.hy_cols[:, i], w=['cols'])
        tl = [(a, min(512, n - a)) for a in range(0, n, 512)]
        for (src, wmat, dst, bcol, fcol, kin, kout) in [(zT, w1, h1, 0, 2, 'zT', 'h1'), (h1, w2, h2, 1, 3, 'h1', 'h2')]:
            for (a, tn) in tl:
                pb = P.ps()
                mm(P, K.psum[pb][:64, :tn], wmat[:], src[:, a:a + tn], True, True, r=[kin, 'w1', 'w2'], w=['ps%d' % pb])
                P.op('dve', lambda e: e.tensor_scalar(out=arg[:64, :tn], in0=K.psum[pb][:64, :tn],
                                                      scalar1=cols[:, bcol:bcol + 1], scalar2=cols[:, fcol:fcol + 1],
                                                      op0=ALU.add, op1=ALU.mult), r=['ps%d' % pb, 'cols'], w=['ang'])
                reduce_angle(P, rt, arg, tn, 'ang', 64)
                wrap_sin(P, wt, dst[:, a:a + tn], arg[:64, :tn], 0.0, 'ang', tn, 64)
                P.op('dve', lambda e: e.tensor_copy(dst[:, a:a + tn], dst[:, a:a + tn]), r=['ango'], w=[kout])
        P.barrier()
      with ExitStack() as es:
        sb = K.mk(es)
        w3 = sb('w3', [64, 2048])
        rate = sb('rate', [128, 2048])
        win = sb('win', [128, 2048])
        ntc = sb('ntc', [128, nb])
        hw = sb('hw', [128, 2048])
        hs = sb('hs', [128, nb, 1024], BF16)
        hd = sb('hd', [128, nb, 1024], BF16)
        tc = [sb('tc%d' % j, [128, nb, 128], BF16) for j in range(1)] * 2
        ts = [sb('ts%d' % j, [128, nb, 128], BF16) for j in range(1)] * 2
        biasb = sb('biasb', [128, 1024])
        altc = sb('altc', [128, 1], BF16)
        kt = sb('kt', [128, 512])
        P.dma('sp', w3[:], K.hy_w3[i], w=['w3'])
        P.dma('sp', rate[:], K.hy_ld[:, i], w=['rate'])
        P.dma('sp', ntc[:], K.hy_tcol[n], w=['ntc'])
        P.dma('sp', biasb[:], K.hy_biasb[:, i], w=['biasb'])
        P.dma('sp', altc[:], K.altc, w=['altc'])
        P.op('act', lambda e: e.activation(out=rate[:], in_=rate[:], func=AF.Exp), r=['rate'], w=['rate'])
        P.op('dve', lambda e: e.tensor_scalar(out=ntc[:], in0=ntc[:], scalar1=-1.0, scalar2=None, op0=ALU.mult),
             r=['ntc'], w=['ntc'])
        for c in range(nb):
            P.op('act', lambda e: e.activation(out=win[:], in_=rate[:], func=AF.Exp, scale=ntc[:, c:c + 1]),
                 r=['rate', 'ntc'], w=['win'])
            for q in range(4):
                pb = P.ps()
                mm(P, K.psum[pb][:], h2[:, c * 128:(c + 1) * 128], w3[:, q * 512:(q + 1) * 512], True, True,
                   r=['h2', 'w3'], w=['ps%d' % pb])
                P.op('dve', lambda e: e.tensor_tensor(out=hw[:, q * 512:(q + 1) * 512], in0=K.psum[pb][:],
                                                      in1=win[:, q * 512:(q + 1) * 512], op=ALU.mult),
                     r=['ps%d' % pb, 'win'], w=['hw'])
            if c == 0:
                P.op('dve', lambda e: e.memset(hw[0:1, 1024:2048], 0.0), r=['hw'], w=['hw'])
            P.op('dve', lambda e: e.tensor_tensor(out=hs[:, c, :], in0=hw[:, 0:1024], in1=hw[:, 1024:2048], op=ALU.add),
                 r=['hw'], w=['hs'])
            P.op('pool', lambda e: e.tensor_tensor(out=hd[:, c, :], in0=hw[:, 1024:2048], in1=hw[:, 0:1024],
                                                   op=ALU.subtract), r=['hw'], w=['hd'])
        sc = 1.0 / n
        for fc in range(nb):
            s_ = 0
            P.dma('sp', tc[s_][:], dc[fc], w=['tc%d' % s_])
            P.dma('sp', ts[s_][:], ds[fc], w=['ts%d' % s_])
            for q in range(2):
                pb = P.ps()
                for jc in range(nb):
                    mm(P, K.psum[pb][:], tc[s_][:, jc, :], hs[:, jc, q * 512:(q + 1) * 512], jc == 0, jc == nb - 1,
                       r=['tc%d' % s_, 'hs'], w=['ps%d' % pb])
                P.op('dve', lambda e: e.tensor_tensor(out=kt[:], in0=K.psum[pb][:], in1=biasb[:, q * 512:(q + 1) * 512],
                                                      op=ALU.add), r=['ps%d' % pb, 'biasb'], w=['kt'])
                if fc == 0:
                    P.op('dve', lambda e: e.tensor_scalar(out=kt[0:1, :], in0=kt[0:1, :], scalar1=0.5, scalar2=None,
                                                          op0=ALU.mult), r=['kt'], w=['kt'])
                P.op('act', lambda e: e.activation(out=Kr[:, fc, q * 512:(q + 1) * 512], in_=kt[:], func=AF.Copy,
                                                   scale=sc), r=['kt'], w=['Kr'])
                pb = P.ps()
                for jc in range(nb):
                    mm(P, K.psum[pb][:], ts[s_][:, jc, :], hd[:, jc, q * 512:(q + 1) * 512], jc == 0, jc == nb - 1,
                       r=['ts%d' % s_, 'hd'], w=['ps%d' % pb])
                P.op('act', lambda e: e.activation(out=Ki[:, fc, q * 512:(q + 1) * 512], in_=K.psum[pb][:], func=AF.Copy,
                                                   scale=sc), r=['ps%d' % pb], w=['Ki'])
        for q in range(2):
            pb = P.ps()
            for jc in range(nb):
                mm(P, K.psum[pb][0:1, :], altc[:, 0:1], hs[:, jc, q * 512:(q + 1) * 512], jc == 0, jc == nb - 1,
                   r=['altc', 'hs'], w=['ps%d' % pb])
            P.op('dve', lambda e: e.tensor_tensor(out=kt[0:1, :], in0=K.psum[pb][0:1, :],
                                                  in1=biasb[0:1, q * 512:(q + 1) * 512], op=ALU.add),
                 r=['ps%d' % pb, 'biasb'], w=['kt'])
            P.op('act', lambda e: e.activation(out=Kn[0:1, q * 512:(q + 1) * 512], in_=kt[0:1, :], func=AF.Copy,
                                               scale=0.5 * sc), r=['kt'], w=['Kn'])
        P.barrier()


def hyena_conv(K, n, stm, Kr, Ki, Kn, emit_out):
    P = K.P
    nb = n // 128
    dc, ds = K.dft[n]
    with ExitStack() as es:
        sb = K.mk(es)
        Yr = sb('Yr', [128, nb, 512], BF16)
        Yi = sb('Yi', [128, nb, 512], BF16)
        Yn = sb('Yn', [1, 512], BF16)
        z1 = stm[0]
        tc = [sb('tc%d' % j, [128, nb, 128], BF16) for j in range(1)] * 2
        ts = [sb('ts%d' % j, [128, nb, 128], BF16) for j in range(1)] * 2
        tm = [sb('tm%d' % j, [128, 512]) for j in range(4)]
        altc = sb('altc', [128, 1], BF16)
        altr = sb('altr', [1, 128], BF16)
        zo = sb('zo', [128, 512], BF16)
        P.dma('sp', altc[:], K.altc, w=['altc'])
        P.dma('sp', altr[:], K.altr, w=['altr'])
        for o in range(2):
            u = stm[0] if o == 0 else z1
            ku = 'stm0'
            for fc in range(nb):
                s_ = 0
                P.dma('sp', tc[s_][:], dc[fc], w=['tc%d' % s_])
                P.dma('sp', ts[s_][:], ds[fc], w=['ts%d' % s_])
                pa = P.ps()
                for jc in range(nb):
                    mm(P, K.psum[pa][:], tc[s_][:, jc, :], u[:, jc, :], jc == 0, jc == nb - 1, r=['tc%d' % s_, ku],
                       w=['ps%d' % pa])
                pbb = P.ps()
                for jc in range(nb):
                    mm(P, K.psum[pbb][:], ts[s_][:, jc, :], u[:, jc, :], jc == 0, jc == nb - 1, r=['ts%d' % s_, ku],
                       w=['ps%d' % pbb])
                kr = Kr[:, fc, o * 512:(o + 1) * 512]
                ki = Ki[:, fc, o * 512:(o + 1) * 512]
                xr, xs_ = K.psum[pa][:], K.psum[pbb][:]
                rk = ['ps%d' % pa, 'ps%d' % pbb, 'Kr', 'Ki']
                P.op('dve', lambda e: e.tensor_tensor(out=tm[0][:], in0=xr, in1=kr, op=ALU.mult), r=rk, w=['tm0'])
                P.op('dve', lambda e: e.tensor_tensor(out=tm[1][:], in0=xs_, in1=ki, op=ALU.mult), r=rk, w=['tm1'])
                P.op('pool', lambda e: e.tensor_tensor(out=Yr[:, fc, :], in0=tm[0][:], in1=tm[1][:], op=ALU.add),
                     r=['tm0', 'tm1'], w=['Yr'])
                P.op('dve', lambda e: e.tensor_tensor(out=tm[2][:], in0=xs_, in1=kr, op=ALU.mult), r=rk, w=['tm2'])
                P.op('dve', lambda e: e.tensor_tensor(out=tm[3][:], in0=xr, in1=ki, op=ALU.mult), r=rk, w=['tm3'])
                P.op('pool', lambda e: e.tensor_tensor(out=Yi[:, fc, :], in0=tm[2][:], in1=tm[3][:], op=ALU.subtract),
                     r=['tm2', 'tm3'], w=['Yi'])
            pa = P.ps()
            for jc in range(nb):
                mm(P, K.psum[pa][0:1, :], altc[:, 0:1], u[:, jc, :], jc == 0, jc == nb - 1, r=['altc', ku], w=['ps%d' % pa])
            P.op('dve', lambda e: e.tensor_tensor(out=Yn[0:1, :], in0=K.psum[pa][0:1, :],
                                                  in1=Kn[0:1, o * 512:(o + 1) * 512], op=ALU.mult),
                 r=['ps%d' % pa, 'Kn'], w=['Yn'])
            for tci in range(nb):
                s_ = 0
                P.dma('sp', tc[s_][:], dc[tci], w=['tc%d' % s_])
                P.dma('sp', ts[s_][:], ds[tci], w=['ts%d' % s_])
                pa = P.ps()
                for fc in range(nb):
                    mm(P, K.psum[pa][:], tc[s_][:, fc, :], Yr[:, fc, :], fc == 0, False, r=['tc%d' % s_, 'Yr'],
                       w=['ps%d' % pa])
                    mm(P, K.psum[pa][:], ts[s_][:, fc, :], Yi[:, fc, :], False, False, r=['ts%d' % s_, 'Yi'],
                       w=['ps%d' % pa])
                mm(P, K.psum[pa][:], altr[0:1, :], Yn[0:1, :], False, True, r=['altr', 'Yn'], w=['ps%d' % pa])
                if o == 0:
                    P.op('dve', lambda e: e.tensor_tensor(out=z1[:, tci, :], in0=K.psum[pa][:], in1=stm[1][:, tci, :],
                                                          op=ALU.mult), r=['ps%d' % pa, 'stm1'], w=['stm0'])
                else:
                    P.op('dve', lambda e: e.tensor_tensor(out=zo[:], in0=K.psum[pa][:], in1=stm[2][:, tci, :],
                                                          op=ALU.mult), r=['ps%d' % pa, 'stm2'], w=['zo'])
                    emit_out(tci, zo)
        P.barrier()


def mixer_hyena(K, l):
    i = l // 2
    P = K.P
    with ExitStack() as es:
        sb = K.mk(es)
        Kr = {n: sb('Kr%d' % n, [128, n // 128, 1024], BF16) for n in (2048, 256)}
        Ki = {n: sb('Ki%d' % n, [128, n // 128, 1024], BF16) for n in (2048, 256)}
        Kn = {n: sb('Kn%d' % n, [1, 1024], BF16) for n in (2048, 256)}
        for n in (256, 2048):
            hyena_filters(K, i, n, Kr[n], Ki[n], Kn[n])
        stm = {2048: [sb('stl%d' % j, [128, 16, 512], BF16) for j in range(3)],
               256: [sb('stc%d' % j, [128, 2, 512], BF16) for j in range(3)]}
        identb = sb('identb', [128, 128], BF16)
        ostg = sb('ostg', [128, 4, NT], BF16)
        P.dma('sp', identb[:], K.identb, w=['identb'])
        with ExitStack() as es2:
            sb2 = K.mk(es2)
            xf = [sb2('xf%d' % j, [128, NT]) for j in range(2)]
            yf = sb2('yf', [128, NT])
            yb = sb2('yb', [128, NT], BF16)
            cw = sb2('cw', [128, 12, 4])
            P.dma('sp', cw[:], K.hy_cw[:, i], w=['cw'])
            for cc in range(12):
                s_ = cc % 2
                st_i, cs_ = cc // 4, cc % 4
                P.dma('sp', xf[s_][:], K.pfm[512 + cc * 128:512 + (cc + 1) * 128, :], r=['pfm'], w=['xf%d' % s_])
                kx = 'xf%d' % s_
                for (a, n_) in [(0, NC_), (NC_, NL)]:
                    P.op('act', lambda e: e.activation(out=yf[:, a:a + n_], in_=xf[s_][:, a:a + n_], func=AF.Identity,
                                                       scale=cw[:, cc, 1:2], bias=cw[:, cc, 3:4]), r=[kx, 'cw'], w=['yf'])
                    P.op('dve', lambda e: e.scalar_tensor_tensor(out=yf[:, a + 1:a + n_], in0=xf[s_][:, a:a + n_ - 1],
                                                                 scalar=cw[:, cc, 0:1], in1=yf[:, a + 1:a + n_],
                                                                 op0=ALU.mult, op1=ALU.add), r=[kx, 'cw', 'yf'], w=['yf'])
                    P.op('dve', lambda e: e.scalar_tensor_tensor(out=yb[:, a:a + n_ - 1], in0=xf[s_][:, a + 1:a + n_],
                                                                 scalar=cw[:, cc, 2:3], in1=yf[:, a:a + n_ - 1],
                                                                 op0=ALU.mult, op1=ALU.add), r=[kx, 'cw', 'yf'], w=['yb'])
                    P.op('act', lambda e: e.activation(out=yb[:, a + n_ - 1:a + n_], in_=yf[:, a + n_ - 1:a + n_],
                                                       func=AF.Copy), r=['yf'], w=['yb'])
                for gi_, (c0, cn) in enumerate([(0, 2), (2, 4), (6, 4), (10, 4), (14, 4)]):
                    pb = P.ps()
                    pst = K.psum[pb][:].bitcast(BF16)
                    for q in range(cn):
                        c = c0 + q
                        P.op('pe', lambda e: e.transpose(pst[:, q * 128:(q + 1) * 128], yb[:, c * 128:(c + 1) * 128],
                                                         identb[:]), r=['yb', 'identb'], w=['ps%d' % pb])
                    if c0 == 0:
                        dst = stm[256][st_i][:, 0:2, cs_ * 128:(cs_ + 1) * 128]
                        kd = 'stc'
                    else:
                        dst = stm[2048][st_i][:, c0 - 2:c0 - 2 + cn, cs_ * 128:(cs_ + 1) * 128]
                        kd = 'stl'
                    src_ = pst[:, 0:cn * 128].rearrange("p (q t) -> p q t", q=cn)
                    if gi_ % 2:
                        P.op('act', lambda e: e.activation(out=dst, in_=src_, func=AF.Copy), r=['ps%d' % pb], w=[kd])
                    else:
                        P.op('dve', lambda e: e.tensor_copy(dst, src_), r=['ps%d' % pb], w=[kd])
            P.barrier()
        for n, coff in [(256, 0), (2048, 2)]:
            def emit_out(tci, zo, coff=coff):
                pb = P.ps()
                pst = K.psum[pb][:].bitcast(BF16)
                for q in range(4):
                    P.op('pe', lambda e: e.transpose(pst[:, q * 128:(q + 1) * 128], zo[:, q * 128:(q + 1) * 128],
                                                     identb[:]), r=['zo', 'identb'], w=['ps%d' % pb])
                t0 = (coff + tci) * 128
                P.op('act', lambda e: e.activation(out=ostg[:, :, t0:t0 + 128],
                                                   in_=pst[:, 0:512].rearrange("p (q t) -> p q t", q=4), func=AF.Copy),
                     r=['ps%d' % pb], w=['ostg'])
            hyena_conv(K, n, stm[n], Kr[n], Ki[n], Kn[n], emit_out)
        for cq in range(4):
            P.dma('sp', K.mixd[l][512 + cq * 128:512 + (cq + 1) * 128, :], ostg[:, cq, :], r=['ostg'], w=['mix'])


def conv3(P, y, x, cw3, kx, ky):
    for (a, n_) in [(0, NC_), (NC_, NL)]:
        P.op('act', lambda e: e.activation(out=y[:, a:a + n_], in_=x[:, a:a + n_], func=AF.Identity,
                                           scale=cw3[:, 1:2]), r=[kx], w=[ky])
        P.op('dve', lambda e: e.scalar_tensor_tensor(out=y[:, a + 1:a + n_], in0=x[:, a:a + n_ - 1], scalar=cw3[:, 0:1],
                                                     in1=y[:, a + 1:a + n_], op0=ALU.mult, op1=ALU.add),
             r=[kx, ky], w=[ky])
        P.op('dve', lambda e: e.scalar_tensor_tensor(out=y[:, a:a + n_ - 1], in0=x[:, a + 1:a + n_], scalar=cw3[:, 2:3],
                                                     in1=y[:, a:a + n_ - 1], op0=ALU.mult, op1=ALU.add),
             r=[kx, ky], w=[ky])


def rw_vis(T_, d):
    if d == 0:
        return [(T_[:, 0:NC_].rearrange("p (c t) -> p c t", t=64), 0),
                (T_[:, NC_:NT].rearrange("p (c t) -> p c t", t=64), 4)]
    return [(T_[:, NC_ - 1::-1].rearrange("p (c t) -> p c t", t=64), 0),
            (T_[:, NT - 1:NC_ - 1:-1].rearrange("p (c t) -> p c t", t=64), 4)]


def rwkv_pre(K, i, d):
    P = K.P
    with ExitStack() as es:
        sb = K.mk(es)
        cw = sb('cw', [128, 12, 3])
        w0a0 = sb('w0a0', [128, 4, 2])
        cols = sb('cols', [128, 4, 5])
        omka = sb('omka', [128, 4])
        wlo = sb('wlo', [64, NT])
        alo = sb('alo', [64, NT])
        wup = sb('wup', [64, 512])
        aup = sb('aup', [64, 512])
        blk = sb('blk', [128, 128], BF16)
        ones64 = sb('ones64', [128, 64])
        tiny = sb('tiny', [128, 1])
        xin = sb('xin', [128, NT])
        bufs = {nm: sb('b_' + nm, [128, NT]) for nm in ['r', 'k', 'v', 'lw', 'lam', 'gi', 'gv', 'ge', 'a', 'kk', 'ke', 'b']}
        sqb = sb('sqb', [128, 512], BF16)
        rs = sb('rs', [128, 512])
        QRs = sb('QRs', [128, 36, 128], BF16)
        BKs = sb('BKs', [128, 36, 128], BF16)
        Vs = sb('Vs', [128, NT], BF16)
        gCt = sb('gCt', [128, 36])
        B = bufs
        P.dma('sp', cw[:], K.rw_cw[:, i], w=['cw'])
        P.dma('sp', w0a0[:], K.rw_w0a0[:, i, d], w=['w0a0'])
        P.dma('sp', cols[:], K.rw_cols[:, i], w=['cols'])
        P.dma('sp', wlo[:], K.pfm[1536:1600, :], r=['pfm'], w=['wlo'])
        P.dma('sp', alo[:], K.pfm[1600:1664, :], r=['pfm'], w=['alo'])
        P.dma('sp', wup[:], K.rw_wup[i, d], w=['wup'])
        P.dma('sp', aup[:], K.rw_aup[i, d], w=['aup'])
        P.dma('sp', blk[:], K.rw_blk, w=['blk'])
        P.op('dve', lambda e: e.memset(ones64[:], 1.0), w=['ones64'])
        P.op('dve', lambda e: e.memset(tiny[:], 2.0 ** -60), w=['tiny'])
        P.op('act', lambda e: e.activation(out=wlo[:], in_=wlo[:], func=AF.Tanh), r=['wlo'], w=['wlo'])
        P.op('dve', lambda e: e.tensor_scalar(out=omka[:], in0=cols[:, :, 1], scalar1=-1.0, scalar2=1.0, op0=ALU.mult,
                                              op1=ALU.add), r=['cols'], w=['omka'])

        def tt(out, in0, in1, op, r, w, eng='dve'):
            P.op(eng, lambda e: e.tensor_tensor(out=out, in0=in0, in1=in1, op=op), r=r, w=w)
        for cq in range(4):
            for j, nm in enumerate(['r', 'k', 'v']):
                P.dma('sp', xin[:], K.pfm[j * 512 + cq * 128:j * 512 + (cq + 1) * 128, :], r=['pfm'], w=['xin'])
                conv3(P, B[nm], xin, cw[:, j * 4 + cq, :], 'xin', nm)
            if d == 0:
                P.dma('sp', K.rw_rv[0, cq * 128:(cq + 1) * 128, :], B['r'][:], r=['r'], w=['rw_rv'])
                P.dma('sp', K.rw_rv[1, cq * 128:(cq + 1) * 128, :], B['v'][:], r=['v'], w=['rw_rv'])
            for (a, n) in TILES:
                pb = P.ps()
                mm(P, K.psum[pb][:, :n], wup[:, cq * 128:(cq + 1) * 128], wlo[:, a:a + n], True, True, r=['wup', 'wlo'],
                   w=['ps%d' % pb])
                P.op('act', lambda e: e.activation(out=B['lw'][:, a:a + n], in_=K.psum[pb][:, :n], func=AF.Sigmoid,
                                                   bias=w0a0[:, cq, 0:1], scale=1.0), r=['ps%d' % pb, 'w0a0'], w=['lw'])
                pb = P.ps()
                mm(P, K.psum[pb][:, :n], aup[:, cq * 128:(cq + 1) * 128], alo[:, a:a + n], True, True, r=['aup', 'alo'],
                   w=['ps%d' % pb])
                P.op('act', lambda e: e.activation(out=B['a'][:, a:a + n], in_=K.psum[pb][:, :n], func=AF.Sigmoid,
                                                   bias=w0a0[:, cq, 1:2], scale=1.0), r=['ps%d' % pb, 'w0a0'], w=['a'])
            P.op('dve', lambda e: e.tensor_scalar(out=B['lw'][:], in0=B['lw'][:], scalar1=-math.exp(-0.5), scalar2=None,
                                                  op0=ALU.mult), r=['lw'], w=['lw'])
            for c in range(36):
                sl = slice(c * 64, (c + 1) * 64)
                if d == 0:
                    o_, i_ = B['lam'][:, sl], B['lw'][:, sl]
                else:
                    lo = c * 64
                    hi = c * 64 + 63
                    o_ = B['lam'][:, hi::-1] if lo == 0 else B['lam'][:, hi:lo - 1:-1]
                    i_ = B['lw'][:, hi::-1] if lo == 0 else B['lw'][:, hi:lo - 1:-1]
                P.op('dve', lambda e: e.tensor_tensor_scan(o_, ones64[:], i_, 0.0, ALU.mult, ALU.add),
                     r=['lw', 'ones64'], w=['lam'])
            P.op('act', lambda e: e.activation(out=B['gi'][:], in_=B['lam'][:], func=AF.Exp), r=['lam'], w=['gi'])
            P.op('act', lambda e: e.activation(out=B['gv'][:], in_=B['lam'][:], func=AF.Exp, scale=-1.0), r=['lam'],
                 w=['gv'])
            tt(B['ge'][:], B['lam'][:], B['lw'][:], ALU.subtract, ['lam', 'lw'], ['ge'])
            P.op('act', lambda e: e.activation(out=B['ge'][:], in_=B['ge'][:], func=AF.Exp), r=['ge'], w=['ge'])
            gsrc = B['gi'][:, 63::64] if d == 0 else B['gi'][:, 0::64]
            P.op('dve', lambda e: e.tensor_copy(gCt[:], gsrc), r=['gi'], w=['gCt'])
            P.dma('sp', K.rw_gC[cq * 128:(cq + 1) * 128, :], gCt[:], r=['gCt'], w=['rw_gC'])
            P.op('dve', lambda e: e.tensor_scalar(out=B['kk'][:], in0=B['k'][:], scalar1=cols[:, cq, 0:1], scalar2=None,
                                                  op0=ALU.mult), r=['k', 'cols'], w=['kk'])
            for (a, n) in TILES:
                P.op('act', lambda e: e.activation(out=sqb[:, :n], in_=B['kk'][:, a:a + n], func=AF.Square), r=['kk'],
                     w=['sqb'])
                pb = P.ps()
                mm(P, K.psum[pb][:, :n], blk[:], sqb[:, :n], True, True, r=['blk', 'sqb'], w=['ps%d' % pb])
                P.op('act', lambda e: e.activation(out=rs[:, :n], in_=K.psum[pb][:, :n], func=AF.Ln, bias=tiny[:, 0:1],
                                                   scale=1.0), r=['ps%d' % pb, 'tiny'], w=['rs'])
                P.op('act', lambda e: e.activation(out=rs[:, :n], in_=rs[:, :n], func=AF.Exp, scale=-0.5), r=['rs'],
                     w=['rs'])
                tt(B['kk'][:, a:a + n], B['kk'][:, a:a + n], rs[:, :n], ALU.mult, ['kk', 'rs'], ['kk'])
            P.op('dve', lambda e: e.tensor_scalar(out=B['ke'][:], in0=B['a'][:], scalar1=cols[:, cq, 1:2],
                                                  scalar2=omka[:, cq:cq + 1], op0=ALU.mult, op1=ALU.add),
                 r=['a', 'cols', 'omka'], w=['ke'])
            tt(B['ke'][:], B['ke'][:], B['k'][:], ALU.mult, ['ke', 'k'], ['ke'])
            tt(B['b'][:], B['kk'][:], B['a'][:], ALU.mult, ['kk', 'a'], ['b'], eng='pool')
            P.dma('sp', K.rw_keff[d, cq * 128:(cq + 1) * 128, :], B['ke'][:], r=['ke'], w=['rw_keff'])
            for (dst, half, x_, g_, kd, eng) in [(QRs, 0, 'kk', 'ge', 'QRs', 'dve'), (QRs, 1, 'r', 'gi', 'QRs', 'pool'),
                                                 (BKs, 0, 'b', 'gv', 'BKs', 'dve'), (BKs, 1, 'ke', 'gv', 'BKs', 'pool')]:
                for (xv, c0), (gv_, _) in zip(rw_vis(B[x_], d), rw_vis(B[g_], d)):
                    ncn = xv.shape[1]
                    tt(dst[:, c0:c0 + ncn, half * 64:(half + 1) * 64], xv, gv_, ALU.mult, [x_, g_], [kd], eng=eng)
            for (xv, c0) in rw_vis(B['v'], d):
                ncn = xv.shape[1]
                P.op('act', lambda e: e.activation(out=Vs[:, c0 * 64:(c0 + ncn) * 64].rearrange("p (c t) -> p c t", t=64),
                                                   in_=xv, func=AF.Copy), r=['v'], w=['Vs'])
            P.dma('sp', K.rw_QR[cq * 128:(cq + 1) * 128], QRs[:], r=['QRs'], w=['rw_QR'])
            P.dma('sp', K.rw_BK[cq * 128:(cq + 1) * 128], BKs[:], r=['BKs'], w=['rw_BK'])
            P.dma('sp', K.rw_V[cq * 128:(cq + 1) * 128, :], Vs[:], r=['Vs'], w=['rw_V'])
        P.barrier()


def rwkv_scan(K, d, yacc):
    P = K.P
    with ExitStack() as es:
        sb = K.mk(es)
        QR = sb('QR', [128, 4, 36, 128], BF16)
        BK = sb('BK', [128, 4, 36, 128], BF16)
        V = sb('V', [128, 4, NT], BF16)
        gC = sb('gC', [128, 4, 36])
        mk = sb('mk', [128, 2, 128])
        mk3 = sb('mk3', [128, 2, 64])
        idb = sb('idb', [128, 64], BF16)
        Mf = sb('Mf', [128, 4, 64])
        Mg = sb('Mg', [128, 4, 64])
        Mb = sb('Mb', [128, 4, 64], BF16)
        W1 = sb('W1', [128, 4, 128], BF16)
        W2 = sb('W2', [128, 4, 128], BF16)
        Aa = [sb('Aa%d' % j, [128, 4, 64], BF16) for j in range(2)]
        Bb = [sb('Bb%d' % j, [128, 4, 64], BF16) for j in range(2)]
        Pm = sb('Pm', [128, 4, 64], BF16)
        Vtm = sb('Vtm', [128, 4, 64], BF16)
        RH = sb('RH', [128, 4, 64], BF16)
        nU = sb('nU', [128, 4, 64], BF16)
        bT = sb('bT', [128, 4, 64], BF16)
        kT = sb('kT', [128, 4, 64], BF16)
        for cq in range(4):
            P.dma('sp', QR[:, cq], K.rw_QR[cq * 128:(cq + 1) * 128], r=['rw_QR'], w=['QR'])
            P.dma('sp', BK[:, cq], K.rw_BK[cq * 128:(cq + 1) * 128], r=['rw_BK'], w=['BK'])
            P.dma('sp', V[:, cq], K.rw_V[cq * 128:(cq + 1) * 128, :], r=['rw_V'], w=['V'])
            P.dma('sp', gC[:, cq], K.rw_gC[cq * 128:(cq + 1) * 128, :], r=['rw_gC'], w=['gC'])
        P.dma('sp', mk[:], K.rw_mk, w=['mk'])
        P.dma('sp', mk3[:], K.rw_mk3, w=['mk3'])
        P.dma('sp', idb[:], K.rw_idb, w=['idb'])
        P.op('dve', lambda e: e.memset(Mf[:], 0.0), w=['Mf0', 'Mf1'])
        P.op('dve', lambda e: e.memset(Mb[:], 0.0), w=['Mb0', 'Mb1'])
        HH = (0, 1)

        def rg(hh):
            return slice(hh * 64, hh * 64 + 64)

        def bank(hh, j):
            return K.psum[hh * 4 + j], 'ps%d' % (hh * 4 + j)

        def bc(ap, shape):
            return ap.unsqueeze(1).to_broadcast(shape)
        for n in range(36):
            for hh in HH:
                p_ = rg(hh)
                (x1, k1), (x2, k2), (x3, k3) = bank(hh, 0), bank(hh, 1), bank(hh, 2)
                for u in range(4):
                    bt, kt_ = BK[p_, u, n, 0:64], BK[p_, u, n, 64:128]
                    qr, qt = QR[p_, u, n, :], QR[p_, u, n, 0:64]
                    mm(P, x1[p_, u * 128:(u + 1) * 128], bt, qr, True, True, r=['BK', 'QR'], w=[k1])
                    mm(P, x2[p_, u * 128:(u + 1) * 128], kt_, qr, True, True, r=['BK', 'QR'], w=[k2])
                    mm(P, x3[p_, u * 64:(u + 1) * 64], qt, bt, True, True, r=['BK', 'QR'], w=[k3])
            for hh in HH:
                p_ = rg(hh)
                h = str(hh)
                (x1, k1), (x2, k2), (x3, k3) = bank(hh, 0), bank(hh, 1), bank(hh, 2)
                v4 = lambda x, w_: x[p_, 0:4 * w_].rearrange("p (u c) -> p u c", u=4)
                P.op('dve', lambda e: e.tensor_tensor(out=W1[p_], in0=v4(x1, 128), in1=bc(mk[p_, 0, :], [64, 4, 128]),
                                                      op=ALU.mult), r=[k1, 'mk'], w=['W1' + h])
                P.op('dve', lambda e: e.tensor_tensor(out=W2[p_], in0=v4(x2, 128), in1=bc(mk[p_, 1, :], [64, 4, 128]),
                                                      op=ALU.mult), r=[k2, 'mk'], w=['W2' + h])
                P.op('dve', lambda e: e.tensor_tensor(out=Bb[0][p_], in0=v4(x3, 64), in1=bc(mk3[p_, 0, :], [64, 4, 64]),
                                                      op=ALU.mult), r=[k3, 'mk3'], w=['B0' + h])
                P.op('dve', lambda e: e.tensor_tensor(out=Pm[p_], in0=W1[p_, :, 0:64], in1=bc(mk3[p_, 1, :], [64, 4, 64]),
                                                      op=ALU.add), r=['W1' + h, 'mk3'], w=['Pm' + h])
            cur = 0
            for lev in range(1, 6):
                nxt = 1 - cur
                for hh in HH:
                    p_ = rg(hh)
                    h = str(hh)
                    (xa, ka), (xb, kb) = bank(hh, 0), bank(hh, 1)
                    for u in range(4):
                        a_cur = W1[p_, u, 0:64] if lev == 1 else Aa[cur][p_, u, :]
                        ka_cur = ('W1' + h) if lev == 1 else ('A%d' % cur + h)
                        if lev < 5:
                            mm(P, xa[p_, u * 64:(u + 1) * 64], Bb[cur][p_, u, :], a_cur, True, True,
                               r=[ka_cur, 'B%d' % cur + h], w=[ka])
                        mm(P, xb[p_, u * 64:(u + 1) * 64], a_cur, Bb[cur][p_, u, :], True, True,
                           r=[ka_cur, 'B%d' % cur + h], w=[kb])
                for hh in HH:
                    p_ = rg(hh)
                    h = str(hh)
                    (xa, ka), (xb, kb) = bank(hh, 0), bank(hh, 1)
                    v4 = lambda x: x[p_, 0:256].rearrange("p (u c) -> p u c", u=4)
                    if lev < 5:
                        P.op('act', lambda e: e.activation(out=Aa[nxt][p_], in_=v4(xa), func=AF.Copy), r=[ka],
                             w=['A%d' % nxt + h])
                    P.op('dve', lambda e: e.tensor_copy(Bb[nxt][p_], v4(xb)), r=[kb], w=['B%d' % nxt + h])
                for hh in HH:
                    p_ = rg(hh)
                    h = str(hh)
                    (xp, kp) = bank(hh, 2)
                    for u in range(4):
                        mm(P, xp[p_, u * 64:(u + 1) * 64], Bb[nxt][p_, u, :], Pm[p_, u, :], True, True,
                           r=['B%d' % nxt + h, 'Pm' + h], w=[kp])
                for hh in HH:
                    p_ = rg(hh)
                    h = str(hh)
                    (xp, kp) = bank(hh, 2)
                    P.op('dve', lambda e: e.tensor_tensor(out=Pm[p_], in0=xp[p_, 0:256].rearrange("p (u c) -> p u c", u=4),
                                                          in1=Pm[p_], op=ALU.add), r=[kp, 'Pm' + h], w=['Pm' + h])
                cur = nxt
            for hh in HH:
                p_ = rg(hh)
                (xt, kx) = bank(hh, 3)
                xtb = xt[:].bitcast(BF16)
                for u in range(4):
                    P.op('pe', lambda e: e.transpose(xtb[p_, u * 64:(u + 1) * 64], V[p_, u, n * 64:(n + 1) * 64],
                                                     idb[p_, :]), r=['V', 'idb'], w=[kx])
            for hh in HH:
                p_ = rg(hh)
                (xt, kx) = bank(hh, 3)
                xtb = xt[:].bitcast(BF16)
                src_ = xtb[p_, 0:256].rearrange("p (u c) -> p u c", u=4)
                if hh == 0:
                    P.op('dve', lambda e: e.tensor_copy(Vtm[p_], src_), r=[kx], w=['Vtm0'])
                else:
                    P.op('act', lambda e: e.activation(out=Vtm[p_], in_=src_, func=AF.Copy), r=[kx], w=['Vtm1'])
            for hh in HH:
                p_ = rg(hh)
                h = str(hh)
                (xr, kr) = bank(hh, 0)
                for u in range(4):
                    mm(P, xr[p_, u * 64:(u + 1) * 64], QR[p_, u, n, 0:64], Mb[p_, u, :], True, False,
                       r=['QR', 'Mb' + h], w=[kr])
                    mm(P, xr[p_, u * 64:(u + 1) * 64], W2[p_, u, 0:64], Vtm[p_, u, :], False, True,
                       r=['W2' + h, 'Vtm' + h], w=[kr])
            for hh in HH:
                p_ = rg(hh)
                h = str(hh)
                (xr, kr) = bank(hh, 0)
                src_ = xr[p_, 0:256].rearrange("p (u c) -> p u c", u=4)
                if hh == 0:
                    P.op('dve', lambda e: e.tensor_copy(RH[p_], src_), r=[kr], w=['RH0'])
                else:
                    P.op('act', lambda e: e.activation(out=RH[p_], in_=src_, func=AF.Copy), r=[kr], w=['RH1'])
            for hh in HH:
                p_ = rg(hh)
                h = str(hh)
                (xu, ku) = bank(hh, 1)
                for u in range(4):
                    mm(P, xu[p_, u * 64:(u + 1) * 64], Pm[p_, u, :], RH[p_, u, :], True, True, r=['Pm' + h, 'RH' + h],
                       w=[ku])
            for hh in HH:
                p_ = rg(hh)
                (xu, ku) = bank(hh, 1)
                src_ = xu[p_, 0:256].rearrange("p (u c) -> p u c", u=4)
                if hh == 0:
                    P.op('dve', lambda e: e.tensor_scalar(out=nU[p_], in0=src_, scalar1=-1.0, scalar2=None, op0=ALU.mult),
                         r=[ku], w=['nU0'])
                else:
                    P.op('act', lambda e: e.activation(out=nU[p_], in_=src_, func=AF.Copy, scale=-1.0), r=[ku], w=['nU1'])
            for hh in HH:
                p_ = rg(hh)
                h = str(hh)
                (xy, ky) = bank(hh, 2)
                (xt, kx) = bank(hh, 3)
                xtb = xt[:].bitcast(BF16)
                for u in range(4):
                    o_ = xy[p_, u * 64:(u + 1) * 64]
                    mm(P, o_, Mb[p_, u, :], QR[p_, u, n, 64:128], True, False, r=['Mb' + h, 'QR'], w=[ky])
                    mm(P, o_, nU[p_, u, :], W1[p_, u, 64:128], False, False, r=['nU' + h, 'W1' + h], w=[ky])
                    mm(P, o_, Vtm[p_, u, :], W2[p_, u, 64:128], False, True, r=['Vtm' + h, 'W2' + h], w=[ky])
                for u in range(4):
                    P.op('pe', lambda e: e.transpose(xtb[p_, u * 64:(u + 1) * 64], BK[p_, u, n, 0:64], idb[p_, :]),
                         r=['BK', 'idb'], w=[kx])
                    P.op('pe', lambda e: e.transpose(xtb[p_, 256 + u * 64:256 + (u + 1) * 64], BK[p_, u, n, 64:128],
                                                     idb[p_, :]), r=['BK', 'idb'], w=[kx])
            for hh in HH:
                p_ = rg(hh)
                h = str(hh)
                (xy, ky) = bank(hh, 2)
                (xt, kx) = bank(hh, 3)
                xtb = xt[:].bitcast(BF16)
                if d == 0:
                    yo = yacc[p_, :, n * 64:(n + 1) * 64]
                else:
                    if n < 4:
                        hi = NC_ - 1 - 64 * n
                    else:
                        hi = 2559 - 64 * n
                    lo = hi - 63
                    yo = yacc[p_, :, hi::-1] if lo == 0 else yacc[p_, :, hi:lo - 1:-1]
                src_ = xy[p_, 0:256].rearrange("p (u c) -> p u c", u=4)
                if d == 0:
                    P.op('dve', lambda e: e.tensor_copy(yo, src_), r=[ky], w=['yacc' + h])
                else:
                    P.op('dve', lambda e: e.tensor_tensor(out=yo, in0=src_, in1=yo, op=ALU.add), r=[ky, 'yacc' + h],
                         w=['yacc' + h])
                P.op('act', lambda e: e.activation(out=bT[p_], in_=xtb[p_, 0:256].rearrange("p (u c) -> p u c", u=4),
                                                   func=AF.Copy), r=[kx], w=['bT' + h])
                P.op('act', lambda e: e.activation(out=kT[p_], in_=xtb[p_, 256:512].rearrange("p (u c) -> p u c", u=4),
                                                   func=AF.Copy), r=[kx], w=['kT' + h])
            for hh in HH:
                p_ = rg(hh)
                h = str(hh)
                (xm, km) = bank(hh, 0)
                for u in range(4):
                    o_ = xm[p_, u * 64:(u + 1) * 64]
                    mm(P, o_, bT[p_, u, :], nU[p_, u, :], True, False, r=['bT' + h, 'nU' + h], w=[km])
                    mm(P, o_, kT[p_, u, :], Vtm[p_, u, :], False, True, r=['kT' + h, 'Vtm' + h], w=[km])
            for hh in HH:
                p_ = rg(hh)
                h = str(hh)
                (xm, km) = bank(hh, 0)
                for u in range(4):
                    g_ = gC[p_, u, n:n + 1]
                    P.op('dve', lambda e: e.tensor_scalar(out=Mg[p_, u, :], in0=Mf[p_, u, :], scalar1=g_, scalar2=None,
                                                          op0=ALU.mult), r=['Mf' + h, 'gC'], w=['Mg' + h])
                    P.op('dve', lambda e: e.scalar_tensor_tensor(out=Mf[p_, u, :], in0=xm[p_, u * 64:(u + 1) * 64],
                                                                 scalar=g_, in1=Mg[p_, u, :], op0=ALU.mult, op1=ALU.add),
                         r=[km, 'Mg' + h, 'gC'], w=['Mf' + h])
                P.op('act', lambda e: e.activation(out=Mb[p_], in_=Mf[p_], func=AF.Copy), r=['Mf' + h], w=['Mb' + h])
        P.barrier()


def rwkv_post(K, l, yacc):
    i = l // 2
    P = K.P
    with ExitStack() as es:
        sb = K.mk(es)
        cols = sb('cols', [128, 4, 5])
        blk = sb('blk', [128, 128], BF16)
        glo = sb('glo', [128, NT])
        gup = sb('gup', [128, 512])
        rr = sb('rr', [128, NT])
        vv = sb('vv', [128, NT])
        k0 = sb('k0', [128, NT])
        k1 = sb('k1', [128, NT])
        ybf = sb('ybf', [128, 512], BF16)
        yc = sb('yc', [128, 512])
        sq = sb('sq', [128, 512], BF16)
        rs = sb('rs', [128, 512])
        bon = sb('bon', [128, 512])
        eps = sb('eps', [128, 1])
        ostg = [sb('ostg%d' % j, [128, NT], BF16) for j in range(2)]
        P.dma('sp', cols[:], K.rw_cols[:, i], w=['cols'])
        P.dma('sp', blk[:], K.rw_blk, w=['blk'])
        P.dma('sp', glo[:], K.pfm[1664:1792, :], r=['pfm'], w=['glo'])
        P.dma('sp', gup[:], K.rw_gup[i], w=['gup'])
        P.op('dve', lambda e: e.memset(eps[:], 64e-5), w=['eps'])
        P.op('act', lambda e: e.activation(out=glo[:], in_=glo[:], func=AF.Sigmoid), r=['glo'], w=['glo'])
        for cq in range(4):
            os_ = cq % 2
            rows = slice(cq * 128, (cq + 1) * 128)
            P.dma('sp', rr[:], K.rw_rv[0, rows, :], r=['rw_rv'], w=['rr'])
            P.dma('sp', vv[:], K.rw_rv[1, rows, :], r=['rw_rv'], w=['vv'])
            P.dma('sp', k0[:], K.rw_keff[0, rows, :], r=['rw_keff'], w=['k0'])
            P.dma('sp', k1[:], K.rw_keff[1, rows, :], r=['rw_keff'], w=['k1'])
            P.op('pool', lambda e: e.tensor_tensor(out=k0[:], in0=k0[:], in1=k1[:], op=ALU.add), r=['k0', 'k1'], w=['k0'])
            P.op('pool', lambda e: e.tensor_tensor(out=k0[:], in0=k0[:], in1=rr[:], op=ALU.mult), r=['k0', 'rr'], w=['k0'])
            for (a, n) in TILES:
                y = yacc[:, cq, a:a + n]
                P.op('act', lambda e: e.activation(out=ybf[:, :n], in_=y, func=AF.Copy), r=['yacc0', 'yacc1'], w=['ybf'])
                pb = P.ps()
                mm(P, K.psum[pb][:, :n], blk[:], ybf[:, :n], True, True, r=['blk', 'ybf'], w=['ps%d' % pb])
                P.op('dve', lambda e: e.scalar_tensor_tensor(out=yc[:, :n], in0=K.psum[pb][:, :n], scalar=-1.0 / 64, in1=y,
                                                             op0=ALU.mult, op1=ALU.add), r=['ps%d' % pb, 'yacc0', 'yacc1'],
                     w=['yc'])
                P.op('act', lambda e: e.activation(out=sq[:, :n], in_=yc[:, :n], func=AF.Square), r=['yc'], w=['sq'])
                pb = P.ps()
                mm(P, K.psum[pb][:, :n], blk[:], sq[:, :n], True, True, r=['blk', 'sq'], w=['ps%d' % pb])
                P.op('act', lambda e: e.activation(out=rs[:, :n], in_=K.psum[pb][:, :n], func=AF.Sqrt, scale=1.0 / 64,
                                                   bias=eps[:, 0:1]), r=['ps%d' % pb, 'eps'], w=['rs'])
                P.op('dve', lambda e: e.reciprocal(rs[:, :n], rs[:, :n]), r=['rs'], w=['rs'])
                P.op('dve', lambda e: e.tensor_tensor(out=yc[:, :n], in0=yc[:, :n], in1=rs[:, :n], op=ALU.mult),
                     r=['yc', 'rs'], w=['yc'])
                P.op('dve', lambda e: e.tensor_scalar(out=yc[:, :n], in0=yc[:, :n], scalar1=cols[:, cq, 3:4],
                                                      scalar2=cols[:, cq, 4:5], op0=ALU.mult, op1=ALU.add),
                     r=['yc', 'cols'], w=['yc'])
                P.op('act', lambda e: e.activation(out=sq[:, :n], in_=k0[:, a:a + n], func=AF.Identity,
                                                   scale=cols[:, cq, 2:3]), r=['k0', 'cols'], w=['sq'])
                pb = P.ps()
                mm(P, K.psum[pb][:, :n], blk[:], sq[:, :n], True, True, r=['blk', 'sq'], w=['ps%d' % pb])
                P.op('dve', lambda e: e.tensor_tensor(out=bon[:, :n], in0=K.psum[pb][:, :n], in1=vv[:, a:a + n],
                                                      op=ALU.mult), r=['ps%d' % pb, 'vv'], w=['bon'])
                P.op('dve', lambda e: e.tensor_tensor(out=yc[:, :n], in0=yc[:, :n], in1=bon[:, :n], op=ALU.add),
                     r=['yc', 'bon'], w=['yc'])
                pb = P.ps()
                mm(P, K.psum[pb][:, :n], gup[:, cq * 128:(cq + 1) * 128], glo[:, a:a + n], True, True, r=['gup', 'glo'],
                   w=['ps%d' % pb])
                P.op('dve', lambda e: e.tensor_tensor(out=ostg[os_][:, a:a + n], in0=K.psum[pb][:, :n], in1=yc[:, :n],
                                                      op=ALU.mult), r=['ps%d' % pb, 'yc'], w=['ostg%d' % os_])
            P.dma('sp', K.mixd[l][rows, :], ostg[os_][:], r=['ostg%d' % os_], w=['mix'])
        P.barrier()


def mixer_rwkv(K, l):
    i = l // 2
    with ExitStack() as es:
        yacc = K.mk(es)('yacc', [128, 4, NT])
        for d in range(2):
            rwkv_pre(K, i, d)
            rwkv_scan(K, d, yacc)
        rwkv_post(K, l, yacc)


_CACHE = {}


def kernel(**inputs):
    inp = {k: np.asarray(v) for k, v in inputs.items()}
    if 'nc' not in _CACHE:
        _CACHE['nc'] = build({})
    nc = _CACHE['nc']
    hm = host_mixer(inp)
    in_maps = []
    for b in range(8):
        m = host_common(inp, b)
        m.update(hm)
        in_maps.append(m)
    res = run_bass_kernel_spmd(nc, in_maps, core_ids=list(range(8)))
    out = np.stack([np.asarray(r["out"], dtype=np.float32) for r in res.results], axis=0)
    return out
```
